# Optimizing a Trainium2 kernel written in Bass

```python
import math
import jax, jax.numpy as jnp
from jax import lax
import numpy as np

D_MODEL = 1024
BATCH = 8
SEQ = 8192
DEPTH = 2

SSM_D_INNER = D_MODEL
SSM_HEAD_DIM = 64
SSM_HEADS = SSM_D_INNER // SSM_HEAD_DIM
SSM_GROUPS = 4
SSM_D_STATE = 128
SSM_CONV = 4
SSM_CHUNK = 128
SSM_CONV_DIM = SSM_D_INNER + 2 * SSM_GROUPS * SSM_D_STATE
SB_HEAD_DIM = 128
SB_HEADS = 4
SB_QBLK = 128
NSA_HEAD_DIM = 64
NSA_Q_HEADS = 8
NSA_KV_HEADS = 2
CMP_STRIDE = 16
CMP_LEN = 2 * CMP_STRIDE
SEL_BLOCK = 64
N_SEL = 8
WINDOW = 512
NSA_QBLK = 32
D_FF = 11 * D_MODEL // 4
FFN_CONV = 3

NORM_EPS = 1e-6
NEG = -1e30
BIG = 1e30

IN_SPLIT_SIZES = (SSM_D_INNER, SSM_CONV_DIM, SSM_HEADS, 3 * SB_HEADS * SB_HEAD_DIM,
                  NSA_Q_HEADS * NSA_HEAD_DIM, 6 * NSA_KV_HEADS * NSA_HEAD_DIM, 3 * NSA_Q_HEADS, 3 * D_MODEL)
D_IN = sum(IN_SPLIT_SIZES)

kernel_name = "hybrid_ssd_stickbreak_nsa_block"


def rms_norm(x, w):
    xf = x.astype(jnp.float32)
    y = xf * lax.rsqrt(jnp.mean(xf * xf, axis=-1, keepdims=True) + NORM_EPS)
    return (y * w.astype(jnp.float32)).astype(x.dtype)


def causal_dwconv(x, w, b):
    k, c = w.shape
    out = lax.conv_general_dilated(x, w[:, None, :].astype(x.dtype), window_strides=(1,),
                                   padding=[(k - 1, 0)], dimension_numbers=('NWC', 'WIO', 'NWC'),
                                   feature_group_count=c)
    return out + b.astype(x.dtype)


def masked_softmax(s, mask):
    p = jax.nn.softmax(jnp.where(mask, s, NEG), axis=-1)
    return jnp.where(mask, p, 0.0)


def alibi_slopes(n):
    return jnp.asarray(2.0 ** (-8.0 * np.arange(1, n + 1) / n), dtype=jnp.float32)


def split_points():
    return np.cumsum(np.array(IN_SPLIT_SIZES))[:-1].tolist()


def ssd_chunked(x, dt, a, bmat, cmat):
    bsz, s, h, p = x.shape
    g, n = bmat.shape[2], bmat.shape[3]
    hg = h // g
    nc = s // SSM_CHUNK

    def chunks(t, tail):
        return jnp.moveaxis(t.reshape((bsz, nc, SSM_CHUNK) + tail), 1, 0)

    xc = chunks(x, (g, hg, p))
    dtc = chunks(dt, (g, hg))
    bc = chunks(bmat, (g, n))
    cc = chunks(cmat, (g, n))
    a = a.reshape(g, hg)
    tri = jnp.tril(jnp.ones((SSM_CHUNK, SSM_CHUNK), dtype=bool))[None, :, :, None, None]

    def step(state, inp):
        x_c, dt_c, b_c, c_c = inp
        acum = jnp.cumsum(dt_c * a, axis=1)
        seg = acum[:, :, None] - acum[:, None, :]
        decay = jnp.exp(jnp.where(tri, seg, -jnp.inf))
        cb = jnp.einsum('btgn,bsgn->btsg', c_c, b_c)
        w = cb[..., None] * decay * dt_c[:, None]
        y = jnp.einsum('btsgh,bsghp->btghp', w, x_c)
        y = y + jnp.einsum('btgn,bghpn->btghp', c_c, state) * jnp.exp(acum)[..., None]
        to_end = jnp.exp(acum[:, -1:] - acum) * dt_c
        state = (state * jnp.exp(acum[:, -1])[..., None, None]
                 + jnp.einsum('bsgh,bsghp,bsgn->bghpn', to_end, x_c, b_c))
        return state, y

    state0 = jnp.zeros((bsz, g, hg, p, n), jnp.float32)
    _, ys = lax.scan(step, state0, (xc, dtc, bc, cc))
    return jnp.moveaxis(ys, 0, 1).reshape(bsz, s, h, p)


def ssm_mixer(z, xbc, dt_raw, conv_w, conv_b, dt_bias, a_log, d_skip, norm_w):
    bsz, s, _ = z.shape
    xbc = jax.nn.silu(causal_dwconv(xbc, conv_w, conv_b))
    xs, bm, cm = jnp.split(xbc, [SSM_D_INNER, SSM_D_INNER + SSM_GROUPS * SSM_D_STATE], axis=-1)
    xs = xs.reshape(bsz, s, SSM_HEADS, SSM_HEAD_DIM).astype(jnp.float32)
    bm = bm.reshape(bsz, s, SSM_GROUPS, SSM_D_STATE).astype(jnp.float32)
    cm = cm.reshape(bsz, s, SSM_GROUPS, SSM_D_STATE).astype(jnp.float32)
    dt = jax.nn.softplus(dt_raw.astype(jnp.float32) + dt_bias.astype(jnp.float32))
    a = -jnp.exp(a_log.astype(jnp.float32))
    y = ssd_chunked(xs, dt, a, bm, cm) + d_skip.astype(jnp.float32)[:, None] * xs
    y = y.reshape(bsz, s, SSM_D_INNER) * jax.nn.silu(z.astype(jnp.float32))
    yg = y.reshape(bsz, s, SSM_GROUPS, SSM_D_INNER // SSM_GROUPS)
    yg = yg * lax.rsqrt(jnp.mean(yg * yg, axis=-1, keepdims=True) + NORM_EPS)
    return (yg.reshape(bsz, s, SSM_D_INNER) * norm_w.astype(jnp.float32)).astype(z.dtype)


def stick_breaking_attention(qkv):
    bsz, s, _ = qkv.shape
    q, k, v = [jnp.moveaxis(t.reshape(bsz, s, SB_HEADS, SB_HEAD_DIM), 2, 1)
               for t in jnp.split(qkv, 3, axis=-1)]
    scale = SB_HEAD_DIM ** -0.5
    outs = []
    for i in range(s // SB_QBLK):
        q0 = i * SB_QBLK
        kl = q0 + SB_QBLK
        qb = q[:, :, q0:kl]
        zl = jnp.einsum('bhqd,bhkd->bhqk', qb, k[:, :, :kl]).astype(jnp.float32) * scale
        mask = jnp.arange(kl)[None, :] < (q0 + jnp.arange(SB_QBLK))[:, None]
        ls = jax.nn.log_sigmoid(zl)
        lk = jnp.where(mask, ls - zl, 0.0)
        later = lax.cumsum(lk, axis=3, reverse=True) - lk
        w = jnp.where(mask, jnp.exp(ls + later), 0.0)
        outs.append(jnp.einsum('bhqk,bhkd->bhqd', w.astype(v.dtype), v[:, :, :kl]))
    out = jnp.concatenate(outs, axis=2)
    return out.transpose(0, 2, 1, 3).reshape(bsz, s, SB_HEADS * SB_HEAD_DIM)


def compress_blocks(kraw, pos_emb, w1, w2):
    bsz, s, g, dh = kraw.shape
    chunks = kraw.reshape(bsz, s // CMP_STRIDE, CMP_STRIDE, g, dh)
    blocks = jnp.concatenate([chunks[:, :-1], chunks[:, 1:]], axis=2) + pos_emb[None, None, :, None, :].astype(kraw.dtype)
    blocks = blocks.transpose(0, 1, 3, 2, 4).reshape(bsz, s // CMP_STRIDE - 1, g, CMP_LEN * dh)
    return jax.nn.silu(blocks @ w1) @ w2


def nsa_attention(q, kv, gate, cmp_pos_k, cmp_w1_k, cmp_w2_k, cmp_pos_v, cmp_w1_v, cmp_w2_v):
    bsz, s, _ = q.shape
    g, hg, dh = NSA_KV_HEADS, NSA_Q_HEADS // NSA_KV_HEADS, NSA_HEAD_DIM
    q = q.reshape(bsz, s, g, hg, dh)
    k_cmp, v_cmp, k_sel, v_sel, k_win, v_win = [t.reshape(bsz, s, g, dh) for t in jnp.split(kv, 6, axis=-1)]
    gate = jax.nn.sigmoid(gate.astype(jnp.float32)).reshape(bsz, s, 3, g, hg)
    kc = compress_blocks(k_cmp, cmp_pos_k, cmp_w1_k, cmp_w2_k)
    vc = compress_blocks(v_cmp, cmp_pos_v, cmp_w1_v, cmp_w2_v)
    nc = kc.shape[1]
    cmp_end = jnp.arange(nc) * CMP_STRIDE + CMP_LEN - 1
    nb = s // SEL_BLOCK
    n_sel = min(N_SEL, nb)
    ratio = SEL_BLOCK // CMP_STRIDE
    ks_blk = k_sel.reshape(bsz, nb, SEL_BLOCK, g, dh).transpose(0, 3, 1, 2, 4)
    vs_blk = v_sel.reshape(bsz, nb, SEL_BLOCK, g, dh).transpose(0, 3, 1, 2, 4)
    kw = jnp.pad(k_win, ((0, 0), (WINDOW, 0), (0, 0), (0, 0)))
    vw = jnp.pad(v_win, ((0, 0), (WINDOW, 0), (0, 0), (0, 0)))
    slopes = alibi_slopes(NSA_Q_HEADS).reshape(g, hg)
    scale = dh ** -0.5
    bi = jnp.arange(bsz)[:, None, None, None]
    gi = jnp.arange(g)[None, :, None, None]
    blk_ids = jnp.arange(nb)

    def block(q0):
        qb = lax.dynamic_slice_in_dim(q, q0, NSA_QBLK, axis=1)
        gb = lax.dynamic_slice_in_dim(gate, q0, NSA_QBLK, axis=1)
        tpos = q0 + jnp.arange(NSA_QBLK)
        dist_c = tpos[:, None] - cmp_end[None, :]
        sc = (jnp.einsum('bqghd,bngd->bghqn', qb, kc).astype(jnp.float32) * scale
              - slopes[:, :, None, None] * dist_c.astype(jnp.float32))
        p_cmp = masked_softmax(sc, dist_c >= 0)
        o_cmp = jnp.einsum('bghqn,bngd->bqghd', p_cmp.astype(vc.dtype), vc)
        imp_pad = jnp.pad(p_cmp.sum(axis=2), ((0, 0), (0, 0), (0, 0), (1, 0)))
        imp = (imp_pad.reshape(bsz, g, NSA_QBLK, nb, ratio).sum(axis=-1)
               + jnp.pad(imp_pad[..., ratio::ratio], ((0, 0), (0, 0), (0, 0), (0, 1))))
        cur = tpos // SEL_BLOCK
        forced = ((blk_ids[None, :] == 0) | (blk_ids[None, :] == cur[:, None])
                  | (blk_ids[None, :] == cur[:, None] - 1))
        valid = blk_ids[None, :] * SEL_BLOCK <= tpos[:, None]
        imp = jnp.where(valid, jnp.where(forced, BIG, imp), NEG)
        _, idx = lax.top_k(imp, n_sel)
        ks = ks_blk[bi, gi, idx]
        vs = vs_blk[bi, gi, idx]
        pos = idx[..., None] * SEL_BLOCK + jnp.arange(SEL_BLOCK)
        dist_s = tpos[None, None, :, None, None] - pos
        ss = (jnp.einsum('bqghd,bgqnkd->bghqnk', qb, ks).astype(jnp.float32) * scale
              - slopes[None, :, :, None, None, None] * dist_s[:, :, None].astype(jnp.float32))
        ss = ss.reshape(bsz, g, hg, NSA_QBLK, n_sel * SEL_BLOCK)
        p_sel = masked_softmax(ss, (dist_s >= 0).reshape(bsz, g, 1, NSA_QBLK, n_sel * SEL_BLOCK))
        o_sel = jnp.einsum('bghqm,bgqmd->bqghd', p_sel.astype(vs.dtype),
                           vs.reshape(bsz, g, NSA_QBLK, n_sel * SEL_BLOCK, dh))
        kwb = lax.dynamic_slice_in_dim(kw, q0, WINDOW + NSA_QBLK, axis=1)
        vwb = lax.dynamic_slice_in_dim(vw, q0, WINDOW + NSA_QBLK, axis=1)
        wpos = q0 - WINDOW + jnp.arange(WINDOW + NSA_QBLK)
        dist_w = tpos[:, None] - wpos[None, :]
        wmask = (dist_w >= 0) & (dist_w < WINDOW) & (wpos[None, :] >= 0)
        sw = (jnp.einsum('bqghd,bkgd->bghqk', qb, kwb).astype(jnp.float32) * scale
              - slopes[:, :, None, None] * dist_w.astype(jnp.float32))
        o_win = jnp.einsum('bghqk,bkgd->bqghd', masked_softmax(sw, wmask).astype(vwb.dtype), vwb)
        out = (gb[:, :, 0, :, :, None] * o_cmp + gb[:, :, 1, :, :, None] * o_sel
               + gb[:, :, 2, :, :, None] * o_win)
        return out.astype(q.dtype)

    out = lax.map(block, jnp.arange(0, s, NSA_QBLK))
    return jnp.moveaxis(out, 0, 1).reshape(bsz, s, NSA_Q_HEADS * NSA_HEAD_DIM)


def setup_inputs(seed: int = 0) -> dict:
    key = jax.random.key(seed)
    ks = iter(jax.random.split(key, 40))
    L = DEPTH
    f32 = jnp.float32

    def nrm(shape, scale):
        return jax.random.normal(next(ks), shape, f32) * scale

    def gain(shape):
        return 1.0 + nrm(shape, 0.02)

    dt0 = jnp.exp(jax.random.uniform(next(ks), (L, SSM_HEADS), f32)
                  * (math.log(0.1) - math.log(1e-3)) + math.log(1e-3))
    dt_bias = dt0 + jnp.log(-jnp.expm1(-dt0))
    a_log = jnp.log(jax.random.uniform(next(ks), (L, SSM_HEADS), f32, minval=1.0, maxval=16.0))
    return {
        "x": nrm((BATCH, SEQ, D_MODEL), 1.0),
        "pre_mix_norm": gain((L, D_MODEL)),
        "w_in": nrm((L, D_MODEL, D_IN), D_MODEL ** -0.5),
        "ssm_conv_w": nrm((L, SSM_CONV, SSM_CONV_DIM), SSM_CONV ** -0.5),
        "ssm_conv_b": nrm((L, SSM_CONV_DIM), 0.01),
        "ssm_dt_bias": dt_bias,
        "ssm_a_log": a_log,
        "ssm_d": 1.0 + nrm((L, SSM_HEADS), 0.1),
        "ssm_norm": gain((L, SSM_D_INNER)),
        "cmp_pos_k": nrm((L, CMP_LEN, NSA_HEAD_DIM), 0.02),
        "cmp_w1_k": nrm((L, CMP_LEN * NSA_HEAD_DIM, NSA_HEAD_DIM), (CMP_LEN * NSA_HEAD_DIM) ** -0.5),
        "cmp_w2_k": nrm((L, NSA_HEAD_DIM, NSA_HEAD_DIM), NSA_HEAD_DIM ** -0.5),
        "cmp_pos_v": nrm((L, CMP_LEN, NSA_HEAD_DIM), 0.02),
        "cmp_w1_v": nrm((L, CMP_LEN * NSA_HEAD_DIM, NSA_HEAD_DIM), (CMP_LEN * NSA_HEAD_DIM) ** -0.5),
        "cmp_w2_v": nrm((L, NSA_HEAD_DIM, NSA_HEAD_DIM), NSA_HEAD_DIM ** -0.5),
        "w_br_ssm": nrm((L, SSM_D_INNER, D_MODEL), SSM_D_INNER ** -0.5),
        "w_br_sb": nrm((L, SB_HEADS * SB_HEAD_DIM, D_MODEL), (SB_HEADS * SB_HEAD_DIM) ** -0.5),
        "w_br_nsa": nrm((L, NSA_Q_HEADS * NSA_HEAD_DIM, D_MODEL), (NSA_Q_HEADS * NSA_HEAD_DIM) ** -0.5),
        "w_out": nrm((L, D_MODEL, D_MODEL), D_MODEL ** -0.5),
        "post_mix_norm": gain((L, D_MODEL)),
        "pre_ffn_norm": gain((L, D_MODEL)),
        "ffn_w_up": nrm((L, D_MODEL, 2 * D_FF), D_MODEL ** -0.5),
        "ffn_conv_w": nrm((L, FFN_CONV, 2 * D_FF), FFN_CONV ** -0.5),
        "ffn_conv_b": nrm((L, 2 * D_FF), 0.01),
        "ffn_w_down": nrm((L, D_FF, D_MODEL), D_FF ** -0.5),
        "post_ffn_norm": gain((L, D_MODEL)),
    }


def reference(x, pre_mix_norm, w_in, ssm_conv_w, ssm_conv_b, ssm_dt_bias, ssm_a_log, ssm_d, ssm_norm,
              cmp_pos_k, cmp_w1_k, cmp_w2_k, cmp_pos_v, cmp_w1_v, cmp_w2_v,
              w_br_ssm, w_br_sb, w_br_nsa, w_out, post_mix_norm, pre_ffn_norm,
              ffn_w_up, ffn_conv_w, ffn_conv_b, ffn_w_down, post_ffn_norm):
    bsz, s, _ = x.shape
    for l in range(DEPTH):
        h = rms_norm(x, pre_mix_norm[l])
        proj = h @ w_in[l]
        z, xbc, dt_raw, sb_qkv, nsa_q, nsa_kv, nsa_gate, merge_gate = jnp.split(proj, split_points(), axis=-1)
        y_ssm = ssm_mixer(z, xbc, dt_raw, ssm_conv_w[l], ssm_conv_b[l], ssm_dt_bias[l],
                          ssm_a_log[l], ssm_d[l], ssm_norm[l])
        y_sb = stick_breaking_attention(sb_qkv)
        y_nsa = nsa_attention(nsa_q, nsa_kv, nsa_gate, cmp_pos_k[l], cmp_w1_k[l], cmp_w2_k[l],
                              cmp_pos_v[l], cmp_w1_v[l], cmp_w2_v[l])
        gm = jax.nn.sigmoid(merge_gate.astype(jnp.float32)).reshape(bsz, s, 3, D_MODEL)
        mixed = (gm[:, :, 0] * (y_ssm @ w_br_ssm[l]) + gm[:, :, 1] * (y_sb @ w_br_sb[l])
                 + gm[:, :, 2] * (y_nsa @ w_br_nsa[l])).astype(x.dtype)
        x = x + rms_norm(mixed @ w_out[l], post_mix_norm[l])
        h = rms_norm(x, pre_ffn_norm[l])
        u = causal_dwconv(h @ ffn_w_up[l], ffn_conv_w[l], ffn_conv_b[l])
        gate, val = jnp.split(u, 2, axis=-1)
        f = (jax.nn.gelu(gate, approximate=True) * val) @ ffn_w_down[l]
        x = x + rms_norm(f, post_ffn_norm[l])
    return x
```

```python
import numpy as np
from contextlib import ExitStack
import concourse.bass as bass
import concourse.mybir as mybir
from concourse.bass_utils import run_bass_kernel_spmd
import ml_dtypes

F32 = mybir.dt.float32
BF16 = mybir.dt.bfloat16
AF = mybir.ActivationFunctionType
ALU = mybir.AluOpType
AX = mybir.AxisListType

ENGS = ("sync", "gpsimd", "scalar", "vector", "tensor")
GEN = 30000
NS = 8
DGEN = (GEN // 16) * NS

D = 1024
DIN = 9000
L_ = 2
DFF = 2816
EPS = 1e-6
C_Z, C_XBC, C_DT, C_SBQ, C_SBK, C_SBV, C_NQ, C_NKV, C_NG, C_MG = 0, 1024, 3072, 3088, 3600, 4112, 4624, 5136, 5904, 5928


class Buf:
    __slots__ = ("name", "lw", "rd")

    def __init__(self, name="b"):
        self.name = name
        self.lw = None
        self.rd = []


class Sched:
    def __init__(self, nc, stack, ngen=9, ndgen=3):
        self.nc = nc
        self.ops = {e: [] for e in ENGS}
        self.fl = {e: 0 for e in ENGS}
        self.nsig = {e: 0 for e in ENGS}
        self.ndma = {e: 0 for e in ENGS}
        self.seen = {e: {} for e in ENGS}
        self.key = {}
        self.esem = {e: [stack.enter_context(nc.semaphore(f"s_{e}_{g}")) for g in range(ngen)]
                     for e in ("gpsimd", "scalar", "vector", "tensor")}
        self.dsem = {e: [[stack.enter_context(nc.semaphore(f"d_{e}_{g}_{i}")) for i in range(NS)]
                         for g in range(ndgen)] for e in ("sync", "gpsimd")}
        self.cleared = False

    def op(self, eng, fn, r=(), w=(), dma=False):
        ops = self.ops[eng]
        idx = len(ops)
        me = (eng, idx)
        raw = set()
        oth = set()
        for b in r:
            if b.lw is not None:
                raw.add(b.lw)
        for b in w:
            if b.lw is not None:
                oth.add(b.lw)
            for x in b.rd:
                oth.add(x)
        oth -= raw
        deps = []
        for is_raw, group in ((True, raw), (False, oth)):
            for (e2, i2) in group:
                o2 = self.ops[e2][i2]
                if o2["dma"]:
                    deps.append((e2, i2))
                    continue
                if i2 < self.fl[e2]:
                    continue
                if e2 == eng and not dma:
                    if eng == "tensor" or not is_raw:
                        continue
                deps.append((e2, i2))
                o2["sig"] = True
        rec = {"fn": fn, "dma": dma, "deps": deps, "sig": False, "dj": None}
        if dma:
            rec["dj"] = self.ndma[eng]
            self.ndma[eng] += 1
        ops.append(rec)
        for b in r:
            b.rd.append(me)
        for b in w:
            b.lw = me
            b.rd = []
        return me

    def _dma_key(self, eng, j):
        g = j // DGEN
        jj = j % DGEN
        return (self.dsem[eng][g][jj % NS], 16 * (jj // NS + 1))

    def flush(self):
        nc = self.nc
        if not self.cleared:
            self.cleared = True
            with nc.Block() as block:
                for e in ENGS:
                    def body(engine, e=e):
                        if e in self.esem:
                            for s in self.esem[e]:
                                engine.sem_clear(s)
                        if e in self.dsem:
                            for g in self.dsem[e]:
                                for s in g:
                                    engine.sem_clear(s)
                    getattr(block, e)(body)
        for e in ENGS:
            for i in range(self.fl[e], len(self.ops[e])):
                o = self.ops[e][i]
                if o["dma"]:
                    self.key[(e, i)] = self._dma_key(e, o["dj"])
                elif o["sig"]:
                    c = self.nsig[e]
                    self.nsig[e] += 1
                    self.key[(e, i)] = (self.esem[e][c // GEN], c % GEN + 1)
        with nc.Block() as block:
            for e in ENGS:
                lo, hi = self.fl[e], len(self.ops[e])

                def body(engine, e=e, lo=lo, hi=hi):
                    seen = self.seen[e]

                    def do_waits(waits):
                        for sem, cnt in waits:
                            k = id(sem)
                            if seen.get(k, 0) >= cnt:
                                continue
                            seen[k] = cnt
                            engine.wait_ge(sem, cnt)
                    for i in range(lo, hi):
                        o = self.ops[e][i]
                        waits = [self.key[d] for d in o["deps"]]
                        if o["dma"] and o["dj"] % DGEN >= NS:
                            waits.append(self._dma_key(e, o["dj"] - NS))
                        do_waits(waits)
                        ins = o["fn"](engine)
                        if o["dma"]:
                            ins.then_inc(self.key[(e, i)][0], 16)
                        elif o["sig"]:
                            ins.then_inc(self.key[(e, i)][0], 1)
                        o["fn"] = None
                    if e in self.dsem:
                        n = self.ndma[e]
                        do_waits([self._dma_key(e, j) for j in range(max(0, n - NS), n)])
                if lo == hi and e not in self.dsem:
                    continue
                getattr(block, e)(body)
        for e in ENGS:
            self.fl[e] = len(self.ops[e])


class Ctx:
    pass


def vtt(e, out, in0, in1, op):
    return e.scalar_tensor_tensor(out=out, in0=in0, scalar=1.0, in1=in1, op0=ALU.mult, op1=op)


_UID = [0]


def _pool(C, st):
    nc = C.nc

    def T(name, shape, dt=F32):
        _UID[0] += 1
        return st.enter_context(nc.sbuf_tensor(f"{name}_{_UID[0]}", shape, dt))

    def P(name, shape, dt=F32):
        _UID[0] += 1
        return st.enter_context(nc.psum_tensor(f"{name}_{_UID[0]}", shape, dt))
    return T, P


def rstd_ops(C, ssq, rs, n, Bssq, Brs):
    S = C.S
    S.op("scalar", lambda e: e.activation(out=rs, in_=ssq, func=AF.Ln, scale=1.0 / n, bias=C.eps_t[:, 0:1]),
         r=[Bssq, C.Bconst], w=[Brs])
    S.op("scalar", lambda e: e.activation(out=rs, in_=rs, func=AF.Exp, scale=-0.5), r=[Brs], w=[Brs])


class NormT:
    def __init__(self, C, st, gain_row, tag, nxn=2, junk=None, Bjunk=None, nxt=2):
        self.C = C
        T, P = _pool(C, st)
        self.xt = [T(f"{tag}_xt{i}", [128, D]) for i in range(nxt)] * (2 // nxt)
        self.xn = [T(f"{tag}_xn{i}", [128, D]) for i in range(nxn)] * (2 // nxn)
        self.junk = junk if junk is not None else T(f"{tag}_junk", [128, D], BF16)
        self.gbc = T(f"{tag}_g", [128, D])
        self.ssq = [T(f"{tag}_ssq{i}", [128, 1]) for i in range(2)]
        self.rs = [T(f"{tag}_rs{i}", [128, 1]) for i in range(2)]
        self.psT = [P(f"{tag}_ps{i}", [128, D]) for i in range(nxn)] * (2 // nxn)
        self.Bxt, self.Bxn, self.Bssq, self.Brs, self.Bps = ([Buf() for _ in range(2)] for _ in range(5))
        if nxn == 1:
            self.Bxn = [self.Bxn[0]] * 2
            self.Bps = [self.Bps[0]] * 2
        if nxt == 1:
            self.Bxt = [self.Bxt[0]] * 2
        self.Bjunk, self.Bg = (Bjunk if Bjunk is not None else Buf()), Buf()
        self.n = 0
        gbc = self.gbc
        C.S.op("sync", lambda e: e.dma_start(out=gbc[:], in_=gain_row.partition_broadcast(128)), w=[self.Bg], dma=True)

    def emit(self, src_rows, out_ap, Bout, src_buf=None):
        C, S = self.C, self.C.S
        s = self.n % 2
        i = self.n
        self.n += 1
        xt, xn, junk, gbc, ssq, rs, psT = self.xt[s], self.xn[s], self.junk, self.gbc, self.ssq[s], self.rs[s], self.psT[s]
        Bxt, Bxn, Bssq, Brs, Bps = self.Bxt[s], self.Bxn[s], self.Bssq[s], self.Brs[s], self.Bps[s]
        S.op("sync", lambda e: e.dma_start(out=xt[:], in_=src_rows), r=([src_buf] if src_buf else []), w=[Bxt], dma=True)
        S.op("scalar", lambda e: e.activation(out=junk[:], in_=xt[:], func=AF.Square, accum_out=ssq[:]),
             r=[Bxt], w=[self.Bjunk, Bssq])
        rstd_ops(C, ssq[:], rs[:], D, Bssq, Brs)
        S.op("vector", lambda e: e.scalar_tensor_tensor(out=xn[:], in0=xt[:], scalar=rs[:, 0:1], in1=gbc[:],
                                                        op0=ALU.mult, op1=ALU.mult), r=[Bxt, Brs, self.Bg], w=[Bxn])
        for k in range(8):
            S.op("tensor", lambda e, k=k: e.transpose(out=psT[:, k * 128:(k + 1) * 128], in_=xn[:, k * 128:(k + 1) * 128],
                                                      identity=C.ident[:]), r=[Bxn, C.Bconst], w=[Bps])
        src3 = psT[:].rearrange("p (k t) -> p k t", k=8)
        if i % 2 == 0:
            S.op("vector", lambda e: e.tensor_copy(out=out_ap, in_=src3), r=[Bps], w=[Bout])
        else:
            S.op("scalar", lambda e: e.copy(out=out_ap, in_=src3), r=[Bps], w=[Bout])


def in_proj_blocks():
    blks = []
    for j in range(2):
        blks.append((C_Z + 512 * j, 512, [(0, 512, "T", "z_tm", 512 * j, "silu")]))
    for j in range(4):
        blks.append((C_XBC + 512 * j, 512, [(128 * q, 128, "F", "xbcT", 512 * j + 128 * q, None) for q in range(4)]))
    blks.append((C_DT, 16, [(0, 16, "T", "dt_tm", 0, None)]))
    blks.append((C_SBQ, 512, [(128 * q, 128, "F", "sbqT", 128 * q, None) for q in range(4)]))
    blks.append((C_SBK, 512, [(128 * q, 128, "F", "sbkT", 128 * q, "kscale") for q in range(4)]))
    blks.append((C_SBV, 512, [(0, 512, "T", "sbv", 0, None)]))
    blks.append((C_NQ, 512, [(128 * q, 128, "F", "nqT", 128 * q, "qscale") for q in range(4)]))
    blks.append((C_NKV, 512, [(0, 128, "F", "kcmpT", 0, None), (128, 128, "F", "vcmpT", 0, None),
                              (256, 128, "F", "kselT", 0, None), (384, 128, "T", "vsel", 0, None)]))
    blks.append((C_NKV + 512, 256, [(0, 128, "F", "kwinT", 0, None), (128, 128, "T", "vwin", 0, None)]))
    blks.append((C_NG, 24, [(0, 24, "T", "ngate", 0, "sigmoid")]))
    for j in range(6):
        blks.append((C_MG + 512 * j, 512, [(128 * q, 128, "F", "mgT", 512 * j + 128 * q, "sigmoid") for q in range(4)]))
    return blks


def phase_A(C, l, x_src, Bsrc):
    nc, S, SEQ = C.nc, C.S, C.SEQ
    NT = SEQ // 128
    with ExitStack() as st:
        T, P = _pool(C, st)
        hT = T("A_hT", [128, 8, SEQ], BF16)
        BhT = [Buf() for _ in range(SEQ // 512)]
        with ExitStack() as st1:
            nrm = NormT(C, st1, C.dr["pre_mix_norm"][l:l + 1, :], "A1")
            for i in range(NT):
                nrm.emit(x_src[i * 128:(i + 1) * 128, :], hT[:, :, i * 128:(i + 1) * 128], BhT[i // 4], Bsrc)
            S.flush()
        with ExitStack() as st2:
            T2, P2 = _pool(C, st2)
            wt = [T2(f"A_wt{i}", [128, 8, 512], BF16) for i in range(2)]
            Bwt = [Buf() for _ in range(2)]
            stF = [T2(f"A_stF{i}", [128, 2048], BF16) for i in range(2)]
            stT = [T2(f"A_stT{i}", [128, 4, 512], F32) for i in range(2)]
            stTb = [T2(f"A_stTb{i}", [128, 4, 512], BF16) for i in range(2)]
            BstF, BstT, BstTb = ([Buf() for _ in range(2)] for _ in range(3))
            ps = [P2(f"A_ps{i}", [128, 512]) for i in range(4)]
            Bps = [Buf() for _ in range(4)]
            w_l = C.dr["w_in"][l].rearrange("(kc p) f -> p kc f", p=128)
            cnt = {"ps": 0, "F": 0, "T": 0, "Tb": 0, "ev": 0}
            for bi, (c0, wd, subs) in enumerate(in_proj_blocks()):
                ws = bi % 2
                S.op("gpsimd", lambda e, ws=ws, c0=c0, wd=wd: e.dma_start(out=wt[ws][:, :, 0:wd], in_=w_l[:, :, c0:c0 + wd]),
                     w=[Bwt[ws]], dma=True)
                for (off, w, kind, dname, doff, act) in subs:
                    dst = C.dr[dname]
                    if kind == "F":
                        for tt in range(SEQ // 512):
                            pi = cnt["ps"] % 4
                            cnt["ps"] += 1
                            for k in range(8):
                                S.op("tensor", lambda e, pi=pi, ws=ws, k=k, off=off, tt=tt: e.matmul(
                                    ps[pi][:, :], lhsT=wt[ws][:, k, off:off + 128], rhs=hT[:, k, tt * 512:(tt + 1) * 512],
                                    start=(k == 0), stop=(k == 7)), r=[Bwt[ws], BhT[tt]], w=[Bps[pi]])
                            nq = min(4, SEQ // 512)
                            si = (cnt["F"] // nq) % 2
                            q = cnt["F"] % nq
                            cnt["F"] += 1
                            o_ap = stF[si][:, q * 512:(q + 1) * 512]
                            if act == "sigmoid":
                                S.op("scalar", lambda e, o_ap=o_ap, pi=pi: e.activation(out=o_ap, in_=ps[pi][:, :], func=AF.Sigmoid),
                                     r=[Bps[pi]], w=[BstF[si]])
                            elif act in ("kscale", "qscale"):
                                mulv = float(128 ** -0.5) if act == "kscale" else 0.125
                                S.op("scalar", lambda e, o_ap=o_ap, pi=pi, mulv=mulv: e.mul(out=o_ap, in_=ps[pi][:, :], mul=mulv),
                                     r=[Bps[pi]], w=[BstF[si]])
                            else:
                                cnt["ev"] += 1
                                if cnt["ev"] % 2:
                                    S.op("vector", lambda e, o_ap=o_ap, pi=pi: e.tensor_copy(out=o_ap, in_=ps[pi][:, :]),
                                         r=[Bps[pi]], w=[BstF[si]])
                                else:
                                    S.op("scalar", lambda e, o_ap=o_ap, pi=pi: e.copy(out=o_ap, in_=ps[pi][:, :]),
                                         r=[Bps[pi]], w=[BstF[si]])
                            if q == nq - 1:
                                t0 = (tt + 1 - nq) * 512
                                S.op("sync", lambda e, si=si, dst=dst, doff=doff, t0=t0, nq=nq: e.dma_start(
                                    out=dst[doff:doff + 128, t0:t0 + nq * 512], in_=stF[si][:, 0:nq * 512]),
                                    r=[BstF[si]], w=[C.Bdr[dname]], dma=True)
                    else:
                        isb = dst.dtype == BF16
                        stg, Bstg, ck = (stTb, BstTb, "Tb") if isb else (stT, BstT, "T")
                        for i in range(NT):
                            pi = cnt["ps"] % 4
                            cnt["ps"] += 1
                            for k in range(8):
                                S.op("tensor", lambda e, pi=pi, ws=ws, k=k, off=off, w=w, i=i: e.matmul(
                                    ps[pi][:, 0:w], lhsT=hT[:, k, i * 128:(i + 1) * 128], rhs=wt[ws][:, k, off:off + w],
                                    start=(k == 0), stop=(k == 7)), r=[Bwt[ws], BhT[i // 4]], w=[Bps[pi]])
                            si = (cnt[ck] // 4) % 2
                            q = cnt[ck] % 4
                            cnt[ck] += 1
                            o_ap = stg[si][:, q, 0:w]
                            if act == "silu":
                                S.op("scalar", lambda e, o_ap=o_ap, pi=pi, w=w: e.activation(out=o_ap, in_=ps[pi][:, 0:w], func=AF.Silu),
                                     r=[Bps[pi]], w=[Bstg[si]])
                            elif act == "sigmoid":
                                S.op("scalar", lambda e, o_ap=o_ap, pi=pi, w=w: e.activation(out=o_ap, in_=ps[pi][:, 0:w], func=AF.Sigmoid),
                                     r=[Bps[pi]], w=[Bstg[si]])
                            else:
                                S.op("vector", lambda e, o_ap=o_ap, pi=pi, w=w: e.tensor_copy(out=o_ap, in_=ps[pi][:, 0:w]),
                                     r=[Bps[pi]], w=[Bstg[si]])
                            if q == 3:
                                t0 = (i - 3) * 128
                                S.op("sync", lambda e, stg=stg, si=si, dst=dst, doff=doff, t0=t0, w=w: e.dma_start(
                                    out=dst[t0:t0 + 512, doff:doff + w].rearrange("(t p) w -> p t w", p=128), in_=stg[si][:, :, 0:w]),
                                    r=[Bstg[si]], w=[C.Bdr[dname]], dma=True)
            S.flush()


def phase_Ea(C, l, x_src, Bsrc):
    nc, S, SEQ = C.nc, C.S, C.SEQ
    dr = C.dr
    with ExitStack() as st:
        T, P = _pool(C, st)
        Wssm = T("Ea_wssm", [128, 8, D], BF16)
        Wsb = T("Ea_wsb", [128, 4, D], BF16)
        Wnsa = T("Ea_wnsa", [128, 4, D], BF16)
        Wout = T("Ea_wout", [128, 8, D], BF16)
        gbc = T("Ea_g", [128, D])
        BW = Buf()
        for (t_, nm, kc) in ((Wssm, "w_br_ssm", 8), (Wsb, "w_br_sb", 4), (Wnsa, "w_br_nsa", 4), (Wout, "w_out", 8)):
            S.op("gpsimd", lambda e, t_=t_, nm=nm: e.dma_start(out=t_[:], in_=dr[nm][l].rearrange("(kc p) f -> p kc f", p=128)),
                 w=[BW], dma=True)
        S.op("sync", lambda e: e.dma_start(out=gbc[:], in_=dr["post_mix_norm"][l:l + 1, :].partition_broadcast(128)), w=[BW], dma=True)
        ys = [T(f"Ea_ys{i}", [128, 8, 512], BF16) for i in range(2)]
        yb = [T(f"Ea_yb{i}", [128, 4, 512], BF16) for i in range(2)]
        yn = [T(f"Ea_yn{i}", [128, 4, 512], BF16) for i in range(2)]
        By = [Buf() for _ in range(2)]
        mg = [T(f"Ea_mg{i}", [128, 3, 512], BF16) for i in range(2)]
        Bmg = [Buf() for _ in range(2)]
        t0_, t1_, t2_ = (T(f"Ea_t{i}", [128, 512]) for i in range(3))
        Bt = [Buf() for _ in range(3)]
        mixT2 = [T(f"Ea_mixT{i}", [128, 8, 512], BF16) for i in range(2)]
        Bmix2 = [Buf() for _ in range(2)]
        psb = [P(f"Ea_ps{i}", [128, 512]) for i in range(6)]
        Bpsb = [Buf() for _ in range(6)]
        pso = P("Ea_pso", [128, D])
        Bpso = Buf()
        xt = [T(f"Ea_xt{i}", [128, D]) for i in range(2)]
        Bxt = [Buf() for _ in range(2)]
        tn = T("Ea_tn", [128, D])
        Btn = Buf()
        xo = [T(f"Ea_xo{i}", [128, D]) for i in range(2)]
        Bxo = [Buf() for _ in range(2)]
        junk = T("Ea_junk", [128, D], BF16)
        ssq = T("Ea_ssq", [128, 1])
        rs = T("Ea_rs", [128, 1])
        Bjunk, Bssq, Brs = Buf(), Buf(), Buf()
        mgT = dr["mgT"].rearrange("(j o p) t -> p j o t", j=3, p=128)
        def out_pieces(tt):
            mixT, Bmix = mixT2[tt % 2], Bmix2[tt % 2]
            pcs = []
            for sub in range(4):
                i = tt * 4 + sub
                xs = i % 2

                def ld(xs=xs, i=i):
                    S.op("sync", lambda e: e.dma_start(out=xt[xs][:], in_=x_src[i * 128:(i + 1) * 128, :]), r=[Bsrc], w=[Bxt[xs]], dma=True)
                pcs.append(ld)
                for half in range(2):
                    for o in range(8):
                        def mm(half=half, o=o, sub=sub):
                            S.op("tensor", lambda e: e.matmul(pso[:, half * 512:(half + 1) * 512], lhsT=mixT[:, o, sub * 128:(sub + 1) * 128],
                                                              rhs=Wout[:, o, half * 512:(half + 1) * 512], start=(o == 0), stop=(o == 7)),
                                 r=[Bmix, BW], w=[Bpso])
                        pcs.append(mm)

                def post(xs=xs, i=i):
                    post_norm_residual(C, pso, Bpso, gbc, BW, xt[xs], Bxt[xs], xo[xs], Bxo[xs], tn, Btn, junk, Bjunk, ssq, Bssq, rs, Brs)
                    S.op("sync", lambda e: e.dma_start(out=dr["x1"][i * 128:(i + 1) * 128, :], in_=xo[xs][:]),
                         r=[Bxo[xs]], w=[C.Bdr["x1"]], dma=True)
                pcs.append(post)
            return pcs

        pend = []
        NT5 = SEQ // 512
        for tt in range(NT5):
            s = tt % 2
            mixT, Bmix = mixT2[tt % 2], Bmix2[tt % 2]
            tok = slice(tt * 512, (tt + 1) * 512)
            S.op("sync", lambda e, s=s, tok=tok: e.dma_start(out=ys[s][:], in_=dr["yssmT"][:, tok].rearrange("(c p) t -> p c t", p=128)),
                 r=[C.Bdr["yssmT"]], w=[By[s]], dma=True)
            S.op("sync", lambda e, s=s, tok=tok: e.dma_start(out=yb[s][:], in_=dr["ysbT"][:, tok].rearrange("(c p) t -> p c t", p=128)),
                 r=[C.Bdr["ysbT"]], w=[By[s]], dma=True)
            S.op("sync", lambda e, s=s, tok=tok: e.dma_start(out=yn[s][:], in_=dr["ynsaT"][:, tok].rearrange("(c p) t -> p c t", p=128)),
                 r=[C.Bdr["ynsaT"]], w=[By[s]], dma=True)
            npp = (len(pend) + 7) // 8
            for o in range(8):
                ms = o % 2
                S.op("sync", lambda e, ms=ms, o=o, tok=tok: e.dma_start(out=mg[ms][:], in_=mgT[:, :, o, tok]),
                     r=[C.Bdr["mgT"]], w=[Bmg[ms]], dma=True)
                pb = 3 * (o % 2)
                osl = slice(o * 128, (o + 1) * 128)
                for c in range(8):
                    S.op("tensor", lambda e, pb=pb, c=c, osl=osl, s=s: e.matmul(psb[pb][:, :], lhsT=Wssm[:, c, osl], rhs=ys[s][:, c, :],
                                                                               start=(c == 0), stop=(c == 7)), r=[BW, By[s]], w=[Bpsb[pb]])
                for c in range(4):
                    S.op("tensor", lambda e, pb=pb, c=c, osl=osl, s=s: e.matmul(psb[pb + 1][:, :], lhsT=Wsb[:, c, osl], rhs=yb[s][:, c, :],
                                                                               start=(c == 0), stop=(c == 3)), r=[BW, By[s]], w=[Bpsb[pb + 1]])
                for c in range(4):
                    S.op("tensor", lambda e, pb=pb, c=c, osl=osl, s=s: e.matmul(psb[pb + 2][:, :], lhsT=Wnsa[:, c, osl], rhs=yn[s][:, c, :],
                                                                               start=(c == 0), stop=(c == 3)), r=[BW, By[s]], w=[Bpsb[pb + 2]])
                S.op("vector", lambda e, pb=pb, ms=ms: vtt(e, out=t0_[:], in0=psb[pb][:, :], in1=mg[ms][:, 0, :], op=ALU.mult),
                     r=[Bpsb[pb], Bmg[ms]], w=[Bt[0]])
                S.op("vector", lambda e, pb=pb, ms=ms: vtt(e, out=t1_[:], in0=psb[pb + 1][:, :], in1=mg[ms][:, 1, :], op=ALU.mult),
                     r=[Bpsb[pb + 1], Bmg[ms]], w=[Bt[1]])
                S.op("vector", lambda e, pb=pb, ms=ms: vtt(e, out=t2_[:], in0=psb[pb + 2][:, :], in1=mg[ms][:, 2, :], op=ALU.mult),
                     r=[Bpsb[pb + 2], Bmg[ms]], w=[Bt[2]])
                S.op("gpsimd", lambda e: e.tensor_tensor(out=t0_[:], in0=t0_[:], in1=t1_[:], op=ALU.add), r=[Bt[0], Bt[1]], w=[Bt[0]])
                S.op("gpsimd", lambda e, o=o, mixT=mixT: e.tensor_tensor(out=mixT[:, o, :], in0=t0_[:], in1=t2_[:], op=ALU.add),
                     r=[Bt[0], Bt[2]], w=[Bmix])
                for _ in range(npp):
                    if pend:
                        pend.pop(0)()
            while pend:
                pend.pop(0)()
            pend = out_pieces(tt)
        while pend:
            pend.pop(0)()
        S.flush()


def post_norm_residual(C, pso, Bpso, gbc, Bg, xt, Bxt, xo, Bxo, tn, Btn, junk, Bjunk, ssq, Bssq, rs, Brs):
    S = C.S
    S.op("scalar", lambda e: e.activation(out=junk[:], in_=pso[:, :], func=AF.Square, accum_out=ssq[:]),
         r=[Bpso], w=[Bjunk, Bssq])
    rstd_ops(C, ssq[:], rs[:], D, Bssq, Brs)
    for half in range(2):
        hs = slice(half * 512, (half + 1) * 512)
        S.op("vector", lambda e, hs=hs: e.scalar_tensor_tensor(out=tn[:, hs], in0=pso[:, hs], scalar=rs[:, 0:1], in1=gbc[:, hs],
                                                               op0=ALU.mult, op1=ALU.mult), r=[Bpso, Brs, Bg], w=[Btn])
    S.op("gpsimd", lambda e: e.tensor_tensor(out=xo[:], in0=tn[:], in1=xt[:], op=ALU.add), r=[Btn, Bxt], w=[Bxo])


def phase_Eb(C, l, dst, Bdst):
    nc, S, SEQ = C.nc, C.S, C.SEQ
    dr = C.dr
    import os
    TT = 256
    NTT = min(SEQ // TT, int(os.environ.get("EB_MAXTT", "100000")))
    nsub = TT // 128
    with ExitStack() as st:
        T, P = _pool(C, st)
        Wup = T("Eb_wup", [128, 8, 2 * DFF], BF16)
        Wdn = T("Eb_wdn", [128, 22, D], BF16)
        cw = T("Eb_cw", [128, 44, 3])
        cb = T("Eb_cb", [128, 44])
        gbc = T("Eb_g", [128, D])
        halo = T("Eb_halo", [128, 44, 2])
        BW = Buf()
        Bhalo = [Buf() for _ in range(44)]
        upv = dr["ffn_w_up"][l].rearrange("(kc p) f -> p kc f", p=128)
        for j in range(11):
            S.op("gpsimd", lambda e, j=j: e.dma_start(out=Wup[:, :, j * 512:(j + 1) * 512], in_=upv[:, :, j * 512:(j + 1) * 512]),
                 w=[BW], dma=True)
        S.op("gpsimd", lambda e: e.dma_start(out=Wdn[:], in_=dr["ffn_w_down"][l].rearrange("(kc p) f -> p kc f", p=128)), w=[BW], dma=True)
        S.op("sync", lambda e: e.dma_start(out=cw[:], in_=dr["ffn_conv_wT"][l]), w=[BW], dma=True)
        S.op("sync", lambda e: e.dma_start(out=cb[:], in_=dr["ffn_conv_bT"][l]), w=[BW], dma=True)
        S.op("sync", lambda e: e.dma_start(out=gbc[:], in_=dr["post_ffn_norm"][l:l + 1, :].partition_broadcast(128)), w=[BW], dma=True)
        S.op("gpsimd", lambda e: e.memset(halo[:], 0.0), w=Bhalo)
        h2T = T("Eb_h2T", [128, 8, 2 * TT], BF16)
        Bh2 = [Buf() for _ in range(2)]
        uext = [[T(f"Eb_ue{g}{i}", [128, TT + 2]) for i in range(2)] for g in range(2)]
        Bue = [[Buf() for _ in range(2)] for _ in range(2)]
        acc = [[T(f"Eb_acc{g}{i}", [128, TT]) for i in range(2)] for g in range(2)]
        Bacc = [[Buf() for _ in range(2)] for _ in range(2)]
        gg = T("Eb_gg", [128, TT])
        Bgg = Buf()
        actT2 = [T(f"Eb_actT{i}", [128, 22, TT], BF16) for i in range(2)]
        Bact2 = [Buf() for _ in range(2)]
        psu = [[P(f"Eb_psu{g}{i}", [128, 512]) for i in range(2)] for g in range(2)]
        Bpsu = [[Buf() for _ in range(2)] for _ in range(2)]
        psd = P("Eb_psd", [128, D])
        Bpsd = Buf()
        xt = [T("Eb_xt0", [128, D])] * 2
        Bxt = [Buf()] * 2
        tn, Btn = xt[0], None
        xo = [T("Eb_xo0", [128, D])] * 2
        Bxo = [Buf()] * 2
        junk = T("Eb_junk", [128, D], BF16)
        ssq = T("Eb_ssq", [128, 1])
        rs = T("Eb_rs", [128, 1])
        Bjunk, Bssq, Brs = Buf(), Buf(), Buf()
        nrm = NormT(C, st, dr["pre_ffn_norm"][l:l + 1, :], "Ebn", nxn=1, junk=junk, Bjunk=Bjunk, nxt=1)

        def emit_norm(tt):
            hs = tt % 2
            for sub in range(nsub):
                i = tt * nsub + sub
                nrm.emit(dr["x1"][i * 128:(i + 1) * 128, :], h2T[:, :, hs * TT + sub * 128: hs * TT + (sub + 1) * 128], Bh2[hs], C.Bdr["x1"])
        def down_pieces(tt):
            actT, Bact = actT2[tt % 2], Bact2[tt % 2]
            pcs = []
            for sub in range(nsub):
                i = tt * nsub + sub
                xs = i % 2

                def ld(xs=xs, i=i):
                    S.op("sync", lambda e: e.dma_start(out=xt[xs][:], in_=dr["x1"][i * 128:(i + 1) * 128, :]),
                         r=[C.Bdr["x1"]], w=[Bxt[xs]], dma=True)
                pcs.append(ld)
                for half in range(2):
                    for j in range(22):
                        def mm(half=half, j=j, sub=sub):
                            S.op("tensor", lambda e: e.matmul(psd[:, half * 512:(half + 1) * 512], lhsT=actT[:, j, sub * 128:(sub + 1) * 128],
                                                              rhs=Wdn[:, j, half * 512:(half + 1) * 512], start=(j == 0), stop=(j == 21)),
                                 r=[Bact, BW], w=[Bpsd])
                        pcs.append(mm)

                def post(xs=xs, i=i):
                    post_norm_residual(C, psd, Bpsd, gbc, BW, xt[xs], Bxt[xs], xo[xs], Bxo[xs], xo[xs], Bxo[xs], junk, Bjunk, ssq, Bssq, rs, Brs)
                    S.op("sync", lambda e: e.dma_start(out=dst[i * 128:(i + 1) * 128, :], in_=xo[xs][:]),
                         r=[Bxo[xs]], w=[Bdst], dma=True)
                pcs.append(post)
            return pcs

        emit_norm(0)
        pend = []
        for tt in range(NTT):
            hs = tt % 2
            actT, Bact = actT2[tt % 2], Bact2[tt % 2]
            if tt + 1 < NTT:
                emit_norm(tt + 1)
            hsl = slice(hs * TT, (hs + 1) * TT)
            npp = (len(pend) + 21) // 22
            for j in range(22):
                sl = j % 2
                for g in range(2):
                    ch = g * 22 + j
                    csl = slice(ch * 128, (ch + 1) * 128)
                    for k in range(8):
                        S.op("tensor", lambda e, g=g, sl=sl, k=k, csl=csl, hsl=hsl: e.matmul(
                            psu[g][sl][:, 0:TT], lhsT=Wup[:, k, csl], rhs=h2T[:, k, hsl], start=(k == 0), stop=(k == 7)),
                            r=[BW, Bh2[hs]], w=[Bpsu[g][sl]])
                    ue, Bu = uext[g][sl], Bue[g][sl]
                    ac, Ba = acc[g][sl], Bacc[g][sl]
                    S.op("scalar", lambda e, ue=ue, ch=ch: e.copy(out=ue[:, 0:2], in_=halo[:, ch, :]), r=[Bhalo[ch]], w=[Bu])
                    S.op("scalar", lambda e, ue=ue, g=g, sl=sl: e.copy(out=ue[:, 2:TT + 2], in_=psu[g][sl][:, 0:TT]),
                         r=[Bpsu[g][sl]], w=[Bu])
                    S.op("scalar", lambda e, ue=ue, ch=ch: e.copy(out=halo[:, ch, :], in_=ue[:, TT:TT + 2]), r=[Bu], w=[Bhalo[ch]])
                    S.op("vector", lambda e, ue=ue, ac=ac, ch=ch: e.tensor_scalar(out=ac[:], in0=ue[:, 0:TT], scalar1=cw[:, ch, 0:1],
                                                                                  scalar2=cb[:, ch:ch + 1], op0=ALU.mult, op1=ALU.add),
                         r=[Bu, BW], w=[Ba])
                    S.op("vector", lambda e, ue=ue, ac=ac, ch=ch: e.scalar_tensor_tensor(out=ac[:], in0=ue[:, 1:TT + 1], scalar=cw[:, ch, 1:2],
                                                                                         in1=ac[:], op0=ALU.mult, op1=ALU.add),
                         r=[Bu, BW, Ba], w=[Ba])
                    S.op("vector", lambda e, ue=ue, ac=ac, ch=ch: e.scalar_tensor_tensor(out=ac[:], in0=ue[:, 2:TT + 2], scalar=cw[:, ch, 2:3],
                                                                                         in1=ac[:], op0=ALU.mult, op1=ALU.add),
                         r=[Bu, BW, Ba], w=[Ba])
                S.op("scalar", lambda e, sl=sl: e.activation(out=gg[:], in_=acc[0][sl][:], func=AF.Gelu_apprx_tanh), r=[Bacc[0][sl]], w=[Bgg])
                S.op("vector", lambda e, sl=sl, j=j, actT=actT: vtt(e, out=actT[:, j, :], in0=gg[:], in1=acc[1][sl][:], op=ALU.mult),
                     r=[Bgg, Bacc[1][sl]], w=[Bact])
                for _ in range(npp):
                    if pend:
                        pend.pop(0)()
            while pend:
                pend.pop(0)()
            pend = down_pieces(tt)
            if tt % 16 == 15 and tt + 1 < NTT:
                while pend:
                    pend.pop(0)()
                S.flush()
        while pend:
            pend.pop(0)()
        S.flush()


SCRATCH = {
    "z_tm": (lambda S: [S, 1024], F32), "xbcT": (lambda S: [2048, S], BF16), "dt_tm": (lambda S: [S, 16], F32),
    "sbqT": (lambda S: [512, S], BF16), "sbkT": (lambda S: [512, S], BF16), "sbv": (lambda S: [S, 512], BF16),
    "nqT": (lambda S: [512, S], BF16), "kcmpT": (lambda S: [128, S], BF16), "vcmpT": (lambda S: [128, S], BF16),
    "kselT": (lambda S: [128, S], BF16), "vsel": (lambda S: [S, 128], BF16), "kwinT": (lambda S: [128, S], BF16),
    "vwin": (lambda S: [S, 128], BF16), "ngate": (lambda S: [S, 24], F32), "mgT": (lambda S: [3072, S], BF16),
    "yssmT": (lambda S: [1024, S], BF16), "ysbT": (lambda S: [512, S], BF16), "ynsaT": (lambda S: [512, S], BF16),
    "x1": (lambda S: [S, 1024], F32), "xmid": (lambda S: [S, 1024], F32),
    "ssm_xB": (lambda S: [S, 1536], BF16), "ssm_BT": (lambda S: [512, S], BF16), "ssm_CT": (lambda S: [512, S], BF16),
}
WEIGHTS = {
    "pre_mix_norm": [L_, D], "w_in": [L_, D, DIN], "w_br_ssm": [L_, 1024, D], "w_br_sb": [L_, 512, D], "w_br_nsa": [L_, 512, D],
    "w_out": [L_, D, D], "post_mix_norm": [L_, D], "pre_ffn_norm": [L_, D], "ffn_w_up": [L_, D, 2 * DFF],
    "ssm_conv_wT": [L_, 128, 16, 4], "ssm_conv_bT": [L_, 128, 16], "ssm_dt_bias": [L_, 16], "ssm_a_log": [L_, 16], "ssm_d": [L_, 16],
    "ssm_norm": [L_, D], "c_tri": [128, 128], "c_sgt": [128, 128], "c_nuti": [128, 128], "c_sbmask": [128, 4, 512], "c_sbneg": [128, 4, 512],
    "cmp_w1_k": [L_, 2048, 64], "cmp_w2_k": [L_, 64, 64], "cmp_posT_k": [L_, 64, 32],
    "cmp_w1_v": [L_, 2048, 64], "cmp_w2_v": [L_, 64, 64], "cmp_posT_v": [L_, 64, 32],
    "ffn_conv_wT": [L_, 128, 44, 3], "ffn_conv_bT": [L_, 128, 44], "ffn_w_down": [L_, DFF, D], "post_ffn_norm": [L_, D],
}


def build(SEQ, phases, ext_in=(), ext_out=(), nlayers=L_):
    nc = bass.Bass("TRN2", target_bir_lowering=False)
    C = Ctx()
    C.nc, C.SEQ = nc, SEQ
    C.dr, C.Bdr = {}, {}
    x_in = nc.dram_tensor("x", [SEQ, D], F32, kind="ExternalInput").ap()
    out = nc.dram_tensor("out", [SEQ, D], F32, kind="ExternalOutput").ap()
    for nm, shp in WEIGHTS.items():
        C.dr[nm] = nc.dram_tensor(nm, shp, F32, kind="ExternalInput").ap()
    C.dr["ident"] = nc.dram_tensor("ident", [128, 128], F32, kind="ExternalInput").ap()
    for nm, shp in nsa_const_shapes(SEQ).items():
        C.dr[nm] = nc.dram_tensor(nm, shp, F32, kind="ExternalInput").ap()
    for nm, (sf, dt) in SCRATCH.items():
        kind = "ExternalInput" if nm in ext_in else ("ExternalOutput" if nm in ext_out else "Internal")
        C.dr[nm] = nc.dram_tensor(nm, sf(SEQ), dt, kind=kind).ap()
        C.Bdr[nm] = Buf(nm)
    C.Bx, C.Bout = Buf(), Buf()
    with ExitStack() as st:
        C.S = Sched(nc, st)
        T, P = _pool(C, st)
        C.ident = T("c_ident", [128, 128])
        C.eps_t = T("c_eps", [128, 1])
        C.Bconst = Buf()
        C.S.op("sync", lambda e: e.dma_start(out=C.ident[:], in_=C.dr["ident"][:, :]), w=[C.Bconst], dma=True)
        C.S.op("gpsimd", lambda e: e.memset(C.eps_t[:], EPS), w=[C.Bconst])
        C.identb = T("c_identb", [128, 128], BF16)
        C.one_t = T("c_one", [128, 1])
        C.S.op("gpsimd", lambda e: e.memset(C.one_t[:], 1.0), w=[C.Bconst])
        C.S.op("vector", lambda e: e.tensor_copy(out=C.identb[:], in_=C.ident[:]), r=[C.Bconst], w=[C.Bconst])
        x_src, Bsrc = x_in, C.Bx
        for l in range(nlayers):
            last = l == nlayers - 1
            if "A" in phases:
                phase_A(C, l, x_src, Bsrc)
            if "B" in phases:
                phase_B0(C, l)
                phase_B1(C, l)
            if "SB" in phases:
                phase_SB(C, l)
            if "NSA" in phases:
                phase_NSA(C, l)
            if "Ea" in phases:
                phase_Ea(C, l, x_src, Bsrc)
            if "Eb" in phases:
                phase_Eb(C, l, out if last else C.dr["xmid"], C.Bout if last else C.Bdr["xmid"])
            x_src, Bsrc = C.dr["xmid"], C.Bdr["xmid"]
        C.S.flush()
    return nc


def nsa_const_shapes(SEQ):
    NCP = ((SEQ // 16 - 1 + 127) // 128) * 128
    return {"c_qaug": [4, 8, SEQ], "c_kaug": [4, SEQ], "c_kcaug": [4, NCP], "c_pool": [128, NCP // 128, 128], "c_G": [128, SEQ],
            "c_caus": [128, 4, 128], "c_anti": [128, 4, 128], "c_cm": [128, 16, 2, 128], "c_patf": [128, 256], "c_patv": [128, 256]}


def nsa_consts(SEQ):
    NC_ = SEQ // 16 - 1
    NCP = ((NC_ + 127) // 128) * 128
    t = np.arange(SEQ)
    slopes = 2.0 ** (-(np.arange(8) + 1.0))
    qaug = np.zeros((4, 8, SEQ), np.float32)
    qaug[0] = slopes[:, None]
    qaug[1] = 128.0 * slopes[:, None]
    qaug[2] = -128.0 * slopes[:, None] * (t // 128)[None, :]
    qaug[3] = 2048.0 * slopes[:, None]
    kaug = np.zeros((4, SEQ), np.float32)
    kaug[0] = (t % 128) - 127
    kaug[1] = t // 128
    kaug[2] = 1.0
    n = np.arange(NCP)
    kcaug = np.zeros((4, NCP), np.float32)
    kcaug[0] = 16.0 * ((n % 128) - 6)
    kcaug[2] = 1.0
    kcaug[3] = n // 128
    j = np.arange(128)
    pool = np.zeros((NCP, 128), np.float32)
    for nn in range(NC_):
        for jj in range(max(0, (nn - 3 + 3) // 4), 128):
            if 4 * jj - 1 <= nn <= 4 * jj + 3:
                pool[nn, jj] = 1.0
            if 4 * jj - 1 > nn:
                break
    pool = np.ascontiguousarray(pool.reshape(NCP // 128, 128, 128).transpose(1, 0, 2))
    G = (t[None, :] // 64 == j[:, None]).astype(np.float32)
    k = np.arange(128)[:, None]
    q = np.arange(128)[None, :]
    caus = np.where(k <= q, 0.0, NEGB).astype(np.float32)
    anti = np.where(k > q, 0.0, NEGB).astype(np.float32)
    caus4 = np.ascontiguousarray(np.broadcast_to(caus[:, None, :], (128, 4, 128)))
    anti4 = np.ascontiguousarray(np.broadcast_to(anti[:, None, :], (128, 4, 128)))
    cm = np.zeros((128, 16, 2, 128), np.float32)
    for r in range(16):
        for idx in range(2):
            rel = 1 - idx
            cm[:, r, idx, :] = np.where(16 * k + 31 <= 128 * (r + 16 * rel) + q, 0.0, NEGB)
    u = np.arange(256)[None, :] - 127
    cq = (np.arange(128) // 64)[:, None]
    patf = np.where((u == cq) | (u == cq - 1), 1e30, -3e38).astype(np.float32)
    patv = np.where(u <= cq, 3e38, -1e30).astype(np.float32)
    return {"c_qaug": qaug, "c_kaug": kaug, "c_kcaug": kcaug, "c_pool": pool, "c_G": G, "c_caus": caus4, "c_anti": anti4,
            "c_cm": cm, "c_patf": patf, "c_patv": patv}


def host_inputs(inputs):
    f = lambda a: np.ascontiguousarray(np.asarray(a, dtype=np.float32))
    w = {k: f(inputs[k]) for k in ("pre_mix_norm", "w_in", "w_br_ssm", "w_br_sb", "w_br_nsa", "w_out", "post_mix_norm",
                                   "pre_ffn_norm", "ffn_w_up", "ffn_w_down", "post_ffn_norm")}
    w["ffn_conv_wT"] = f(np.transpose(np.asarray(inputs["ffn_conv_w"]).reshape(L_, 3, 44, 128), (0, 3, 2, 1)))
    w["ffn_conv_bT"] = f(np.transpose(np.asarray(inputs["ffn_conv_b"]).reshape(L_, 44, 128), (0, 2, 1)))
    w["ident"] = np.eye(128, dtype=np.float32)
    w["ssm_conv_wT"] = f(np.transpose(np.asarray(inputs["ssm_conv_w"]).reshape(L_, 4, 16, 128), (0, 3, 2, 1)))
    w["ssm_conv_bT"] = f(np.transpose(np.asarray(inputs["ssm_conv_b"]).reshape(L_, 16, 128), (0, 2, 1)))
    for k in ("ssm_dt_bias", "ssm_a_log", "ssm_d", "ssm_norm", "cmp_w1_k", "cmp_w2_k", "cmp_w1_v", "cmp_w2_v"):
        w[k] = f(inputs[k])
    w["cmp_posT_k"] = f(np.transpose(np.asarray(inputs["cmp_pos_k"]), (0, 2, 1)))
    w["cmp_posT_v"] = f(np.transpose(np.asarray(inputs["cmp_pos_v"]), (0, 2, 1)))
    ii = np.arange(128)
    w["c_tri"] = (ii[:, None] <= ii[None, :]).astype(np.float32)
    w["c_sgt"] = (ii[:, None] > ii[None, :]).astype(np.float32)
    w["c_nuti"] = -(ii[:, None] >= ii[None, :]).astype(np.float32)
    md = np.zeros((128, 4, 512), np.float32)
    for j in range(4):
        for b in range(4):
            if b == j:
                md[:, j, b * 128:(b + 1) * 128] = (ii[:, None] < ii[None, :])
            elif b > j:
                md[:, j, b * 128:(b + 1) * 128] = 1.0
    w["c_sbmask"] = md
    w["c_sbneg"] = ((1.0 - md) * -30000.0).astype(np.float32)
    return w


def phase_B0(C, l):
    nc, S, SEQ = C.nc, C.S, C.SEQ
    dr = C.dr
    with ExitStack() as st:
        T, P = _pool(C, st)
        cw = T("B0_cw", [128, 16, 4])
        cb = T("B0_cb", [128, 16])
        dg = T("B0_dg", [128, 16, 4, 128], BF16)
        BW = Buf()
        S.op("sync", lambda e: e.dma_start(out=cw[:], in_=dr["ssm_conv_wT"][l]), w=[BW], dma=True)
        S.op("sync", lambda e: e.dma_start(out=cb[:], in_=dr["ssm_conv_bT"][l]), w=[BW], dma=True)
        for c in range(16):
            for k in range(4):
                S.op("vector", lambda e, c=c, k=k: e.tensor_scalar(out=dg[:, c, k, :], in0=C.ident[:], scalar1=cw[:, c, k:k + 1], scalar2=None,
                                                                   op0=ALU.mult), r=[BW, C.Bconst], w=[BW])
        xin = [T(f"B0_xin{i}", [128, 16, 515], BF16) for i in range(2)]
        Bxin = [Buf() for _ in range(2)]
        xc = [T(f"B0_xc{i}", [128, 16, 512], BF16) for i in range(2)]
        Bxc = [Buf() for _ in range(2)]
        ps = [P(f"B0_ps{i}", [128, 512]) for i in range(4)]
        Bps = [Buf() for _ in range(4)]
        pst = [P(f"B0_pst{i}", [128, 1024], BF16) for i in range(2)]
        Bpst = [Buf() for _ in range(2)]
        xs = [T(f"B0_xs{i}", [128, 1536], BF16) for i in range(2)]
        Bxs = [Buf() for _ in range(2)]
        NT5 = SEQ // 512
        xv = dr["xbcT"].rearrange("(c p) t -> p c t", p=128)
        n_ps = 0
        for tt in range(NT5):
            s = tt % 2
            t0 = tt * 512
            if tt == 0:
                S.op("gpsimd", lambda e, s=s: e.memset(xin[s][:, :, 0:3], 0.0), w=[Bxin[s]])
                S.op("sync", lambda e, s=s: e.dma_start(out=xin[s][:, :, 3:515], in_=xv[:, :, 0:512]), r=[C.Bdr["xbcT"]], w=[Bxin[s]], dma=True)
            else:
                S.op("sync", lambda e, s=s, t0=t0: e.dma_start(out=xin[s][:, :, :], in_=xv[:, :, t0 - 3:t0 + 512]),
                     r=[C.Bdr["xbcT"]], w=[Bxin[s]], dma=True)
            for c in range(16):
                pi = n_ps % 4
                n_ps += 1
                for k in range(4):
                    S.op("tensor", lambda e, pi=pi, c=c, k=k, s=s: e.matmul(ps[pi][:, :], lhsT=dg[:, c, k, :], rhs=xin[s][:, c, k:k + 512],
                                                                           start=(k == 0), stop=(k == 3)), r=[BW, Bxin[s]], w=[Bps[pi]])
                S.op("scalar", lambda e, pi=pi, c=c, s=s: e.activation(out=xc[s][:, c, :], in_=ps[pi][:, :], func=AF.Silu, bias=cb[:, c:c + 1]),
                     r=[Bps[pi], BW], w=[Bxc[s]])
            S.op("sync", lambda e, s=s, t0=t0: e.dma_start(out=dr["ssm_BT"][:, t0:t0 + 512].rearrange("(c p) t -> p c t", p=128),
                                                           in_=xc[s][:, 8:12, :]), r=[Bxc[s]], w=[C.Bdr["ssm_BT"]], dma=True)
            S.op("sync", lambda e, s=s, t0=t0: e.dma_start(out=dr["ssm_CT"][:, t0:t0 + 512].rearrange("(c p) t -> p c t", p=128),
                                                           in_=xc[s][:, 12:16, :]), r=[Bxc[s]], w=[C.Bdr["ssm_CT"]], dma=True)
            for sub in range(4):
                i = tt * 4 + sub
                q = i % 2
                for c in range(8):
                    S.op("tensor", lambda e, q=q, c=c, s=s, sub=sub: e.transpose(out=pst[q][:, c * 128:(c + 1) * 128],
                                                                                 in_=xc[s][:, c, sub * 128:(sub + 1) * 128], identity=C.identb[:]),
                         r=[Bxc[s], C.Bconst], w=[Bpst[q]])
                S.op("vector", lambda e, q=q: e.tensor_copy(out=xs[q][:, 0:1024], in_=pst[q][:, :]), r=[Bpst[q]], w=[Bxs[q]])
                for c in range(4):
                    S.op("tensor", lambda e, q=q, c=c, s=s, sub=sub: e.transpose(out=pst[q][:, c * 128:(c + 1) * 128],
                                                                                 in_=xc[s][:, 8 + c, sub * 128:(sub + 1) * 128], identity=C.identb[:]),
                         r=[Bxc[s], C.Bconst], w=[Bpst[q]])
                S.op("scalar", lambda e, q=q: e.copy(out=xs[q][:, 1024:1536], in_=pst[q][:, 0:512]), r=[Bpst[q]], w=[Bxs[q]])
                S.op("sync", lambda e, q=q, i=i: e.dma_start(out=dr["ssm_xB"][i * 128:(i + 1) * 128, :], in_=xs[q][:, :]),
                     r=[Bxs[q]], w=[C.Bdr["ssm_xB"]], dma=True)
        S.flush()


def phase_B1(C, l):
    nc, S, SEQ = C.nc, C.S, C.SEQ
    dr = C.dr
    NCH = SEQ // 128
    with ExitStack() as st:
        T, P = _pool(C, st)
        tri = T("B1_tri", [128, 128])
        sgt = T("B1_sgt", [128, 128])
        ones = T("B1_ones", [128, 128])
        dt = T("B1_dt", [128, NCH, 16])
        dta = T("B1_dta", [128, NCH, 16])
        tmpb = T("B1_tmpb", [128, 16])
        ea = T("B1_ea", [128, 16])
        Dbc = T("B1_D", [128, 16])
        nw = T("B1_nw", [128, D])
        BK = Buf()
        S.op("sync", lambda e: e.dma_start(out=tri[:], in_=dr["c_tri"][:, :]), w=[BK], dma=True)
        S.op("sync", lambda e: e.dma_start(out=sgt[:], in_=dr["c_sgt"][:, :]), w=[BK], dma=True)
        S.op("gpsimd", lambda e: e.memset(ones[:], 1.0), w=[BK])
        dtv = dr["dt_tm"].rearrange("(c p) h -> p c h", p=128)
        for c0 in range(0, NCH, 8):
            c1 = min(NCH, c0 + 8)
            S.op("sync", lambda e, c0=c0, c1=c1: e.dma_start(out=dt[:, c0:c1, :], in_=dtv[:, c0:c1, :]), r=[C.Bdr["dt_tm"]], w=[BK], dma=True)
        S.op("sync", lambda e: e.dma_start(out=tmpb[:], in_=dr["ssm_dt_bias"][l:l + 1, :].partition_broadcast(128)), w=[BK], dma=True)
        S.op("sync", lambda e: e.dma_start(out=ea[:], in_=dr["ssm_a_log"][l:l + 1, :].partition_broadcast(128)), w=[BK], dma=True)
        S.op("sync", lambda e: e.dma_start(out=Dbc[:], in_=dr["ssm_d"][l:l + 1, :].partition_broadcast(128)), w=[BK], dma=True)
        S.op("sync", lambda e: e.dma_start(out=nw[:], in_=dr["ssm_norm"][l:l + 1, :].partition_broadcast(128)), w=[BK], dma=True)
        S.op("vector", lambda e: vtt(e, out=dt[:], in0=dt[:], in1=tmpb[:].unsqueeze(1).to_broadcast([128, NCH, 16]), op=ALU.add),
             r=[BK], w=[BK])
        S.op("scalar", lambda e: e.activation(out=dt[:], in_=dt[:], func=AF.Exp), r=[BK], w=[BK])
        S.op("scalar", lambda e: e.activation(out=dt[:], in_=dt[:], func=AF.Ln, bias=C.one_t[:, 0:1]), r=[BK, C.Bconst], w=[BK])
        S.op("scalar", lambda e: e.activation(out=ea[:], in_=ea[:], func=AF.Exp), r=[BK], w=[BK])
        S.op("vector", lambda e: e.scalar_tensor_tensor(out=dta[:], in0=dt[:], scalar=-1.0, in1=ea[:].unsqueeze(1).to_broadcast([128, NCH, 16]),
                                                        op0=ALU.mult, op1=ALU.mult), r=[BK], w=[BK])
        xB = [T(f"B1_xB{i}", [128, 1536], BF16) for i in range(2)]
        bT = [T(f"B1_bT{i}", [128, 4, 128], BF16) for i in range(2)]
        cT = [T(f"B1_cT{i}", [128, 4, 128], BF16) for i in range(2)]
        sz = [T(f"B1_sz{i}", [128, D]) for i in range(2)]
        Bin = [Buf() for _ in range(2)]
        Lm = T("B1_Lm", [128, 16, 128])
        E = T("B1_E", [128, 16, 128])
        CBm = T("B1_CBm", [128, 4, 128])
        Wp = T("B1_Wp", [128, 16, 128], BF16)
        xdt = T("B1_xdt", [128, D], BF16)
        xw = T("B1_xw", [128, D], BF16)
        y1 = T("B1_y1", [128, D])
        t2 = T("B1_t2", [128, D])
        sm = T("B1_sm", [128, 4, 16])
        ssq4 = T("B1_ssq4", [128, 4])
        rs4 = T("B1_rs4", [128, 4])
        junk = T("B1_junk", [128, 256], BF16)
        stt_ = T("B1_st", [128, D])
        stbf = T("B1_stbf", [128, D], BF16)
        ysT = [T(f"B1_ysT{i}", [128, 8, 512], BF16) for i in range(2)]
        BLm, BE, BCBm, BWp, Bxdt, Bxw, By1, Bt2, Bsm, Bssq4, Brs4, Bjunk, Bst, Bstbf = (Buf() for _ in range(14))
        BysT = [Buf() for _ in range(2)]
        ps_seg = P("B1_pseg", [128, 1024])
        ps_cb = P("B1_pcb", [128, 512])
        ps_sm = P("B1_psm", [128, 512])
        ps_yi = P("B1_pyi", [128, 1024])
        ps_yo = P("B1_pyo", [128, 1024])
        Bpseg, Bpcb, Bpsm, Bpyi, Bpyo = (Buf() for _ in range(5))
        S.op("gpsimd", lambda e: e.memset(stt_[:], 0.0), w=[Bst])
        S.op("gpsimd", lambda e: e.memset(stbf[:], 0.0), w=[Bstbf])
        BTv = dr["ssm_BT"].rearrange("(g n) t -> n g t", n=128)
        CTv = dr["ssm_CT"].rearrange("(g n) t -> n g t", n=128)

        def loads(c):
            s = c % 2
            tk = slice(c * 128, (c + 1) * 128)
            S.op("sync", lambda e: e.dma_start(out=xB[s][:], in_=dr["ssm_xB"][tk, :]), r=[C.Bdr["ssm_xB"]], w=[Bin[s]], dma=True)
            S.op("sync", lambda e: e.dma_start(out=bT[s][:], in_=BTv[:, :, tk]), r=[C.Bdr["ssm_BT"]], w=[Bin[s]], dma=True)
            S.op("sync", lambda e: e.dma_start(out=cT[s][:], in_=CTv[:, :, tk]), r=[C.Bdr["ssm_CT"]], w=[Bin[s]], dma=True)
            S.op("sync", lambda e: e.dma_start(out=sz[s][:], in_=dr["z_tm"][tk, :]), r=[C.Bdr["z_tm"]], w=[Bin[s]], dma=True)
        loads(0)
        for c in range(NCH):
            s = c % 2
            if c + 1 < NCH:
                loads(c + 1)
            xtm = xB[s][:, 0:1024]
            btm = xB[s][:, 1024:1536]
            S.op("vector", lambda e, c=c: vtt(e, out=Lm[:], in0=sgt[:].unsqueeze(1).to_broadcast([128, 16, 128]),
                                                          in1=dta[:, c, :].unsqueeze(2).to_broadcast([128, 16, 128]), op=ALU.mult),
                 r=[BK], w=[BLm])
            S.op("tensor", lambda e, c=c: e.matmul(ps_sm[:, 0:16], lhsT=tri[:], rhs=dta[:, c, :], start=True, stop=True), r=[BK], w=[Bpsm])
            S.op("tensor", lambda e, c=c: e.matmul(ps_sm[:, 16:32], lhsT=ones[:], rhs=dta[:, c, :], start=True, stop=True), r=[BK], w=[Bpsm])
            S.op("scalar", lambda e: e.copy(out=sm[:, 0, :], in_=ps_sm[:, 0:16]), r=[Bpsm], w=[Bsm])
            S.op("scalar", lambda e: e.activation(out=sm[:, 1:3, :], in_=ps_sm[:, 0:32].rearrange("p (a h) -> p a h", a=2), func=AF.Exp),
                 r=[Bpsm], w=[Bsm])
            S.op("vector", lambda e: vtt(e, out=sm[:, 3, :], in0=ps_sm[:, 16:32], in1=sm[:, 0, :], op=ALU.subtract), r=[Bpsm, Bsm], w=[Bsm])
            S.op("scalar", lambda e: e.activation(out=sm[:, 3, :], in_=sm[:, 3, :], func=AF.Exp), r=[Bsm], w=[Bsm])
            for g in range(4):
                S.op("tensor", lambda e, g=g, s=s: e.matmul(ps_cb[:, g * 128:(g + 1) * 128], lhsT=bT[s][:, g, :], rhs=cT[s][:, g, :],
                                                            start=True, stop=True), r=[Bin[s]], w=[Bpcb])
            S.op("vector", lambda e: vtt(e, out=CBm[:], in0=ps_cb[:, :].rearrange("p (g t) -> p g t", g=4),
                                                     in1=tri[:].unsqueeze(1).to_broadcast([128, 4, 128]), op=ALU.mult), r=[Bpcb, BK], w=[BCBm])
            for hf in range(2):
                for hh in range(8):
                    h = hf * 8 + hh
                    S.op("tensor", lambda e, h=h, hh=hh: e.matmul(ps_seg[:, hh * 128:(hh + 1) * 128], lhsT=Lm[:, h, :], rhs=tri[:],
                                                                  start=True, stop=True), r=[BLm, BK], w=[Bpseg])
                S.op("scalar", lambda e, hf=hf: e.activation(out=E[:, hf * 8:(hf + 1) * 8, :],
                                                             in_=ps_seg[:, :].rearrange("p (h t) -> p h t", h=8), func=AF.Exp),
                     r=[Bpseg], w=[BE])
            for g in range(4):
                S.op("vector", lambda e, g=g: vtt(e, out=Wp[:, 4 * g:4 * g + 4, :], in0=E[:, 4 * g:4 * g + 4, :],
                                                  in1=CBm[:, g, :].unsqueeze(1).to_broadcast([128, 4, 128]), op=ALU.mult),
                     r=[BE, BCBm], w=[BWp])
            S.op("gpsimd", lambda e, c=c, xtm=xtm: e.tensor_tensor(out=xdt[:].rearrange("p (h q) -> p h q", h=16),
                                                                   in0=xtm.rearrange("p (h q) -> p h q", h=16),
                                                                   in1=dt[:, c, :].unsqueeze(2).to_broadcast([128, 16, 64]), op=ALU.mult),
                 r=[Bin[s], BK], w=[Bxdt])
            for h in range(16):
                S.op("tensor", lambda e, h=h: e.matmul(ps_yi[:, h * 64:(h + 1) * 64], lhsT=Wp[:, h, :], rhs=xdt[:, h * 64:(h + 1) * 64],
                                                       start=True, stop=True), r=[BWp, Bxdt], w=[Bpyi])
            for g in range(4):
                S.op("tensor", lambda e, g=g, s=s: e.matmul(ps_yo[:, g * 256:(g + 1) * 256], lhsT=cT[s][:, g, :], rhs=stbf[:, g * 256:(g + 1) * 256],
                                                            start=True, stop=True), r=[Bin[s], Bstbf], w=[Bpyo])
            S.op("vector", lambda e: vtt(e, out=y1[:].rearrange("p (h q) -> p h q", h=16),
                                                     in0=ps_yo[:, :].rearrange("p (h q) -> p h q", h=16),
                                                     in1=sm[:, 1, :].unsqueeze(2).to_broadcast([128, 16, 64]), op=ALU.mult),
                 r=[Bpyo, Bsm], w=[By1])
            S.op("vector", lambda e: vtt(e, out=y1[:], in0=y1[:], in1=ps_yi[:, :], op=ALU.add), r=[By1, Bpyi], w=[By1])
            S.op("gpsimd", lambda e, xtm=xtm: e.tensor_tensor(out=t2[:].rearrange("p (h q) -> p h q", h=16),
                                                              in0=xtm.rearrange("p (h q) -> p h q", h=16),
                                                              in1=Dbc[:].unsqueeze(2).to_broadcast([128, 16, 64]), op=ALU.mult),
                 r=[Bin[s], BK], w=[Bt2])
            S.op("gpsimd", lambda e: e.tensor_tensor(out=y1[:], in0=y1[:], in1=t2[:], op=ALU.add), r=[By1, Bt2], w=[By1])
            S.op("gpsimd", lambda e, s=s: e.tensor_tensor(out=y1[:], in0=y1[:], in1=sz[s][:], op=ALU.mult), r=[By1, Bin[s]], w=[By1])
            for g in range(4):
                S.op("scalar", lambda e, g=g: e.activation(out=junk[:], in_=y1[:, g * 256:(g + 1) * 256], func=AF.Square, accum_out=ssq4[:, g:g + 1]),
                     r=[By1], w=[Bjunk, Bssq4])
            rstd_ops(C, ssq4[:], rs4[:], 256, Bssq4, Brs4)
            S.op("vector", lambda e: vtt(e, out=y1[:].rearrange("p (g q) -> p g q", g=4), in0=y1[:].rearrange("p (g q) -> p g q", g=4),
                                                     in1=rs4[:].unsqueeze(2).to_broadcast([128, 4, 256]), op=ALU.mult), r=[By1, Brs4], w=[By1])
            S.op("vector", lambda e: vtt(e, out=t2[:], in0=y1[:], in1=nw[:], op=ALU.mult), r=[By1, BK], w=[Bt2])
            for k in range(8):
                S.op("tensor", lambda e, k=k: e.transpose(out=ps_yi[:, k * 128:(k + 1) * 128], in_=t2[:, k * 128:(k + 1) * 128], identity=C.ident[:]),
                     r=[Bt2, C.Bconst], w=[Bpyi])
            ys = (c // 4) % 2
            qq = c % 4
            S.op("scalar", lambda e, ys=ys, qq=qq: e.copy(out=ysT[ys][:, :, qq * 128:(qq + 1) * 128],
                                                           in_=ps_yi[:, :].rearrange("p (k t) -> p k t", k=8)), r=[Bpyi], w=[BysT[ys]])
            if qq == 3:
                t0 = (c - 3) * 128
                S.op("sync", lambda e, ys=ys, t0=t0: e.dma_start(out=dr["yssmT"][:, t0:t0 + 512].rearrange("(k p) t -> p k t", p=128),
                                                                 in_=ysT[ys][:]), r=[BysT[ys]], w=[C.Bdr["yssmT"]], dma=True)
            S.op("gpsimd", lambda e: e.tensor_tensor(out=xw[:].rearrange("p (h q) -> p h q", h=16), in0=xdt[:].rearrange("p (h q) -> p h q", h=16),
                                                     in1=sm[:, 3, :].unsqueeze(2).to_broadcast([128, 16, 64]), op=ALU.mult),
                 r=[Bxdt, Bsm], w=[Bxw])
            for g in range(4):
                S.op("tensor", lambda e, g=g, btm=btm: e.matmul(ps_yo[:, g * 256:(g + 1) * 256], lhsT=btm[:, g * 128:(g + 1) * 128],
                                                                rhs=xw[:, g * 256:(g + 1) * 256], start=True, stop=True),
                     r=[Bin[s], Bxw], w=[Bpyo])
            S.op("vector", lambda e: vtt(e, out=stt_[:].rearrange("p (h q) -> p h q", h=16), in0=stt_[:].rearrange("p (h q) -> p h q", h=16),
                                                     in1=sm[:, 2, :].unsqueeze(2).to_broadcast([128, 16, 64]), op=ALU.mult), r=[Bst, Bsm], w=[Bst])
            S.op("vector", lambda e: vtt(e, out=stt_[:], in0=stt_[:], in1=ps_yo[:, :], op=ALU.add), r=[Bst, Bpyo], w=[Bst])
            S.op("scalar", lambda e: e.copy(out=stbf[:], in_=stt_[:]), r=[Bst], w=[Bstbf])
        S.flush()


def phase_SB(C, l):
    nc, S, SEQ = C.nc, C.S, C.SEQ
    dr = C.dr
    NB = SEQ // 128
    NQG = SEQ // 512
    with ExitStack() as st:
        T, P = _pool(C, st)
        nuti = T("SB_nuti", [128, 128])
        nones = T("SB_nones", [128, 128])
        MD = T("SB_MD", [128, 4, 512])
        NEGM = T("SB_NEGM", [128, 4, 512], BF16)
        BK = Buf()
        S.op("sync", lambda e: e.dma_start(out=nuti[:], in_=dr["c_nuti"][:, :]), w=[BK], dma=True)
        S.op("gpsimd", lambda e: e.memset(nones[:], -1.0), w=[BK])
        S.op("sync", lambda e: e.dma_start(out=MD[:], in_=dr["c_sbmask"][:, :, :]), w=[BK], dma=True)
        S.op("gpsimd", lambda e: e.dma_start(out=NEGM[:], in_=dr["c_sbneg"][:, :, :]), w=[BK], dma=True)
        KT = T("SB_KT", [128, SEQ], BF16)
        QT = T("SB_QT", [128, SEQ], BF16)
        V = T("SB_V", [128, NB, 128], BF16)
        BKT, BQT, BV = Buf(), Buf(), Buf()
        sp = [T(f"SB_sp{i}", [128, 512]) for i in range(3)]
        Bsp = [Buf() for _ in range(3)]
        R = T("SB_R", [128, 512])
        BR = Buf()
        wt = [T(f"SB_w{i}", [128, 512], BF16) for i in range(3)]
        Bw = [Buf() for _ in range(3)]
        osb = [T(f"SB_o{i}", [128, 512], BF16) for i in range(2)]
        Bosb = [Buf() for _ in range(2)]
        psA = [P(f"SB_pA{i}", [128, 512]) for i in range(2)]
        psW = [P(f"SB_pW{i}", [128, 512]) for i in range(2)]
        psO = [P(f"SB_pO{i}", [128, 512]) for i in range(2)]
        BpA, BpW, BpO = ([Buf() for _ in range(2)] for _ in range(3))
        for h in range(4):
            hs = slice(h * 128, (h + 1) * 128)
            S.op("sync", lambda e, hs=hs: e.dma_start(out=KT[:], in_=dr["sbkT"][hs, :]), r=[C.Bdr["sbkT"]], w=[BKT], dma=True)
            S.op("sync", lambda e, hs=hs: e.dma_start(out=QT[:], in_=dr["sbqT"][hs, :]), r=[C.Bdr["sbqT"]], w=[BQT], dma=True)
            vview = dr["sbv"][:, hs].rearrange("(b p) d -> p b d", p=128)
            for b0 in range(0, NB, 8):
                S.op("sync", lambda e, vview=vview, b0=b0: e.dma_start(out=V[:, b0:b0 + 8, :], in_=vview[:, b0:b0 + 8, :]),
                     r=[C.Bdr["sbv"]], w=[BV], dma=True)
            units = []
            for qg in range(NQG):
                for kb in range(4 * qg + 3, -1, -1):
                    units.append((qg, kb))
            U = len(units)

            def emitA(u):
                qg, kb = units[u]
                a = u % 2
                S.op("tensor", lambda e: e.matmul(psA[a][:, :], lhsT=KT[:, kb * 128:(kb + 1) * 128], rhs=QT[:, qg * 512:(qg + 1) * 512],
                                                  start=True, stop=True), r=[BKT, BQT], w=[BpA[a]])

            def emitSP(u):
                qg, kb = units[u]
                a, s3 = u % 2, u % 3
                j = kb - 4 * qg
                S.op("scalar", lambda e: e.activation(out=sp[s3][:], in_=psA[a][:, :], func=AF.Exp), r=[BpA[a]], w=[Bsp[s3]])
                S.op("scalar", lambda e: e.activation(out=sp[s3][:], in_=sp[s3][:], func=AF.Ln, bias=C.one_t[:, 0:1]),
                     r=[Bsp[s3], C.Bconst], w=[Bsp[s3]])
                if j >= 0:
                    S.op("vector", lambda e: vtt(e, out=sp[s3][:], in0=sp[s3][:], in1=MD[:, j, :], op=ALU.mult),
                         r=[Bsp[s3], BK], w=[Bsp[s3]])

            def emitW(u):
                qg, kb = units[u]
                a, s3 = u % 2, u % 3
                j = kb - 4 * qg
                first = kb == 4 * qg + 3
                mm = [(KT[:, kb * 128:(kb + 1) * 128], QT[:, qg * 512:(qg + 1) * 512], [BKT, BQT]), (nuti[:], sp[s3][:], [BK, Bsp[s3]])]
                if not first:
                    mm.append((nones[:], R[:], [BK, BR]))
                if j >= 0:
                    mm.append((C.identb[:], NEGM[:, j, :], [C.Bconst, BK]))
                for i, (lt, rh, rb) in enumerate(mm):
                    S.op("tensor", lambda e, lt=lt, rh=rh, i=i: e.matmul(psW[a][:, :], lhsT=lt, rhs=rh, start=(i == 0), stop=(i == len(mm) - 1)),
                         r=rb, w=[BpW[a]])
                if first:
                    S.op("vector", lambda e: e.tensor_copy(out=R[:], in_=sp[s3][:]), r=[Bsp[s3]], w=[BR])
                else:
                    S.op("vector", lambda e: vtt(e, out=R[:], in0=R[:], in1=sp[s3][:], op=ALU.add), r=[BR, Bsp[s3]], w=[BR])

            def emitEW(u):
                a, s3 = u % 2, u % 3
                S.op("scalar", lambda e: e.activation(out=wt[s3][:], in_=psW[a][:, :], func=AF.Exp), r=[BpW[a]], w=[Bw[s3]])

            def emitPV(u):
                qg, kb = units[u]
                s3 = u % 3
                o = qg % 2
                first = kb == 4 * qg + 3
                S.op("tensor", lambda e: e.matmul(psO[o][:, :], lhsT=V[:, kb, :], rhs=wt[s3][:], start=first, stop=(kb == 0)),
                     r=[BV, Bw[s3]], w=[BpO[o]])
                if kb == 0:
                    S.op("vector", lambda e: e.tensor_copy(out=osb[o][:], in_=psO[o][:, :]), r=[BpO[o]], w=[Bosb[o]])
                    S.op("sync", lambda e, hs=hs: e.dma_start(out=dr["ysbT"][hs, qg * 512:(qg + 1) * 512], in_=osb[o][:]),
                         r=[Bosb[o]], w=[C.Bdr["ysbT"]], dma=True)
            emitA(0)
            emitSP(0)
            for u in range(U):
                if u + 1 < U:
                    emitA(u + 1)
                emitW(u)
                if u + 1 < U:
                    emitSP(u + 1)
                emitEW(u)
                if u >= 1:
                    emitPV(u - 1)
            emitPV(U - 1)
            if h == 1:
                S.flush()
        S.flush()


NEGB = -30000.0


def phase_NSA(C, l):
    nc, S, SEQ = C.nc, C.S, C.SEQ
    dr = C.dr
    NB = SEQ // 128
    NC_ = SEQ // 16 - 1
    NCH = (NC_ + 127) // 128
    NCP = NCH * 128
    for g in range(2):
        with ExitStack() as st:
            T, P = _pool(C, st)
            BK = Buf()
            qa = T("N_qa", [68, 4, SEQ], BF16)
            ksa = T("N_ksa", [68, SEQ], BF16)
            kwa = T("N_kwa", [68, SEQ], BF16)
            kca = T("N_kca", [68, NCP], BF16)
            vs = T("N_vs", [128, NB, 65], BF16)
            vw = T("N_vw", [128, NB, 65], BF16)
            VR = T("N_VR", [128, NCH, 193], BF16)
            G = T("N_G", [128, SEQ], BF16)
            CAUS = T("N_caus", [128, 4, 128], BF16)
            ANTI = T("N_anti", [128, 4, 128], BF16)
            CM = T("N_cm", [128, 16, 2, 128], BF16)
            PATF = T("N_patf", [128, 256])
            PATV = T("N_patv", [128, 256])
            ng = T("N_ng", [128, NB, 24])
            qv = dr["nqT"][g * 256:(g + 1) * 256, :].rearrange("(h d) t -> d h t", d=64)
            S.op("sync", lambda e: e.dma_start(out=qa[0:64, :, :], in_=qv), r=[C.Bdr["nqT"]], w=[BK], dma=True)
            S.op("gpsimd", lambda e: e.dma_start(out=qa[64:68, :, :], in_=dr["c_qaug"][:, 4 * g:4 * g + 4, 0:SEQ]), w=[BK], dma=True)
            S.op("sync", lambda e: e.dma_start(out=ksa[0:64, :], in_=dr["kselT"][g * 64:(g + 1) * 64, :]), r=[C.Bdr["kselT"]], w=[BK], dma=True)
            S.op("sync", lambda e: e.dma_start(out=kwa[0:64, :], in_=dr["kwinT"][g * 64:(g + 1) * 64, :]), r=[C.Bdr["kwinT"]], w=[BK], dma=True)
            S.op("gpsimd", lambda e: e.dma_start(out=ksa[64:68, :], in_=dr["c_kaug"][:, 0:SEQ]), w=[BK], dma=True)
            S.op("gpsimd", lambda e: e.dma_start(out=kwa[64:68, :], in_=dr["c_kaug"][:, 0:SEQ]), w=[BK], dma=True)
            S.op("gpsimd", lambda e: e.memset(kca[0:64, :], 0.0), w=[BK])
            S.op("gpsimd", lambda e: e.dma_start(out=kca[64:68, :], in_=dr["c_kcaug"][:, 0:NCP]), w=[BK], dma=True)
            for (t_, nm) in ((vs, "vsel"), (vw, "vwin")):
                S.op("gpsimd", lambda e, t_=t_: e.memset(t_[:, :, 64:65], 1.0), w=[BK])
                vv = dr[nm][:, g * 64:(g + 1) * 64].rearrange("(b p) d -> p b d", p=128)
                for b0 in range(0, NB, 8):
                    S.op("sync", lambda e, t_=t_, vv=vv, b0=b0: e.dma_start(out=t_[:, b0:b0 + 8, 0:64], in_=vv[:, b0:b0 + 8, :]),
                         r=[C.Bdr[nm]], w=[BK], dma=True)
            S.op("gpsimd", lambda e: e.memset(VR[:, :, 0:64], 0.0), w=[BK])
            S.op("gpsimd", lambda e: e.memset(VR[:, :, 64:65], 1.0), w=[BK])
            S.op("gpsimd", lambda e: e.dma_start(out=VR[:, :, 65:193], in_=dr["c_pool"][:, 0:NCH, :]), w=[BK], dma=True)
            S.op("gpsimd", lambda e: e.dma_start(out=G[:], in_=dr["c_G"][:, 0:SEQ]), w=[BK], dma=True)
            S.op("gpsimd", lambda e: e.dma_start(out=CAUS[:], in_=dr["c_caus"][:, :, :]), w=[BK], dma=True)
            S.op("gpsimd", lambda e: e.dma_start(out=ANTI[:], in_=dr["c_anti"][:, :, :]), w=[BK], dma=True)
            S.op("gpsimd", lambda e: e.dma_start(out=CM[:], in_=dr["c_cm"][:, :, :, :]), w=[BK], dma=True)
            S.op("sync", lambda e: e.dma_start(out=PATF[:], in_=dr["c_patf"][:, :]), w=[BK], dma=True)
            S.op("sync", lambda e: e.dma_start(out=PATV[:], in_=dr["c_patv"][:, :]), w=[BK], dma=True)
            ngv = dr["ngate"].rearrange("(b p) c -> p b c", p=128)
            for b0 in range(0, NB, 8):
                S.op("sync", lambda e, b0=b0: e.dma_start(out=ng[:, b0:b0 + 8, :], in_=ngv[:, b0:b0 + 8, :]), r=[C.Bdr["ngate"]], w=[BK], dma=True)
            psc = [P(f"N_psc{i}", [128, 512]) for i in range(2)]
            Bpsc = [Buf() for _ in range(2)]
            pnc = P("N_pnc", [128, 2, 512])
            pns = P("N_pns", [128, 512])
            pnw = P("N_pnw", [128, 512])
            ptr = P("N_ptr", [128, 512])
            pty = P("N_pty", [128, 512])
            Bpnc, Bpns, Bpnw, Bptr, Bpty = (Buf() for _ in range(5))
            with ExitStack() as st0:
                T0, _ = _pool(C, st0)
                B0 = Buf()
                kin = T0("N0_kin", [64, SEQ], BF16)
                vin = T0("N0_vin", [64, SEQ], BF16)
                w1 = [T0(f"N0_w1{i}", [64, 32, 64], BF16) for i in range(2)]
                w2 = [T0(f"N0_w2{i}", [64, 64], BF16) for i in range(2)]
                pT = [T0(f"N0_pT{i}", [64, 32], BF16) for i in range(2)]
                cb = [T0(f"N0_cb{i}", [64, 1]) for i in range(2)]
                hh_ = [T0(f"N0_h{i}", [64, NCP], BF16) for i in range(2)]
                S.op("sync", lambda e: e.dma_start(out=kin[:], in_=dr["kcmpT"][g * 64:(g + 1) * 64, :]), r=[C.Bdr["kcmpT"]], w=[B0], dma=True)
                S.op("sync", lambda e: e.dma_start(out=vin[:], in_=dr["vcmpT"][g * 64:(g + 1) * 64, :]), r=[C.Bdr["vcmpT"]], w=[B0], dma=True)
                for i, kvn in enumerate(("k", "v")):
                    S.op("gpsimd", lambda e, i=i, kvn=kvn: e.dma_start(out=w1[i][:], in_=dr[f"cmp_w1_{kvn}"][l].rearrange("(j d) o -> d j o", d=64)),
                         w=[B0], dma=True)
                    S.op("gpsimd", lambda e, i=i, kvn=kvn: e.dma_start(out=w2[i][:], in_=dr[f"cmp_w2_{kvn}"][l]), w=[B0], dma=True)
                    S.op("gpsimd", lambda e, i=i, kvn=kvn: e.dma_start(out=pT[i][:], in_=dr[f"cmp_posT_{kvn}"][l]), w=[B0], dma=True)
                    S.op("gpsimd", lambda e, i=i: e.memset(hh_[i][:], 0.0), w=[B0])
                for i, src in enumerate((kin, vin)):
                    for j in range(32):
                        S.op("tensor", lambda e, i=i, j=j: e.matmul(ptr[0:64, 0:1], lhsT=w1[i][:, j, :], rhs=pT[i][:, j:j + 1],
                                                                    start=(j == 0), stop=(j == 31)), r=[B0], w=[Bptr])
                    S.op("vector", lambda e, i=i: e.tensor_copy(out=cb[i][:], in_=ptr[0:64, 0:1]), r=[Bptr], w=[B0])
                    for j in range(32):
                        S.op("tensor", lambda e, i=i, j=j, src=src: e.matmul(psc[0][0:64, 0:NC_], lhsT=w1[i][:, j, :],
                                                                             rhs=src[:, j:j + 16 * (NC_ - 1) + 1:16],
                                                                             start=(j == 0), stop=(j == 31)), r=[B0], w=[Bpsc[0]])
                    S.op("scalar", lambda e, i=i: e.activation(out=hh_[i][:, 0:NC_], in_=psc[0][0:64, 0:NC_], func=AF.Silu, bias=cb[i][:, 0:1]),
                         r=[Bpsc[0], B0], w=[B0])
                    if i == 0:
                        S.op("tensor", lambda e: e.matmul(psc[1][0:64, 0:NC_], lhsT=w2[0][:], rhs=hh_[0][:, 0:NC_], start=True, stop=True),
                             r=[B0], w=[Bpsc[1]])
                        S.op("vector", lambda e: e.tensor_copy(out=kca[0:64, 0:NC_], in_=psc[1][0:64, 0:NC_]), r=[Bpsc[1]], w=[BK])
                    else:
                        for c in range(NCH):
                            S.op("tensor", lambda e, c=c: e.matmul(psc[1][:, c * 64:(c + 1) * 64], lhsT=hh_[1][:, c * 128:(c + 1) * 128], rhs=w2[1][:],
                                                                   start=True, stop=True), r=[B0], w=[Bpsc[1]])
                        S.op("vector", lambda e: e.tensor_copy(out=VR[:, :, 0:64], in_=psc[1][:, 0:NCH * 64].rearrange("p (c d) -> p c d", d=64)),
                             r=[Bpsc[1]], w=[BK])
                S.flush()
            pT_ = [T(f"N_pT{i}", [128, 512], BF16) for i in range(3)]
            BpT = [Buf() for _ in range(3)]
            sm = T("N_sm", [128, 8, 4])
            imp = T("N_imp", [128, 128])
            m8 = T("N_m8", [128, 8])
            nmt = [T(f"N_nmt{i}", [128, 128], BF16) for i in range(2)]
            yt = [T(f"N_yt{i}", [128, 256]) for i in range(2)]
            yo = [T(f"N_yo{i}", [128, 2, 128], BF16) for i in range(2)]
            Bsm, Bimp, Bm8 = Buf(), Buf(), Buf()
            Bnmt, Byt, Byo = ([Buf() for _ in range(2)] for _ in range(3))
            cnt = {"u": 0}

            pend = []

            def drain():
                while pend:
                    pend.pop(0)()

            def unit(mms, num, Bnum, vtile_of, first, last, pre_zeroed=False):
                u = cnt["u"]
                cnt["u"] += 1
                a, s3 = u % 2, u % 3
                for i, (lt, rh, rb) in enumerate(mms):
                    S.op("tensor", lambda e, lt=lt, rh=rh, i=i, a=a: e.matmul(psc[a][:, :], lhsT=lt, rhs=rh, start=(i == 0), stop=(i == len(mms) - 1)),
                         r=rb, w=[Bpsc[a]])
                S.op("scalar", lambda e, a=a, s3=s3: e.activation(out=pT_[s3][:], in_=psc[a][:, :], func=AF.Exp), r=[Bpsc[a]], w=[BpT[s3]])
                drain()

                def pv():
                    if first and not pre_zeroed:
                        S.op("vector", lambda e, num=num: e.memset(num, 0.0), w=[Bnum])
                    for hh in range(4):
                        o_ap, r_ap = vtile_of(hh)
                        S.op("tensor", lambda e, hh=hh, s3=s3, o_ap=o_ap, r_ap=r_ap: e.matmul(o_ap, lhsT=pT_[s3][:, hh * 128:(hh + 1) * 128], rhs=r_ap,
                                                                                              start=False, stop=last, skip_group_check=True),
                             r=[BpT[s3], BK], w=[Bnum])
                pend.append(pv)

            def fin(den_ap, gate_ap, Bnum, o_den, o_f):
                S.op("vector", lambda e: e.tensor_scalar(out=o_den, in0=den_ap, scalar1=1e-30, scalar2=None, op0=ALU.max), r=[Bnum], w=[Bsm])
                S.op("vector", lambda e: e.reciprocal(out=o_den, in_=o_den), r=[Bsm], w=[Bsm])
                S.op("vector", lambda e: vtt(e, out=o_f, in0=o_den, in1=gate_ap, op=ALU.mult), r=[Bsm, BK], w=[Bsm])

            S.op("vector", lambda e: e.memset(pnc[:, :, 0:386], 0.0), w=[Bpnc])
            for qb in range(NB):
                qsl = slice(qb * 128, (qb + 1) * 128)
                ys = qb % 2
                q_rhs = qa[:, :, qsl]
                c_hi = min(NCH - 1, qb // 16)
                for c in range(c_hi + 1):
                    mms = [(kca[:, c * 128:(c + 1) * 128], q_rhs, [BK])]
                    rel = c_hi - c
                    if rel <= 1:
                        mms.append((C.identb[:], CM[:, qb % 16, 1 - rel, :].unsqueeze(1).to_broadcast([128, 4, 128]), [C.Bconst, BK]))
                    unit(mms, pnc[:, :, 0:386], Bpnc, lambda hh, c=c: (pnc[:, hh // 2, (hh % 2) * 193:(hh % 2) * 193 + 193], VR[:, c, :]), c == 0, c == c_hi,
                         pre_zeroed=True)
                drain()
                dn = pnc[:, :, 0:386].rearrange("p b (x w) -> p b x w", w=193)
                fin(dn[:, :, :, 64], ng[:, qb, 4 * g:4 * g + 4].rearrange("p (b x) -> p b x", b=2), Bpnc,
                    sm[:, 0, :].rearrange("p (b x) -> p b x", b=2), sm[:, 1, :].rearrange("p (b x) -> p b x", b=2))
                for hh in range(4):
                    S.op("vector", lambda e, hh=hh, ys=ys: e.tensor_scalar(out=yt[ys][:, hh * 64:(hh + 1) * 64],
                                                                          in0=pnc[:, hh // 2, (hh % 2) * 193:(hh % 2) * 193 + 64],
                                                                          scalar1=sm[:, 1, hh:hh + 1], scalar2=None, op0=ALU.mult),
                         r=[Bpnc, Bsm], w=[Byt[ys]])
                for hh in range(4):
                    src = pnc[:, hh // 2, (hh % 2) * 193 + 65:(hh % 2) * 193 + 193]
                    if hh == 0:
                        S.op("vector", lambda e, src=src: e.tensor_scalar(out=imp[:], in0=src, scalar1=sm[:, 0, 0:1], scalar2=None, op0=ALU.mult),
                             r=[Bpnc, Bsm], w=[Bimp])
                    else:
                        S.op("vector", lambda e, src=src, hh=hh: e.scalar_tensor_tensor(out=imp[:], in0=src, scalar=sm[:, 0, hh:hh + 1], in1=imp[:],
                                                                                        op0=ALU.mult, op1=ALU.add), r=[Bpnc, Bsm, Bimp], w=[Bimp])
                off = 127 - 2 * qb
                S.op("vector", lambda e, off=off: vtt(e, out=imp[:], in0=imp[:], in1=PATF[:, off:off + 128], op=ALU.max), r=[Bimp, BK], w=[Bimp])
                S.op("vector", lambda e, off=off: vtt(e, out=imp[:], in0=imp[:], in1=PATV[:, off:off + 128], op=ALU.min), r=[Bimp, BK], w=[Bimp])
                S.op("vector", lambda e: e.memset(imp[:, 0:1], 1e30), r=[Bimp], w=[Bimp])
                S.op("vector", lambda e: e.max(out=m8[:], in_=imp[:]), r=[Bimp], w=[Bm8])
                S.op("vector", lambda e: e.tensor_scalar(out=imp[:], in0=imp[:], scalar1=m8[:, 7:8], scalar2=-NEGB, op0=ALU.is_ge, op1=ALU.mult),
                     r=[Bimp, Bm8], w=[Bimp])
                S.op("vector", lambda e: e.tensor_scalar(out=imp[:], in0=imp[:], scalar1=NEGB, scalar2=None, op0=ALU.add), r=[Bimp], w=[Bimp])
                S.op("tensor", lambda e: e.transpose(out=ptr[:, 0:128], in_=imp[:], identity=C.ident[:]), r=[Bimp, C.Bconst], w=[Bptr])
                S.op("scalar", lambda e, ys=ys: e.copy(out=nmt[ys][:], in_=ptr[:, 0:128]), r=[Bptr], w=[Bnmt[ys]])
                S.op("vector", lambda e: e.memset(pnc[:, :, 0:386], 0.0), w=[Bpnc])
                k0 = max(0, qb - 4)
                for kb in range(k0, qb + 1):
                    mms = [(kwa[:, kb * 128:(kb + 1) * 128], q_rhs, [BK])]
                    if kb == qb:
                        mms.append((C.identb[:], CAUS[:], [C.Bconst, BK]))
                    elif kb == qb - 4:
                        mms.append((C.identb[:], ANTI[:], [C.Bconst, BK]))
                    unit(mms, pnw[:, 0:260], Bpnw, lambda hh, kb=kb: (pnw[:, hh * 65:(hh + 1) * 65], vw[:, kb, :]), kb == k0, kb == qb)
                for kb in range(qb + 1):
                    mms = [(ksa[:, kb * 128:(kb + 1) * 128], q_rhs, [BK]),
                           (G[:, kb * 128:(kb + 1) * 128], nmt[ys][:].unsqueeze(1).to_broadcast([128, 4, 128]), [BK, Bnmt[ys]])]
                    if kb == qb:
                        mms.append((C.identb[:], CAUS[:], [C.Bconst, BK]))
                    unit(mms, pns[:, 0:260], Bpns, lambda hh, kb=kb: (pns[:, hh * 65:(hh + 1) * 65], vs[:, kb, :]), kb == 0, kb == qb)
                drain()
                dsv = pns[:, 0:260].rearrange("p (h w) -> p h w", w=65)
                dwv = pnw[:, 0:260].rearrange("p (h w) -> p h w", w=65)
                fin(dsv[:, :, 64], ng[:, qb, 8 + 4 * g:12 + 4 * g], Bpns, sm[:, 2, :], sm[:, 3, :])
                fin(dwv[:, :, 64], ng[:, qb, 16 + 4 * g:20 + 4 * g], Bpnw, sm[:, 4, :], sm[:, 5, :])
                for hh in range(4):
                    S.op("vector", lambda e, hh=hh, ys=ys: e.scalar_tensor_tensor(out=yt[ys][:, hh * 64:(hh + 1) * 64], in0=pns[:, hh * 65:hh * 65 + 64],
                                                                                 scalar=sm[:, 3, hh:hh + 1], in1=yt[ys][:, hh * 64:(hh + 1) * 64],
                                                                                 op0=ALU.mult, op1=ALU.add), r=[Bpns, Bsm, Byt[ys]], w=[Byt[ys]])
                    S.op("vector", lambda e, hh=hh, ys=ys: e.scalar_tensor_tensor(out=yt[ys][:, hh * 64:(hh + 1) * 64], in0=pnw[:, hh * 65:hh * 65 + 64],
                                                                                 scalar=sm[:, 5, hh:hh + 1], in1=yt[ys][:, hh * 64:(hh + 1) * 64],
                                                                                 op0=ALU.mult, op1=ALU.add), r=[Bpnw, Bsm, Byt[ys]], w=[Byt[ys]])
                for k in range(2):
                    S.op("tensor", lambda e, k=k, ys=ys: e.transpose(out=pty[:, k * 128:(k + 1) * 128], in_=yt[ys][:, k * 128:(k + 1) * 128],
                                                                     identity=C.ident[:]), r=[Byt[ys], C.Bconst], w=[Bpty])
                S.op("scalar", lambda e, ys=ys: e.copy(out=yo[ys][:], in_=pty[:, 0:256].rearrange("p (k t) -> p k t", k=2)), r=[Bpty], w=[Byo[ys]])
                S.op("sync", lambda e, ys=ys, qsl=qsl: e.dma_start(out=dr["ynsaT"][g * 256:(g + 1) * 256, qsl].rearrange("(k p) t -> p k t", p=128),
                                                                   in_=yo[ys][:]), r=[Byo[ys]], w=[C.Bdr["ynsaT"]], dma=True)
                if qb % 32 == 31 and qb + 1 < NB:
                    S.flush()
            S.flush()


ALL_PHASES = ("A", "B", "SB", "NSA", "Ea", "Eb")
_CACHE = {}


def kernel(**inputs):
    x = np.asarray(inputs["x"], dtype=np.float32)
    bsz, SEQ, _ = x.shape
    key = ("nc", SEQ)
    if key not in _CACHE:
        _CACHE[key] = (build(SEQ, set(ALL_PHASES)), nsa_consts(SEQ))
    nc, consts = _CACHE[key]
    w = host_inputs(inputs)
    w.update(consts)
    in_maps = [{"x": np.ascontiguousarray(x[b]), **w} for b in range(bsz)]
    res = run_bass_kernel_spmd(nc, in_maps, core_ids=list(range(bsz)))
    return np.stack([np.asarray(r["out"], dtype=np.float32) for r in res.results], axis=0)
```

```python
import numpy as np
from contextlib import ExitStack
import concourse.bass as bass
import concourse.mybir as mybir
from concourse.bass_utils import run_bass_kernel_spmd
import ml_dtypes

F32 = mybir.dt.float32
BF16 = mybir.dt.bfloat16
AF = mybir.ActivationFunctionType
ALU = mybir.AluOpType
AX = mybir.AxisListType

ENGS = ("sync", "gpsimd", "scalar", "vector", "tensor")
GEN = 30000
NS = 8
DGEN = (GEN // 16) * NS

D = 1024
DIN = 9000
L_ = 2
DFF = 2816
EPS = 1e-6
C_Z, C_XBC, C_DT, C_SBQ, C_SBK, C_SBV, C_NQ, C_NKV, C_NG, C_MG = 0, 1024, 3072, 3088, 3600, 4112, 4624, 5136, 5904, 5928


class Buf:
    __slots__ = ("name", "lw", "rd")

    def __init__(self, name="b"):
        self.name = name
        self.lw = None
        self.rd = []


class Sched:
    def __init__(self, nc, stack, ngen=9, ndgen=3):
        self.nc = nc
        self.ops = {e: [] for e in ENGS}
        self.fl = {e: 0 for e in ENGS}
        self.nsig = {e: 0 for e in ENGS}
        self.ndma = {e: 0 for e in ENGS}
        self.seen = {e: {} for e in ENGS}
        self.key = {}
        self.esem = {e: [stack.enter_context(nc.semaphore(f"s_{e}_{g}")) for g in range(ngen)]
                     for e in ("gpsimd", "scalar", "vector", "tensor")}
        self.dsem = {e: [[stack.enter_context(nc.semaphore(f"d_{e}_{g}_{i}")) for i in range(NS)]
                         for g in range(ndgen)] for e in ("sync", "gpsimd")}
        self.cleared = False

    def op(self, eng, fn, r=(), w=(), dma=False):
        ops = self.ops[eng]
        idx = len(ops)
        me = (eng, idx)
        raw = set()
        oth = set()
        for b in r:
            if b.lw is not None:
                raw.add(b.lw)
        for b in w:
            if b.lw is not None:
                oth.add(b.lw)
            for x in b.rd:
                oth.add(x)
        oth -= raw
        deps = []
        for is_raw, group in ((True, raw), (False, oth)):
            for (e2, i2) in group:
                o2 = self.ops[e2][i2]
                if o2["dma"]:
                    deps.append((e2, i2))
                    continue
                if i2 < self.fl[e2]:
                    continue
                if e2 == eng and not dma:
                    if eng == "tensor" or not is_raw:
                        continue
                deps.append((e2, i2))
                o2["sig"] = True
        rec = {"fn": fn, "dma": dma, "deps": deps, "sig": False, "dj": None}
        if dma:
            rec["dj"] = self.ndma[eng]
            self.ndma[eng] += 1
        ops.append(rec)
        for b in r:
            b.rd.append(me)
        for b in w:
            b.lw = me
            b.rd = []
        return me

    def _dma_key(self, eng, j):
        g = j // DGEN
        jj = j % DGEN
        return (self.dsem[eng][g][jj % NS], 16 * (jj // NS + 1))

    def flush(self):
        nc = self.nc
        if not self.cleared:
            self.cleared = True
            with nc.Block() as block:
                for e in ENGS:
                    def body(engine, e=e):
                        if e in self.esem:
                            for s in self.esem[e]:
                                engine.sem_clear(s)
                        if e in self.dsem:
                            for g in self.dsem[e]:
                                for s in g:
                                    engine.sem_clear(s)
                    getattr(block, e)(body)
        for e in ENGS:
            for i in range(self.fl[e], len(self.ops[e])):
                o = self.ops[e][i]
                if o["dma"]:
                    self.key[(e, i)] = self._dma_key(e, o["dj"])
                elif o["sig"]:
                    c = self.nsig[e]
                    self.nsig[e] += 1
                    self.key[(e, i)] = (self.esem[e][c // GEN], c % GEN + 1)
        with nc.Block() as block:
            for e in ENGS:
                lo, hi = self.fl[e], len(self.ops[e])

                def body(engine, e=e, lo=lo, hi=hi):
                    seen = self.seen[e]

                    def do_waits(waits):
                        for sem, cnt in waits:
                            k = id(sem)
                            if seen.get(k, 0) >= cnt:
                                continue
                            seen[k] = cnt
                            engine.wait_ge(sem, cnt)
                    for i in range(lo, hi):
                        o = self.ops[e][i]
                        waits = [self.key[d] for d in o["deps"]]
                        if o["dma"] and o["dj"] % DGEN >= NS:
                            waits.append(self._dma_key(e, o["dj"] - NS))
                        do_waits(waits)
                        ins = o["fn"](engine)
                        if o["dma"]:
                            ins.then_inc(self.key[(e, i)][0], 16)
                        elif o["sig"]:
                            ins.then_inc(self.key[(e, i)][0], 1)
                        o["fn"] = None
                    if e in self.dsem:
                        n = self.ndma[e]
                        do_waits([self._dma_key(e, j) for j in range(max(0, n - NS), n)])
                if lo == hi and e not in self.dsem:
                    continue
                getattr(block, e)(body)
        for e in ENGS:
            self.fl[e] = len(self.ops[e])


class Ctx:
    pass


def vtt(e, out, in0, in1, op):
    return e.scalar_tensor_tensor(out=out, in0=in0, scalar=1.0, in1=in1, op0=ALU.mult, op1=op)


_UID = [0]


def _pool(C, st):
    nc = C.nc

    def T(name, shape, dt=F32):
        _UID[0] += 1
        return st.enter_context(nc.sbuf_tensor(f"{name}_{_UID[0]}", shape, dt))

    def P(name, shape, dt=F32):
        _UID[0] += 1
        return st.enter_context(nc.psum_tensor(f"{name}_{_UID[0]}", shape, dt))
    return T, P


def rstd_ops(C, ssq, rs, n, Bssq, Brs):
    S = C.S
    S.op("scalar", lambda e: e.activation(out=rs, in_=ssq, func=AF.Ln, scale=1.0 / n, bias=C.eps_t[:, 0:1]),
         r=[Bssq, C.Bconst], w=[Brs])
    S.op("scalar", lambda e: e.activation(out=rs, in_=rs, func=AF.Exp, scale=-0.5), r=[Brs], w=[Brs])


class NormT:
    def __init__(self, C, st, gain_row, tag, nxn=2, junk=None, Bjunk=None, nxt=2):
        self.C = C
        T, P = _pool(C, st)
        self.xt = [T(f"{tag}_xt{i}", [128, D]) for i in range(nxt)] * (2 // nxt)
        self.xn = [T(f"{tag}_xn{i}", [128, D]) for i in range(nxn)] * (2 // nxn)
        self.junk = junk if junk is not None else T(f"{tag}_junk", [128, D], BF16)
        self.gbc = T(f"{tag}_g", [128, D])
        self.ssq = [T(f"{tag}_ssq{i}", [128, 1]) for i in range(2)]
        self.rs = [T(f"{tag}_rs{i}", [128, 1]) for i in range(2)]
        self.psT = [P(f"{tag}_ps{i}", [128, D]) for i in range(nxn)] * (2 // nxn)
        self.Bxt, self.Bxn, self.Bssq, self.Brs, self.Bps = ([Buf() for _ in range(2)] for _ in range(5))
        if nxn == 1:
            self.Bxn = [self.Bxn[0]] * 2
            self.Bps = [self.Bps[0]] * 2
        if nxt == 1:
            self.Bxt = [self.Bxt[0]] * 2
        self.Bjunk, self.Bg = (Bjunk if Bjunk is not None else Buf()), Buf()
        self.n = 0
        gbc = self.gbc
        C.S.op("sync", lambda e: e.dma_start(out=gbc[:], in_=gain_row.partition_broadcast(128)), w=[self.Bg], dma=True)

    def emit(self, src_rows, out_ap, Bout, src_buf=None):
        C, S = self.C, self.C.S
        s = self.n % 2
        i = self.n
        self.n += 1
        xt, xn, junk, gbc, ssq, rs, psT = self.xt[s], self.xn[s], self.junk, self.gbc, self.ssq[s], self.rs[s], self.psT[s]
        Bxt, Bxn, Bssq, Brs, Bps = self.Bxt[s], self.Bxn[s], self.Bssq[s], self.Brs[s], self.Bps[s]
        S.op("sync", lambda e: e.dma_start(out=xt[:], in_=src_rows), r=([src_buf] if src_buf else []), w=[Bxt], dma=True)
        S.op("scalar", lambda e: e.activation(out=junk[:], in_=xt[:], func=AF.Square, accum_out=ssq[:]),
             r=[Bxt], w=[self.Bjunk, Bssq])
        rstd_ops(C, ssq[:], rs[:], D, Bssq, Brs)
        S.op("vector", lambda e: e.scalar_tensor_tensor(out=xn[:], in0=xt[:], scalar=rs[:, 0:1], in1=gbc[:],
                                                        op0=ALU.mult, op1=ALU.mult), r=[Bxt, Brs, self.Bg], w=[Bxn])
        for k in range(8):
            S.op("tensor", lambda e, k=k: e.transpose(out=psT[:, k * 128:(k + 1) * 128], in_=xn[:, k * 128:(k + 1) * 128],
                                                      identity=C.ident[:]), r=[Bxn, C.Bconst], w=[Bps])
        src3 = psT[:].rearrange("p (k t) -> p k t", k=8)
        if i % 2 == 0:
            S.op("vector", lambda e: e.tensor_copy(out=out_ap, in_=src3), r=[Bps], w=[Bout])
        else:
            S.op("scalar", lambda e: e.copy(out=out_ap, in_=src3), r=[Bps], w=[Bout])


def in_proj_blocks():
    blks = []
    for j in range(2):
        blks.append((C_Z + 512 * j, 512, [(0, 512, "T", "z_tm", 512 * j, "silu")]))
    for j in range(4):
        blks.append((C_XBC + 512 * j, 512, [(128 * q, 128, "F", "xbcT", 512 * j + 128 * q, None) for q in range(4)]))
    blks.append((C_DT, 16, [(0, 16, "T", "dt_tm", 0, None)]))
    blks.append((C_SBQ, 512, [(128 * q, 128, "F", "sbqT", 128 * q, None) for q in range(4)]))
    blks.append((C_SBK, 512, [(128 * q, 128, "F", "sbkT", 128 * q, "kscale") for q in range(4)]))
    blks.append((C_SBV, 512, [(0, 512, "T", "sbv", 0, None)]))
    blks.append((C_NQ, 512, [(128 * q, 128, "F", "nqT", 128 * q, "qscale") for q in range(4)]))
    blks.append((C_NKV, 512, [(0, 128, "F", "kcmpT", 0, None), (128, 128, "F", "vcmpT", 0, None),
                              (256, 128, "F", "kselT", 0, None), (384, 128, "T", "vsel", 0, None)]))
    blks.append((C_NKV + 512, 256, [(0, 128, "F", "kwinT", 0, None), (128, 128, "T", "vwin", 0, None)]))
    blks.append((C_NG, 24, [(0, 24, "T", "ngate", 0, "sigmoid")]))
    for j in range(6):
        blks.append((C_MG + 512 * j, 512, [(128 * q, 128, "F", "mgT", 512 * j + 128 * q, "sigmoid") for q in range(4)]))
    return blks


def phase_A(C, l, x_src, Bsrc):
    nc, S, SEQ = C.nc, C.S, C.SEQ
    NT = SEQ // 128
    with ExitStack() as st:
        T, P = _pool(C, st)
        hT = T("A_hT", [128, 8, SEQ], BF16)
        BhT = [Buf() for _ in range(SEQ // 512)]
        with ExitStack() as st1:
            nrm = NormT(C, st1, C.dr["pre_mix_norm"][l:l + 1, :], "A1")
            for i in range(NT):
                nrm.emit(x_src[i * 128:(i + 1) * 128, :], hT[:, :, i * 128:(i + 1) * 128], BhT[i // 4], Bsrc)
            S.flush()
        with ExitStack() as st2:
            T2, P2 = _pool(C, st2)
            wt = [T2(f"A_wt{i}", [128, 8, 512], BF16) for i in range(2)]
            Bwt = [Buf() for _ in range(2)]
            stF = [T2(f"A_stF{i}", [128, 2048], BF16) for i in range(2)]
            stT = [T2(f"A_stT{i}", [128, 4, 512], F32) for i in range(2)]
            stTb = [T2(f"A_stTb{i}", [128, 4, 512], BF16) for i in range(2)]
            BstF, BstT, BstTb = ([Buf() for _ in range(2)] for _ in range(3))
            ps = [P2(f"A_ps{i}", [128, 512]) for i in range(4)]
            Bps = [Buf() for _ in range(4)]
            w_l = C.dr["w_in"][l].rearrange("(kc p) f -> p kc f", p=128)
            cnt = {"ps": 0, "F": 0, "T": 0, "Tb": 0, "ev": 0}
            for bi, (c0, wd, subs) in enumerate(in_proj_blocks()):
                ws = bi % 2
                S.op("gpsimd", lambda e, ws=ws, c0=c0, wd=wd: e.dma_start(out=wt[ws][:, :, 0:wd], in_=w_l[:, :, c0:c0 + wd]),
                     w=[Bwt[ws]], dma=True)
                for (off, w, kind, dname, doff, act) in subs:
                    dst = C.dr[dname]
                    if kind == "F":
                        for tt in range(SEQ // 512):
                            pi = cnt["ps"] % 4
                            cnt["ps"] += 1
                            for k in range(8):
                                S.op("tensor", lambda e, pi=pi, ws=ws, k=k, off=off, tt=tt: e.matmul(
                                    ps[pi][:, :], lhsT=wt[ws][:, k, off:off + 128], rhs=hT[:, k, tt * 512:(tt + 1) * 512],
                                    start=(k == 0), stop=(k == 7)), r=[Bwt[ws], BhT[tt]], w=[Bps[pi]])
                            nq = min(4, SEQ // 512)
                            si = (cnt["F"] // nq) % 2
                            q = cnt["F"] % nq
                            cnt["F"] += 1
                            o_ap = stF[si][:, q * 512:(q + 1) * 512]
                            if act == "sigmoid":
                                S.op("scalar", lambda e, o_ap=o_ap, pi=pi: e.activation(out=o_ap, in_=ps[pi][:, :], func=AF.Sigmoid),
                                     r=[Bps[pi]], w=[BstF[si]])
                            elif act in ("kscale", "qscale"):
                                mulv = float(128 ** -0.5) if act == "kscale" else 0.125
                                S.op("scalar", lambda e, o_ap=o_ap, pi=pi, mulv=mulv: e.mul(out=o_ap, in_=ps[pi][:, :], mul=mulv),
                                     r=[Bps[pi]], w=[BstF[si]])
                            else:
                                cnt["ev"] += 1
                                if cnt["ev"] % 2:
                                    S.op("vector", lambda e, o_ap=o_ap, pi=pi: e.tensor_copy(out=o_ap, in_=ps[pi][:, :]),
                                         r=[Bps[pi]], w=[BstF[si]])
                                else:
                                    S.op("scalar", lambda e, o_ap=o_ap, pi=pi: e.copy(out=o_ap, in_=ps[pi][:, :]),
                                         r=[Bps[pi]], w=[BstF[si]])
                            if q == nq - 1:
                                t0 = (tt + 1 - nq) * 512
                                S.op("sync", lambda e, si=si, dst=dst, doff=doff, t0=t0, nq=nq: e.dma_start(
                                    out=dst[doff:doff + 128, t0:t0 + nq * 512], in_=stF[si][:, 0:nq * 512]),
                                    r=[BstF[si]], w=[C.Bdr[dname]], dma=True)
                    else:
                        isb = dst.dtype == BF16
                        stg, Bstg, ck = (stTb, BstTb, "Tb") if isb else (stT, BstT, "T")
                        for i in range(NT):
                            pi = cnt["ps"] % 4
                            cnt["ps"] += 1
                            for k in range(8):
                                S.op("tensor", lambda e, pi=pi, ws=ws, k=k, off=off, w=w, i=i: e.matmul(
                                    ps[pi][:, 0:w], lhsT=hT[:, k, i * 128:(i + 1) * 128], rhs=wt[ws][:, k, off:off + w],
                                    start=(k == 0), stop=(k == 7)), r=[Bwt[ws], BhT[i // 4]], w=[Bps[pi]])
                            si = (cnt[ck] // 4) % 2
                            q = cnt[ck] % 4
                            cnt[ck] += 1
                            o_ap = stg[si][:, q, 0:w]
                            if act == "silu":
                                S.op("scalar", lambda e, o_ap=o_ap, pi=pi, w=w: e.activation(out=o_ap, in_=ps[pi][:, 0:w], func=AF.Silu),
                                     r=[Bps[pi]], w=[Bstg[si]])
                            elif act == "sigmoid":
                                S.op("scalar", lambda e, o_ap=o_ap, pi=pi, w=w: e.activation(out=o_ap, in_=ps[pi][:, 0:w], func=AF.Sigmoid),
                                     r=[Bps[pi]], w=[Bstg[si]])
                            else:
                                S.op("vector", lambda e, o_ap=o_ap, pi=pi, w=w: e.tensor_copy(out=o_ap, in_=ps[pi][:, 0:w]),
                                     r=[Bps[pi]], w=[Bstg[si]])
                            if q == 3:
                                t0 = (i - 3) * 128
                                S.op("sync", lambda e, stg=stg, si=si, dst=dst, doff=doff, t0=t0, w=w: e.dma_start(
                                    out=dst[t0:t0 + 512, doff:doff + w].rearrange("(t p) w -> p t w", p=128), in_=stg[si][:, :, 0:w]),
                                    r=[Bstg[si]], w=[C.Bdr[dname]], dma=True)
            S.flush()


def phase_Ea(C, l, x_src, Bsrc):
    nc, S, SEQ = C.nc, C.S, C.SEQ
    dr = C.dr
    with ExitStack() as st:
        T, P = _pool(C, st)
        Wssm = T("Ea_wssm", [128, 8, D], BF16)
        Wsb = T("Ea_wsb", [128, 4, D], BF16)
        Wnsa = T("Ea_wnsa", [128, 4, D], BF16)
        Wout = T("Ea_wout", [128, 8, D], BF16)
        gbc = T("Ea_g", [128, D])
        BW = Buf()
        for (t_, nm, kc) in ((Wssm, "w_br_ssm", 8), (Wsb, "w_br_sb", 4), (Wnsa, "w_br_nsa", 4), (Wout, "w_out", 8)):
            S.op("gpsimd", lambda e, t_=t_, nm=nm: e.dma_start(out=t_[:], in_=dr[nm][l].rearrange("(kc p) f -> p kc f", p=128)),
                 w=[BW], dma=True)
        S.op("sync", lambda e: e.dma_start(out=gbc[:], in_=dr["post_mix_norm"][l:l + 1, :].partition_broadcast(128)), w=[BW], dma=True)
        ys = [T(f"Ea_ys{i}", [128, 8, 512], BF16) for i in range(2)]
        yb = [T(f"Ea_yb{i}", [128, 4, 512], BF16) for i in range(2)]
        yn = [T(f"Ea_yn{i}", [128, 4, 512], BF16) for i in range(2)]
        By = [Buf() for _ in range(2)]
        mg = [T(f"Ea_mg{i}", [128, 3, 512], BF16) for i in range(2)]
        Bmg = [Buf() for _ in range(2)]
        t0_, t1_, t2_ = (T(f"Ea_t{i}", [128, 512]) for i in range(3))
        Bt = [Buf() for _ in range(3)]
        mixT2 = [T(f"Ea_mixT{i}", [128, 8, 512], BF16) for i in range(2)]
        Bmix2 = [Buf() for _ in range(2)]
        psb = [P(f"Ea_ps{i}", [128, 512]) for i in range(6)]
        Bpsb = [Buf() for _ in range(6)]
        pso = P("Ea_pso", [128, D])
        Bpso = Buf()
        xt = [T(f"Ea_xt{i}", [128, D]) for i in range(2)]
        Bxt = [Buf() for _ in range(2)]
        tn = T("Ea_tn", [128, D])
        Btn = Buf()
        xo = [T(f"Ea_xo{i}", [128, D]) for i in range(2)]
        Bxo = [Buf() for _ in range(2)]
        junk = T("Ea_junk", [128, D], BF16)
        ssq = T("Ea_ssq", [128, 1])
        rs = T("Ea_rs", [128, 1])
        Bjunk, Bssq, Brs = Buf(), Buf(), Buf()
        mgT = dr["mgT"].rearrange("(j o p) t -> p j o t", j=3, p=128)
        def out_pieces(tt):
            mixT, Bmix = mixT2[tt % 2], Bmix2[tt % 2]
            pcs = []
            for sub in range(4):
                i = tt * 4 + sub
                xs = i % 2

                def ld(xs=xs, i=i):
                    S.op("sync", lambda e: e.dma_start(out=xt[xs][:], in_=x_src[i * 128:(i + 1) * 128, :]), r=[Bsrc], w=[Bxt[xs]], dma=True)
                pcs.append(ld)
                for half in range(2):
                    for o in range(8):
                        def mm(half=half, o=o, sub=sub):
                            S.op("tensor", lambda e: e.matmul(pso[:, half * 512:(half + 1) * 512], lhsT=mixT[:, o, sub * 128:(sub + 1) * 128],
                                                              rhs=Wout[:, o, half * 512:(half + 1) * 512], start=(o == 0), stop=(o == 7)),
                                 r=[Bmix, BW], w=[Bpso])
                        pcs.append(mm)

                def post(xs=xs, i=i):
                    post_norm_residual(C, pso, Bpso, gbc, BW, xt[xs], Bxt[xs], xo[xs], Bxo[xs], tn, Btn, junk, Bjunk, ssq, Bssq, rs, Brs)
                    S.op("sync", lambda e: e.dma_start(out=dr["x1"][i * 128:(i + 1) * 128, :], in_=xo[xs][:]),
                         r=[Bxo[xs]], w=[C.Bdr["x1"]], dma=True)
                pcs.append(post)
            return pcs

        pend = []
        NT5 = SEQ // 512
        for tt in range(NT5):
            s = tt % 2
            mixT, Bmix = mixT2[tt % 2], Bmix2[tt % 2]
            tok = slice(tt * 512, (tt + 1) * 512)
            S.op("sync", lambda e, s=s, tok=tok: e.dma_start(out=ys[s][:], in_=dr["yssmT"][:, tok].rearrange("(c p) t -> p c t", p=128)),
                 r=[C.Bdr["yssmT"]], w=[By[s]], dma=True)
            S.op("sync", lambda e, s=s, tok=tok: e.dma_start(out=yb[s][:], in_=dr["ysbT"][:, tok].rearrange("(c p) t -> p c t", p=128)),
                 r=[C.Bdr["ysbT"]], w=[By[s]], dma=True)
            S.op("sync", lambda e, s=s, tok=tok: e.dma_start(out=yn[s][:], in_=dr["ynsaT"][:, tok].rearrange("(c p) t -> p c t", p=128)),
                 r=[C.Bdr["ynsaT"]], w=[By[s]], dma=True)
            npp = (len(pend) + 7) // 8
            for o in range(8):
                ms = o % 2
                S.op("sync", lambda e, ms=ms, o=o, tok=tok: e.dma_start(out=mg[ms][:], in_=mgT[:, :, o, tok]),
                     r=[C.Bdr["mgT"]], w=[Bmg[ms]], dma=True)
                pb = 3 * (o % 2)
                osl = slice(o * 128, (o + 1) * 128)
                for c in range(8):
                    S.op("tensor", lambda e, pb=pb, c=c, osl=osl, s=s: e.matmul(psb[pb][:, :], lhsT=Wssm[:, c, osl], rhs=ys[s][:, c, :],
                                                                               start=(c == 0), stop=(c == 7)), r=[BW, By[s]], w=[Bpsb[pb]])
                for c in range(4):
                    S.op("tensor", lambda e, pb=pb, c=c, osl=osl, s=s: e.matmul(psb[pb + 1][:, :], lhsT=Wsb[:, c, osl], rhs=yb[s][:, c, :],
                                                                               start=(c == 0), stop=(c == 3)), r=[BW, By[s]], w=[Bpsb[pb + 1]])
                for c in range(4):
                    S.op("tensor", lambda e, pb=pb, c=c, osl=osl, s=s: e.matmul(psb[pb + 2][:, :], lhsT=Wnsa[:, c, osl], rhs=yn[s][:, c, :],
                                                                               start=(c == 0), stop=(c == 3)), r=[BW, By[s]], w=[Bpsb[pb + 2]])
                S.op("vector", lambda e, pb=pb, ms=ms: vtt(e, out=t0_[:], in0=psb[pb][:, :], in1=mg[ms][:, 0, :], op=ALU.mult),
                     r=[Bpsb[pb], Bmg[ms]], w=[Bt[0]])
                S.op("vector", lambda e, pb=pb, ms=ms: vtt(e, out=t1_[:], in0=psb[pb + 1][:, :], in1=mg[ms][:, 1, :], op=ALU.mult),
                     r=[Bpsb[pb + 1], Bmg[ms]], w=[Bt[1]])
                S.op("vector", lambda e, pb=pb, ms=ms: vtt(e, out=t2_[:], in0=psb[pb + 2][:, :], in1=mg[ms][:, 2, :], op=ALU.mult),
                     r=[Bpsb[pb + 2], Bmg[ms]], w=[Bt[2]])
                S.op("gpsimd", lambda e: e.tensor_tensor(out=t0_[:], in0=t0_[:], in1=t1_[:], op=ALU.add), r=[Bt[0], Bt[1]], w=[Bt[0]])
                S.op("gpsimd", lambda e, o=o, mixT=mixT: e.tensor_tensor(out=mixT[:, o, :], in0=t0_[:], in1=t2_[:], op=ALU.add),
                     r=[Bt[0], Bt[2]], w=[Bmix])
                for _ in range(npp):
                    if pend:
                        pend.pop(0)()
            while pend:
                pend.pop(0)()
            pend = out_pieces(tt)
        while pend:
            pend.pop(0)()
        S.flush()


def post_norm_residual(C, pso, Bpso, gbc, Bg, xt, Bxt, xo, Bxo, tn, Btn, junk, Bjunk, ssq, Bssq, rs, Brs):
    S = C.S
    S.op("scalar", lambda e: e.activation(out=junk[:], in_=pso[:, :], func=AF.Square, accum_out=ssq[:]),
         r=[Bpso], w=[Bjunk, Bssq])
    rstd_ops(C, ssq[:], rs[:], D, Bssq, Brs)
    for half in range(2):
        hs = slice(half * 512, (half + 1) * 512)
        S.op("vector", lambda e, hs=hs: e.scalar_tensor_tensor(out=tn[:, hs], in0=pso[:, hs], scalar=rs[:, 0:1], in1=gbc[:, hs],
                                                               op0=ALU.mult, op1=ALU.mult), r=[Bpso, Brs, Bg], w=[Btn])
    S.op("gpsimd", lambda e: e.tensor_tensor(out=xo[:], in0=tn[:], in1=xt[:], op=ALU.add), r=[Btn, Bxt], w=[Bxo])


def phase_Eb(C, l, dst, Bdst):
    nc, S, SEQ = C.nc, C.S, C.SEQ
    dr = C.dr
    import os
    TT = 256
    NTT = min(SEQ // TT, int(os.environ.get("EB_MAXTT", "100000")))
    nsub = TT // 128
    with ExitStack() as st:
        T, P = _pool(C, st)
        Wup = T("Eb_wup", [128, 8, 2 * DFF], BF16)
        Wdn = T("Eb_wdn", [128, 22, D], BF16)
        cw = T("Eb_cw", [128, 44, 3])
        cb = T("Eb_cb", [128, 44])
        gbc = T("Eb_g", [128, D])
        halo = T("Eb_halo", [128, 44, 2])
        BW = Buf()
        Bhalo = [Buf() for _ in range(44)]
        upv = dr["ffn_w_up"][l].rearrange("(kc p) f -> p kc f", p=128)
        for j in range(11):
            S.op("gpsimd", lambda e, j=j: e.dma_start(out=Wup[:, :, j * 512:(j + 1) * 512], in_=upv[:, :, j * 512:(j + 1) * 512]),
                 w=[BW], dma=True)
        S.op("gpsimd", lambda e: e.dma_start(out=Wdn[:], in_=dr["ffn_w_down"][l].rearrange("(kc p) f -> p kc f", p=128)), w=[BW], dma=True)
        S.op("sync", lambda e: e.dma_start(out=cw[:], in_=dr["ffn_conv_wT"][l]), w=[BW], dma=True)
        S.op("sync", lambda e: e.dma_start(out=cb[:], in_=dr["ffn_conv_bT"][l]), w=[BW], dma=True)
        S.op("sync", lambda e: e.dma_start(out=gbc[:], in_=dr["post_ffn_norm"][l:l + 1, :].partition_broadcast(128)), w=[BW], dma=True)
        S.op("gpsimd", lambda e: e.memset(halo[:], 0.0), w=Bhalo)
        h2T = T("Eb_h2T", [128, 8, 2 * TT], BF16)
        Bh2 = [Buf() for _ in range(2)]
        uext = [[T(f"Eb_ue{g}{i}", [128, TT + 2]) for i in range(2)] for g in range(2)]
        Bue = [[Buf() for _ in range(2)] for _ in range(2)]
        acc = [[T(f"Eb_acc{g}{i}", [128, TT]) for i in range(2)] for g in range(2)]
        Bacc = [[Buf() for _ in range(2)] for _ in range(2)]
        gg = T("Eb_gg", [128, TT])
        Bgg = Buf()
        actT2 = [T(f"Eb_actT{i}", [128, 22, TT], BF16) for i in range(2)]
        Bact2 = [Buf() for _ in range(2)]
        psu = [[P(f"Eb_psu{g}{i}", [128, 512]) for i in range(2)] for g in range(2)]
        Bpsu = [[Buf() for _ in range(2)] for _ in range(2)]
        psd = P("Eb_psd", [128, D])
        Bpsd = Buf()
        xt = [T("Eb_xt0", [128, D])] * 2
        Bxt = [Buf()] * 2
        tn, Btn = xt[0], None
        xo = [T("Eb_xo0", [128, D])] * 2
        Bxo = [Buf()] * 2
        junk = T("Eb_junk", [128, D], BF16)
        ssq = T("Eb_ssq", [128, 1])
        rs = T("Eb_rs", [128, 1])
        Bjunk, Bssq, Brs = Buf(), Buf(), Buf()
        nrm = NormT(C, st, dr["pre_ffn_norm"][l:l + 1, :], "Ebn", nxn=1, junk=junk, Bjunk=Bjunk, nxt=1)

        def emit_norm(tt):
            hs = tt % 2
            for sub in range(nsub):
                i = tt * nsub + sub
                nrm.emit(dr["x1"][i * 128:(i + 1) * 128, :], h2T[:, :, hs * TT + sub * 128: hs * TT + (sub + 1) * 128], Bh2[hs], C.Bdr["x1"])
        def down_pieces(tt):
            actT, Bact = actT2[tt % 2], Bact2[tt % 2]
            pcs = []
            for sub in range(nsub):
                i = tt * nsub + sub
                xs = i % 2

                def ld(xs=xs, i=i):
                    S.op("sync", lambda e: e.dma_start(out=xt[xs][:], in_=dr["x1"][i * 128:(i + 1) * 128, :]),
                         r=[C.Bdr["x1"]], w=[Bxt[xs]], dma=True)
                pcs.append(ld)
                for half in range(2):
                    for j in range(22):
                        def mm(half=half, j=j, sub=sub):
                            S.op("tensor", lambda e: e.matmul(psd[:, half * 512:(half + 1) * 512], lhsT=actT[:, j, sub * 128:(sub + 1) * 128],
                                                              rhs=Wdn[:, j, half * 512:(half + 1) * 512], start=(j == 0), stop=(j == 21)),
                                 r=[Bact, BW], w=[Bpsd])
                        pcs.append(mm)

                def post(xs=xs, i=i):
                    post_norm_residual(C, psd, Bpsd, gbc, BW, xt[xs], Bxt[xs], xo[xs], Bxo[xs], xo[xs], Bxo[xs], junk, Bjunk, ssq, Bssq, rs, Brs)
                    S.op("sync", lambda e: e.dma_start(out=dst[i * 128:(i + 1) * 128, :], in_=xo[xs][:]),
                         r=[Bxo[xs]], w=[Bdst], dma=True)
                pcs.append(post)
            return pcs

        emit_norm(0)
        pend = []
        for tt in range(NTT):
            hs = tt % 2
            actT, Bact = actT2[tt % 2], Bact2[tt % 2]
            if tt + 1 < NTT:
                emit_norm(tt + 1)
            hsl = slice(hs * TT, (hs + 1) * TT)
            npp = (len(pend) + 21) // 22
            for j in range(22):
                sl = j % 2
                for g in range(2):
                    ch = g * 22 + j
                    csl = slice(ch * 128, (ch + 1) * 128)
                    for k in range(8):
                        S.op("tensor", lambda e, g=g, sl=sl, k=k, csl=csl, hsl=hsl: e.matmul(
                            psu[g][sl][:, 0:TT], lhsT=Wup[:, k, csl], rhs=h2T[:, k, hsl], start=(k == 0), stop=(k == 7)),
                            r=[BW, Bh2[hs]], w=[Bpsu[g][sl]])
                    ue, Bu = uext[g][sl], Bue[g][sl]
                    ac, Ba = acc[g][sl], Bacc[g][sl]
                    S.op("scalar", lambda e, ue=ue, ch=ch: e.copy(out=ue[:, 0:2], in_=halo[:, ch, :]), r=[Bhalo[ch]], w=[Bu])
                    S.op("scalar", lambda e, ue=ue, g=g, sl=sl: e.copy(out=ue[:, 2:TT + 2], in_=psu[g][sl][:, 0:TT]),
                         r=[Bpsu[g][sl]], w=[Bu])
                    S.op("scalar", lambda e, ue=ue, ch=ch: e.copy(out=halo[:, ch, :], in_=ue[:, TT:TT + 2]), r=[Bu], w=[Bhalo[ch]])
                    S.op("vector", lambda e, ue=ue, ac=ac, ch=ch: e.tensor_scalar(out=ac[:], in0=ue[:, 0:TT], scalar1=cw[:, ch, 0:1],
                                                                                  scalar2=cb[:, ch:ch + 1], op0=ALU.mult, op1=ALU.add),
                         r=[Bu, BW], w=[Ba])
                    S.op("vector", lambda e, ue=ue, ac=ac, ch=ch: e.scalar_tensor_tensor(out=ac[:], in0=ue[:, 1:TT + 1], scalar=cw[:, ch, 1:2],
                                                                                         in1=ac[:], op0=ALU.mult, op1=ALU.add),
                         r=[Bu, BW, Ba], w=[Ba])
                    S.op("vector", lambda e, ue=ue, ac=ac, ch=ch: e.scalar_tensor_tensor(out=ac[:], in0=ue[:, 2:TT + 2], scalar=cw[:, ch, 2:3],
                                                                                         in1=ac[:], op0=ALU.mult, op1=ALU.add),
                         r=[Bu, BW, Ba], w=[Ba])
                S.op("scalar", lambda e, sl=sl: e.activation(out=gg[:], in_=acc[0][sl][:], func=AF.Gelu_apprx_tanh), r=[Bacc[0][sl]], w=[Bgg])
                S.op("vector", lambda e, sl=sl, j=j, actT=actT: vtt(e, out=actT[:, j, :], in0=gg[:], in1=acc[1][sl][:], op=ALU.mult),
                     r=[Bgg, Bacc[1][sl]], w=[Bact])
                for _ in range(npp):
                    if pend:
                        pend.pop(0)()
            while pend:
                pend.pop(0)()
            pend = down_pieces(tt)
            if tt % 16 == 15 and tt + 1 < NTT:
                while pend:
                    pend.pop(0)()
                S.flush()
        while pend:
            pend.pop(0)()
        S.flush()


SCRATCH = {
    "z_tm": (lambda S: [S, 1024], F32), "xbcT": (lambda S: [2048, S], BF16), "dt_tm": (lambda S: [S, 16], F32),
    "sbqT": (lambda S: [512, S], BF16), "sbkT": (lambda S: [512, S], BF16), "sbv": (lambda S: [S, 512], BF16),
    "nqT": (lambda S: [512, S], BF16), "kcmpT": (lambda S: [128, S], BF16), "vcmpT": (lambda S: [128, S], BF16),
    "kselT": (lambda S: [128, S], BF16), "vsel": (lambda S: [S, 128], BF16), "kwinT": (lambda S: [128, S], BF16),
    "vwin": (lambda S: [S, 128], BF16), "ngate": (lambda S: [S, 24], F32), "mgT": (lambda S: [3072, S], BF16),
    "yssmT": (lambda S: [1024, S], BF16), "ysbT": (lambda S: [512, S], BF16), "ynsaT": (lambda S: [512, S], BF16),
    "x1": (lambda S: [S, 1024], F32), "xmid": (lambda S: [S, 1024], F32),
    "ssm_xB": (lambda S: [S, 1536], BF16), "ssm_BT": (lambda S: [512, S], BF16), "ssm_CT": (lambda S: [512, S], BF16),
}
WEIGHTS = {
    "pre_mix_norm": [L_, D], "w_in": [L_, D, DIN], "w_br_ssm": [L_, 1024, D], "w_br_sb": [L_, 512, D], "w_br_nsa": [L_, 512, D],
    "w_out": [L_, D, D], "post_mix_norm": [L_, D], "pre_ffn_norm": [L_, D], "ffn_w_up": [L_, D, 2 * DFF],
    "ssm_conv_wT": [L_, 128, 16, 4], "ssm_conv_bT": [L_, 128, 16], "ssm_dt_bias": [L_, 16], "ssm_a_log": [L_, 16], "ssm_d": [L_, 16],
    "ssm_norm": [L_, D], "c_tri": [128, 128], "c_sgt": [128, 128], "c_nuti": [128, 128], "c_sbmask": [128, 4, 512], "c_sbneg": [128, 4, 512],
    "cmp_w1_k": [L_, 2048, 64], "cmp_w2_k": [L_, 64, 64], "cmp_posT_k": [L_, 64, 32],
    "cmp_w1_v": [L_, 2048, 64], "cmp_w2_v": [L_, 64, 64], "cmp_posT_v": [L_, 64, 32],
    "ffn_conv_wT": [L_, 128, 44, 3], "ffn_conv_bT": [L_, 128, 44], "ffn_w_down": [L_, DFF, D], "post_ffn_norm": [L_, D],
}


def build(SEQ, phases, ext_in=(), ext_out=(), nlayers=L_):
    nc = bass.Bass("TRN2", target_bir_lowering=False)
    C = Ctx()
    C.nc, C.SEQ = nc, SEQ
    C.dr, C.Bdr = {}, {}
    x_in = nc.dram_tensor("x", [SEQ, D], F32, kind="ExternalInput").ap()
    out = nc.dram_tensor("out", [SEQ, D], F32, kind="ExternalOutput").ap()
    for nm, shp in WEIGHTS.items():
        C.dr[nm] = nc.dram_tensor(nm, shp, F32, kind="ExternalInput").ap()
    C.dr["ident"] = nc.dram_tensor("ident", [128, 128], F32, kind="ExternalInput").ap()
    for nm, shp in nsa_const_shapes(SEQ).items():
        C.dr[nm] = nc.dram_tensor(nm, shp, F32, kind="ExternalInput").ap()
    for nm, (sf, dt) in SCRATCH.items():
        kind = "ExternalInput" if nm in ext_in else ("ExternalOutput" if nm in ext_out else "Internal")
        C.dr[nm] = nc.dram_tensor(nm, sf(SEQ), dt, kind=kind).ap()
        C.Bdr[nm] = Buf(nm)
    C.Bx, C.Bout = Buf(), Buf()
    with ExitStack() as st:
        C.S = Sched(nc, st)
        T, P = _pool(C, st)
        C.ident = T("c_ident", [128, 128])
        C.eps_t = T("c_eps", [128, 1])
        C.Bconst = Buf()
        C.S.op("sync", lambda e: e.dma_start(out=C.ident[:], in_=C.dr["ident"][:, :]), w=[C.Bconst], dma=True)
        C.S.op("gpsimd", lambda e: e.memset(C.eps_t[:], EPS), w=[C.Bconst])
        C.identb = T("c_identb", [128, 128], BF16)
        C.one_t = T("c_one", [128, 1])
        C.S.op("gpsimd", lambda e: e.memset(C.one_t[:], 1.0), w=[C.Bconst])
        C.S.op("vector", lambda e: e.tensor_copy(out=C.identb[:], in_=C.ident[:]), r=[C.Bconst], w=[C.Bconst])
        x_src, Bsrc = x_in, C.Bx
        for l in range(nlayers):
            last = l == nlayers - 1
            if "A" in phases:
                phase_A(C, l, x_src, Bsrc)
            if "B" in phases:
                phase_B0(C, l)
                phase_B1(C, l)
            if "SB" in phases:
                phase_SB(C, l)
            if "NSA" in phases:
                phase_NSA(C, l)
            if "Ea" in phases:
                phase_Ea(C, l, x_src, Bsrc)
            if "Eb" in phases:
                phase_Eb(C, l, out if last else C.dr["xmid"], C.Bout if last else C.Bdr["xmid"])
            x_src, Bsrc = C.dr["xmid"], C.Bdr["xmid"]
        C.S.flush()
    return nc


def nsa_const_shapes(SEQ):
    NCP = ((SEQ // 16 - 1 + 127) // 128) * 128
    return {"c_qaug": [4, 8, SEQ], "c_kaug": [4, SEQ], "c_kcaug": [4, NCP], "c_pool": [128, NCP // 128, 128], "c_G": [128, SEQ],
            "c_caus": [128, 4, 128], "c_anti": [128, 4, 128], "c_cm": [128, 16, 2, 128], "c_patf": [128, 256], "c_patv": [128, 256]}


def nsa_consts(SEQ):
    NC_ = SEQ // 16 - 1
    NCP = ((NC_ + 127) // 128) * 128
    t = np.arange(SEQ)
    slopes = 2.0 ** (-(np.arange(8) + 1.0))
    qaug = np.zeros((4, 8, SEQ), np.float32)
    qaug[0] = slopes[:, None]
    qaug[1] = 128.0 * slopes[:, None]
    qaug[2] = -128.0 * slopes[:, None] * (t // 128)[None, :]
    qaug[3] = 2048.0 * slopes[:, None]
    kaug = np.zeros((4, SEQ), np.float32)
    kaug[0] = (t % 128) - 127
    kaug[1] = t // 128
    kaug[2] = 1.0
    n = np.arange(NCP)
    kcaug = np.zeros((4, NCP), np.float32)
    kcaug[0] = 16.0 * ((n % 128) - 6)
    kcaug[2] = 1.0
    kcaug[3] = n // 128
    j = np.arange(128)
    pool = np.zeros((NCP, 128), np.float32)
    for nn in range(NC_):
        for jj in range(max(0, (nn - 3 + 3) // 4), 128):
            if 4 * jj - 1 <= nn <= 4 * jj + 3:
                pool[nn, jj] = 1.0
            if 4 * jj - 1 > nn:
                break
    pool = np.ascontiguousarray(pool.reshape(NCP // 128, 128, 128).transpose(1, 0, 2))
    G = (t[None, :] // 64 == j[:, None]).astype(np.float32)
    k = np.arange(128)[:, None]
    q = np.arange(128)[None, :]
    caus = np.where(k <= q, 0.0, NEGB).astype(np.float32)
    anti = np.where(k > q, 0.0, NEGB).astype(np.float32)
    caus4 = np.ascontiguousarray(np.broadcast_to(caus[:, None, :], (128, 4, 128)))
    anti4 = np.ascontiguousarray(np.broadcast_to(anti[:, None, :], (128, 4, 128)))
    cm = np.zeros((128, 16, 2, 128), np.float32)
    for r in range(16):
        for idx in range(2):
            rel = 1 - idx
            cm[:, r, idx, :] = np.where(16 * k + 31 <= 128 * (r + 16 * rel) + q, 0.0, NEGB)
    u = np.arange(256)[None, :] - 127
    cq = (np.arange(128) // 64)[:, None]
    patf = np.where((u == cq) | (u == cq - 1), 1e30, -3e38).astype(np.float32)
    patv = np.where(u <= cq, 3e38, -1e30).astype(np.float32)
    return {"c_qaug": qaug, "c_kaug": kaug, "c_kcaug": kcaug, "c_pool": pool, "c_G": G, "c_caus": caus4, "c_anti": anti4,
            "c_cm": cm, "c_patf": patf, "c_patv": patv}


def host_inputs(inputs):
    f = lambda a: np.ascontiguousarray(np.asarray(a, dtype=np.float32))
    w = {k: f(inputs[k]) for k in ("pre_mix_norm", "w_in", "w_br_ssm", "w_br_sb", "w_br_nsa", "w_out", "post_mix_norm",
                                   "pre_ffn_norm", "ffn_w_up", "ffn_w_down", "post_ffn_norm")}
    w["ffn_conv_wT"] = f(np.transpose(np.asarray(inputs["ffn_conv_w"]).reshape(L_, 3, 44, 128), (0, 3, 2, 1)))
    w["ffn_conv_bT"] = f(np.transpose(np.asarray(inputs["ffn_conv_b"]).reshape(L_, 44, 128), (0, 2, 1)))
    w["ident"] = np.eye(128, dtype=np.float32)
    w["ssm_conv_wT"] = f(np.transpose(np.asarray(inputs["ssm_conv_w"]).reshape(L_, 4, 16, 128), (0, 3, 2, 1)))
    w["ssm_conv_bT"] = f(np.transpose(np.asarray(inputs["ssm_conv_b"]).reshape(L_, 16, 128), (0, 2, 1)))
    for k in ("ssm_dt_bias", "ssm_a_log", "ssm_d", "ssm_norm", "cmp_w1_k", "cmp_w2_k", "cmp_w1_v", "cmp_w2_v"):
        w[k] = f(inputs[k])
    w["cmp_posT_k"] = f(np.transpose(np.asarray(inputs["cmp_pos_k"]), (0, 2, 1)))
    w["cmp_posT_v"] = f(np.transpose(np.asarray(inputs["cmp_pos_v"]), (0, 2, 1)))
    ii = np.arange(128)
    w["c_tri"] = (ii[:, None] <= ii[None, :]).astype(np.float32)
    w["c_sgt"] = (ii[:, None] > ii[None, :]).astype(np.float32)
    w["c_nuti"] = -(ii[:, None] >= ii[None, :]).astype(np.float32)
    md = np.zeros((128, 4, 512), np.float32)
    for j in range(4):
        for b in range(4):
            if b == j:
                md[:, j, b * 128:(b + 1) * 128] = (ii[:, None] < ii[None, :])
            elif b > j:
                md[:, j, b * 128:(b + 1) * 128] = 1.0
    w["c_sbmask"] = md
    w["c_sbneg"] = ((1.0 - md) * -30000.0).astype(np.float32)
    return w


def phase_B0(C, l):
    nc, S, SEQ = C.nc, C.S, C.SEQ
    dr = C.dr
    with ExitStack() as st:
        T, P = _pool(C, st)
        cw = T("B0_cw", [128, 16, 4])
        cb = T("B0_cb", [128, 16])
        dg = T("B0_dg", [128, 16, 4, 128], BF16)
        BW = Buf()
        S.op("sync", lambda e: e.dma_start(out=cw[:], in_=dr["ssm_conv_wT"][l]), w=[BW], dma=True)
        S.op("sync", lambda e: e.dma_start(out=cb[:], in_=dr["ssm_conv_bT"][l]), w=[BW], dma=True)
        for c in range(16):
            for k in range(4):
                S.op("vector", lambda e, c=c, k=k: e.tensor_scalar(out=dg[:, c, k, :], in0=C.ident[:], scalar1=cw[:, c, k:k + 1], scalar2=None,
                                                                   op0=ALU.mult), r=[BW, C.Bconst], w=[BW])
        xin = [T(f"B0_xin{i}", [128, 16, 515], BF16) for i in range(2)]
        Bxin = [Buf() for _ in range(2)]
        xc = [T(f"B0_xc{i}", [128, 16, 512], BF16) for i in range(2)]
        Bxc = [Buf() for _ in range(2)]
        ps = [P(f"B0_ps{i}", [128, 512]) for i in range(4)]
        Bps = [Buf() for _ in range(4)]
        pst = [P(f"B0_pst{i}", [128, 1024], BF16) for i in range(2)]
        Bpst = [Buf() for _ in range(2)]
        xs = [T(f"B0_xs{i}", [128, 1536], BF16) for i in range(2)]
        Bxs = [Buf() for _ in range(2)]
        NT5 = SEQ // 512
        xv = dr["xbcT"].rearrange("(c p) t -> p c t", p=128)
        n_ps = 0
        for tt in range(NT5):
            s = tt % 2
            t0 = tt * 512
            if tt == 0:
                S.op("gpsimd", lambda e, s=s: e.memset(xin[s][:, :, 0:3], 0.0), w=[Bxin[s]])
                S.op("sync", lambda e, s=s: e.dma_start(out=xin[s][:, :, 3:515], in_=xv[:, :, 0:512]), r=[C.Bdr["xbcT"]], w=[Bxin[s]], dma=True)
            else:
                S.op("sync", lambda e, s=s, t0=t0: e.dma_start(out=xin[s][:, :, :], in_=xv[:, :, t0 - 3:t0 + 512]),
                     r=[C.Bdr["xbcT"]], w=[Bxin[s]], dma=True)
            for c in range(16):
                pi = n_ps % 4
                n_ps += 1
                for k in range(4):
                    S.op("tensor", lambda e, pi=pi, c=c, k=k, s=s: e.matmul(ps[pi][:, :], lhsT=dg[:, c, k, :], rhs=xin[s][:, c, k:k + 512],
                                                                           start=(k == 0), stop=(k == 3)), r=[BW, Bxin[s]], w=[Bps[pi]])
                S.op("scalar", lambda e, pi=pi, c=c, s=s: e.activation(out=xc[s][:, c, :], in_=ps[pi][:, :], func=AF.Silu, bias=cb[:, c:c + 1]),
                     r=[Bps[pi], BW], w=[Bxc[s]])
            S.op("sync", lambda e, s=s, t0=t0: e.dma_start(out=dr["ssm_BT"][:, t0:t0 + 512].rearrange("(c p) t -> p c t", p=128),
                                                           in_=xc[s][:, 8:12, :]), r=[Bxc[s]], w=[C.Bdr["ssm_BT"]], dma=True)
            S.op("sync", lambda e, s=s, t0=t0: e.dma_start(out=dr["ssm_CT"][:, t0:t0 + 512].rearrange("(c p) t -> p c t", p=128),
                                                           in_=xc[s][:, 12:16, :]), r=[Bxc[s]], w=[C.Bdr["ssm_CT"]], dma=True)
            for sub in range(4):
                i = tt * 4 + sub
                q = i % 2
                for c in range(8):
                    S.op("tensor", lambda e, q=q, c=c, s=s, sub=sub: e.transpose(out=pst[q][:, c * 128:(c + 1) * 128],
                                                                                 in_=xc[s][:, c, sub * 128:(sub + 1) * 128], identity=C.identb[:]),
                         r=[Bxc[s], C.Bconst], w=[Bpst[q]])
                S.op("vector", lambda e, q=q: e.tensor_copy(out=xs[q][:, 0:1024], in_=pst[q][:, :]), r=[Bpst[q]], w=[Bxs[q]])
                for c in range(4):
                    S.op("tensor", lambda e, q=q, c=c, s=s, sub=sub: e.transpose(out=pst[q][:, c * 128:(c + 1) * 128],
                                                                                 in_=xc[s][:, 8 + c, sub * 128:(sub + 1) * 128], identity=C.identb[:]),
                         r=[Bxc[s], C.Bconst], w=[Bpst[q]])
                S.op("scalar", lambda e, q=q: e.copy(out=xs[q][:, 1024:1536], in_=pst[q][:, 0:512]), r=[Bpst[q]], w=[Bxs[q]])
                S.op("sync", lambda e, q=q, i=i: e.dma_start(out=dr["ssm_xB"][i * 128:(i + 1) * 128, :], in_=xs[q][:, :]),
                     r=[Bxs[q]], w=[C.Bdr["ssm_xB"]], dma=True)
        S.flush()


def phase_B1(C, l):
    nc, S, SEQ = C.nc, C.S, C.SEQ
    dr = C.dr
    NCH = SEQ // 128
    with ExitStack() as st:
        T, P = _pool(C, st)
        tri = T("B1_tri", [128, 128])
        sgt = T("B1_sgt", [128, 128])
        ones = T("B1_ones", [128, 128])
        dt = T("B1_dt", [128, NCH, 16])
        dta = T("B1_dta", [128, NCH, 16])
        tmpb = T("B1_tmpb", [128, 16])
        ea = T("B1_ea", [128, 16])
        Dbc = T("B1_D", [128, 16])
        nw = T("B1_nw", [128, D])
        BK = Buf()
        S.op("sync", lambda e: e.dma_start(out=tri[:], in_=dr["c_tri"][:, :]), w=[BK], dma=True)
        S.op("sync", lambda e: e.dma_start(out=sgt[:], in_=dr["c_sgt"][:, :]), w=[BK], dma=True)
        S.op("gpsimd", lambda e: e.memset(ones[:], 1.0), w=[BK])
        dtv = dr["dt_tm"].rearrange("(c p) h -> p c h", p=128)
        for c0 in range(0, NCH, 8):
            c1 = min(NCH, c0 + 8)
            S.op("sync", lambda e, c0=c0, c1=c1: e.dma_start(out=dt[:, c0:c1, :], in_=dtv[:, c0:c1, :]), r=[C.Bdr["dt_tm"]], w=[BK], dma=True)
        S.op("sync", lambda e: e.dma_start(out=tmpb[:], in_=dr["ssm_dt_bias"][l:l + 1, :].partition_broadcast(128)), w=[BK], dma=True)
        S.op("sync", lambda e: e.dma_start(out=ea[:], in_=dr["ssm_a_log"][l:l + 1, :].partition_broadcast(128)), w=[BK], dma=True)
        S.op("sync", lambda e: e.dma_start(out=Dbc[:], in_=dr["ssm_d"][l:l + 1, :].partition_broadcast(128)), w=[BK], dma=True)
        S.op("sync", lambda e: e.dma_start(out=nw[:], in_=dr["ssm_norm"][l:l + 1, :].partition_broadcast(128)), w=[BK], dma=True)
        S.op("vector", lambda e: vtt(e, out=dt[:], in0=dt[:], in1=tmpb[:].unsqueeze(1).to_broadcast([128, NCH, 16]), op=ALU.add),
             r=[BK], w=[BK])
        S.op("scalar", lambda e: e.activation(out=dt[:], in_=dt[:], func=AF.Exp), r=[BK], w=[BK])
        S.op("scalar", lambda e: e.activation(out=dt[:], in_=dt[:], func=AF.Ln, bias=C.one_t[:, 0:1]), r=[BK, C.Bconst], w=[BK])
        S.op("scalar", lambda e: e.activation(out=ea[:], in_=ea[:], func=AF.Exp), r=[BK], w=[BK])
        S.op("vector", lambda e: e.scalar_tensor_tensor(out=dta[:], in0=dt[:], scalar=-1.0, in1=ea[:].unsqueeze(1).to_broadcast([128, NCH, 16]),
                                                        op0=ALU.mult, op1=ALU.mult), r=[BK], w=[BK])
        xB = [T(f"B1_xB{i}", [128, 1536], BF16) for i in range(2)]
        bT = [T(f"B1_bT{i}", [128, 4, 128], BF16) for i in range(2)]
        cT = [T(f"B1_cT{i}", [128, 4, 128], BF16) for i in range(2)]
        sz = [T(f"B1_sz{i}", [128, D]) for i in range(2)]
        Bin = [Buf() for _ in range(2)]
        Lm = T("B1_Lm", [128, 16, 128])
        E = T("B1_E", [128, 16, 128])
        CBm = T("B1_CBm", [128, 4, 128])
        Wp = T("B1_Wp", [128, 16, 128], BF16)
        xdt = T("B1_xdt", [128, D], BF16)
        xw = T("B1_xw", [128, D], BF16)
        y1 = T("B1_y1", [128, D])
        t2 = T("B1_t2", [128, D])
        sm = T("B1_sm", [128, 4, 16])
        ssq4 = T("B1_ssq4", [128, 4])
        rs4 = T("B1_rs4", [128, 4])
        junk = T("B1_junk", [128, 256], BF16)
        stt_ = T("B1_st", [128, D])
        stbf = T("B1_stbf", [128, D], BF16)
        ysT = [T(f"B1_ysT{i}", [128, 8, 512], BF16) for i in range(2)]
        BLm, BE, BCBm, BWp, Bxdt, Bxw, By1, Bt2, Bsm, Bssq4, Brs4, Bjunk, Bst, Bstbf = (Buf() for _ in range(14))
        BysT = [Buf() for _ in range(2)]
        ps_seg = P("B1_pseg", [128, 1024])
        ps_cb = P("B1_pcb", [128, 512])
        ps_sm = P("B1_psm", [128, 512])
        ps_yi = P("B1_pyi", [128, 1024])
        ps_yo = P("B1_pyo", [128, 1024])
        Bpseg, Bpcb, Bpsm, Bpyi, Bpyo = (Buf() for _ in range(5))
        S.op("gpsimd", lambda e: e.memset(stt_[:], 0.0), w=[Bst])
        S.op("gpsimd", lambda e: e.memset(stbf[:], 0.0), w=[Bstbf])
        BTv = dr["ssm_BT"].rearrange("(g n) t -> n g t", n=128)
        CTv = dr["ssm_CT"].rearrange("(g n) t -> n g t", n=128)

        def loads(c):
            s = c % 2
            tk = slice(c * 128, (c + 1) * 128)
            S.op("sync", lambda e: e.dma_start(out=xB[s][:], in_=dr["ssm_xB"][tk, :]), r=[C.Bdr["ssm_xB"]], w=[Bin[s]], dma=True)
            S.op("sync", lambda e: e.dma_start(out=bT[s][:], in_=BTv[:, :, tk]), r=[C.Bdr["ssm_BT"]], w=[Bin[s]], dma=True)
            S.op("sync", lambda e: e.dma_start(out=cT[s][:], in_=CTv[:, :, tk]), r=[C.Bdr["ssm_CT"]], w=[Bin[s]], dma=True)
            S.op("sync", lambda e: e.dma_start(out=sz[s][:], in_=dr["z_tm"][tk, :]), r=[C.Bdr["z_tm"]], w=[Bin[s]], dma=True)
        loads(0)
        for c in range(NCH):
            s = c % 2
            if c + 1 < NCH:
                loads(c + 1)
            xtm = xB[s][:, 0:1024]
            btm = xB[s][:, 1024:1536]
            S.op("gpsimd", lambda e, c=c: e.tensor_tensor(out=Lm[:], in0=sgt[:].unsqueeze(1).to_broadcast([128, 16, 128]),
                                                          in1=dta[:, c, :].unsqueeze(2).to_broadcast([128, 16, 128]), op=ALU.mult),
                 r=[BK], w=[BLm])
            S.op("tensor", lambda e, c=c: e.matmul(ps_sm[:, 0:16], lhsT=tri[:], rhs=dta[:, c, :], start=True, stop=True), r=[BK], w=[Bpsm])
            S.op("tensor", lambda e, c=c: e.matmul(ps_sm[:, 16:32], lhsT=ones[:], rhs=dta[:, c, :], start=True, stop=True), r=[BK], w=[Bpsm])
            S.op("scalar", lambda e: e.copy(out=sm[:, 0, :], in_=ps_sm[:, 0:16]), r=[Bpsm], w=[Bsm])
            S.op("scalar", lambda e: e.activation(out=sm[:, 1:3, :], in_=ps_sm[:, 0:32].rearrange("p (a h) -> p a h", a=2), func=AF.Exp),
                 r=[Bpsm], w=[Bsm])
            S.op("vector", lambda e: vtt(e, out=sm[:, 3, :], in0=ps_sm[:, 16:32], in1=sm[:, 0, :], op=ALU.subtract), r=[Bpsm, Bsm], w=[Bsm])
            S.op("scalar", lambda e: e.activation(out=sm[:, 3, :], in_=sm[:, 3, :], func=AF.Exp), r=[Bsm], w=[Bsm])
            for g in range(4):
                S.op("tensor", lambda e, g=g, s=s: e.matmul(ps_cb[:, g * 128:(g + 1) * 128], lhsT=bT[s][:, g, :], rhs=cT[s][:, g, :],
                                                            start=True, stop=True), r=[Bin[s]], w=[Bpcb])
            S.op("vector", lambda e: vtt(e, out=CBm[:], in0=ps_cb[:, :].rearrange("p (g t) -> p g t", g=4),
                                                     in1=tri[:].unsqueeze(1).to_broadcast([128, 4, 128]), op=ALU.mult), r=[Bpcb, BK], w=[BCBm])
            for hf in range(2):
                for hh in range(8):
                    h = hf * 8 + hh
                    S.op("tensor", lambda e, h=h, hh=hh: e.matmul(ps_seg[:, hh * 128:(hh + 1) * 128], lhsT=Lm[:, h, :], rhs=tri[:],
                                                                  start=True, stop=True), r=[BLm, BK], w=[Bpseg])
                S.op("scalar", lambda e, hf=hf: e.activation(out=E[:, hf * 8:(hf + 1) * 8, :],
                                                             in_=ps_seg[:, :].rearrange("p (h t) -> p h t", h=8), func=AF.Exp),
                     r=[Bpseg], w=[BE])
            for g in range(4):
                S.op("vector", lambda e, g=g: vtt(e, out=Wp[:, 4 * g:4 * g + 4, :], in0=E[:, 4 * g:4 * g + 4, :],
                                                  in1=CBm[:, g, :].unsqueeze(1).to_broadcast([128, 4, 128]), op=ALU.mult),
                     r=[BE, BCBm], w=[BWp])
            S.op("gpsimd", lambda e, c=c, xtm=xtm: e.tensor_tensor(out=xdt[:].rearrange("p (h q) -> p h q", h=16),
                                                                   in0=xtm.rearrange("p (h q) -> p h q", h=16),
                                                                   in1=dt[:, c, :].unsqueeze(2).to_broadcast([128, 16, 64]), op=ALU.mult),
                 r=[Bin[s], BK], w=[Bxdt])
            for h in range(16):
                S.op("tensor", lambda e, h=h: e.matmul(ps_yi[:, h * 64:(h + 1) * 64], lhsT=Wp[:, h, :], rhs=xdt[:, h * 64:(h + 1) * 64],
                                                       start=True, stop=True), r=[BWp, Bxdt], w=[Bpyi])
            for g in range(4):
                S.op("tensor", lambda e, g=g, s=s: e.matmul(ps_yo[:, g * 256:(g + 1) * 256], lhsT=cT[s][:, g, :], rhs=stbf[:, g * 256:(g + 1) * 256],
                                                            start=True, stop=True), r=[Bin[s], Bstbf], w=[Bpyo])
            S.op("vector", lambda e: vtt(e, out=y1[:].rearrange("p (h q) -> p h q", h=16),
                                                     in0=ps_yo[:, :].rearrange("p (h q) -> p h q", h=16),
                                                     in1=sm[:, 1, :].unsqueeze(2).to_broadcast([128, 16, 64]), op=ALU.mult),
                 r=[Bpyo, Bsm], w=[By1])
            S.op("vector", lambda e: vtt(e, out=y1[:], in0=y1[:], in1=ps_yi[:, :], op=ALU.add), r=[By1, Bpyi], w=[By1])
            S.op("gpsimd", lambda e, xtm=xtm: e.tensor_tensor(out=t2[:].rearrange("p (h q) -> p h q", h=16),
                                                              in0=xtm.rearrange("p (h q) -> p h q", h=16),
                                                              in1=Dbc[:].unsqueeze(2).to_broadcast([128, 16, 64]), op=ALU.mult),
                 r=[Bin[s], BK], w=[Bt2])
            S.op("gpsimd", lambda e: e.tensor_tensor(out=y1[:], in0=y1[:], in1=t2[:], op=ALU.add), r=[By1, Bt2], w=[By1])
            S.op("gpsimd", lambda e, s=s: e.tensor_tensor(out=y1[:], in0=y1[:], in1=sz[s][:], op=ALU.mult), r=[By1, Bin[s]], w=[By1])
            for g in range(4):
                S.op("scalar", lambda e, g=g: e.activation(out=junk[:], in_=y1[:, g * 256:(g + 1) * 256], func=AF.Square, accum_out=ssq4[:, g:g + 1]),
                     r=[By1], w=[Bjunk, Bssq4])
            rstd_ops(C, ssq4[:], rs4[:], 256, Bssq4, Brs4)
            S.op("vector", lambda e: vtt(e, out=y1[:].rearrange("p (g q) -> p g q", g=4), in0=y1[:].rearrange("p (g q) -> p g q", g=4),
                                                     in1=rs4[:].unsqueeze(2).to_broadcast([128, 4, 256]), op=ALU.mult), r=[By1, Brs4], w=[By1])
            S.op("vector", lambda e: vtt(e, out=t2[:], in0=y1[:], in1=nw[:], op=ALU.mult), r=[By1, BK], w=[Bt2])
            for k in range(8):
                S.op("tensor", lambda e, k=k: e.transpose(out=ps_yi[:, k * 128:(k + 1) * 128], in_=t2[:, k * 128:(k + 1) * 128], identity=C.ident[:]),
                     r=[Bt2, C.Bconst], w=[Bpyi])
            ys = (c // 4) % 2
            qq = c % 4
            S.op("scalar", lambda e, ys=ys, qq=qq: e.copy(out=ysT[ys][:, :, qq * 128:(qq + 1) * 128],
                                                           in_=ps_yi[:, :].rearrange("p (k t) -> p k t", k=8)), r=[Bpyi], w=[BysT[ys]])
            if qq == 3:
                t0 = (c - 3) * 128
                S.op("sync", lambda e, ys=ys, t0=t0: e.dma_start(out=dr["yssmT"][:, t0:t0 + 512].rearrange("(k p) t -> p k t", p=128),
                                                                 in_=ysT[ys][:]), r=[BysT[ys]], w=[C.Bdr["yssmT"]], dma=True)
            S.op("gpsimd", lambda e: e.tensor_tensor(out=xw[:].rearrange("p (h q) -> p h q", h=16), in0=xdt[:].rearrange("p (h q) -> p h q", h=16),
                                                     in1=sm[:, 3, :].unsqueeze(2).to_broadcast([128, 16, 64]), op=ALU.mult),
                 r=[Bxdt, Bsm], w=[Bxw])
            for g in range(4):
                S.op("tensor", lambda e, g=g, btm=btm: e.matmul(ps_yo[:, g * 256:(g + 1) * 256], lhsT=btm[:, g * 128:(g + 1) * 128],
                                                                rhs=xw[:, g * 256:(g + 1) * 256], start=True, stop=True),
                     r=[Bin[s], Bxw], w=[Bpyo])
            S.op("vector", lambda e: vtt(e, out=stt_[:].rearrange("p (h q) -> p h q", h=16), in0=stt_[:].rearrange("p (h q) -> p h q", h=16),
                                                     in1=sm[:, 2, :].unsqueeze(2).to_broadcast([128, 16, 64]), op=ALU.mult), r=[Bst, Bsm], w=[Bst])
            S.op("vector", lambda e: vtt(e, out=stt_[:], in0=stt_[:], in1=ps_yo[:, :], op=ALU.add), r=[Bst, Bpyo], w=[Bst])
            S.op("scalar", lambda e: e.copy(out=stbf[:], in_=stt_[:]), r=[Bst], w=[Bstbf])
        S.flush()


def phase_SB(C, l):
    nc, S, SEQ = C.nc, C.S, C.SEQ
    dr = C.dr
    NB = SEQ // 128
    NQG = SEQ // 512
    with ExitStack() as st:
        T, P = _pool(C, st)
        nuti = T("SB_nuti", [128, 128])
        nones = T("SB_nones", [128, 128])
        MD = T("SB_MD", [128, 4, 512])
        NEGM = T("SB_NEGM", [128, 4, 512], BF16)
        BK = Buf()
        S.op("sync", lambda e: e.dma_start(out=nuti[:], in_=dr["c_nuti"][:, :]), w=[BK], dma=True)
        S.op("gpsimd", lambda e: e.memset(nones[:], -1.0), w=[BK])
        S.op("sync", lambda e: e.dma_start(out=MD[:], in_=dr["c_sbmask"][:, :, :]), w=[BK], dma=True)
        S.op("gpsimd", lambda e: e.dma_start(out=NEGM[:], in_=dr["c_sbneg"][:, :, :]), w=[BK], dma=True)
        KT = T("SB_KT", [128, SEQ], BF16)
        QT = T("SB_QT", [128, SEQ], BF16)
        V = T("SB_V", [128, NB, 128], BF16)
        BKT, BQT, BV = Buf(), Buf(), Buf()
        sp = [T(f"SB_sp{i}", [128, 512]) for i in range(3)]
        Bsp = [Buf() for _ in range(3)]
        R = T("SB_R", [128, 512])
        BR = Buf()
        wt = [T(f"SB_w{i}", [128, 512], BF16) for i in range(3)]
        Bw = [Buf() for _ in range(3)]
        osb = [T(f"SB_o{i}", [128, 512], BF16) for i in range(2)]
        Bosb = [Buf() for _ in range(2)]
        psA = [P(f"SB_pA{i}", [128, 512]) for i in range(2)]
        psW = [P(f"SB_pW{i}", [128, 512]) for i in range(2)]
        psO = [P(f"SB_pO{i}", [128, 512]) for i in range(2)]
        BpA, BpW, BpO = ([Buf() for _ in range(2)] for _ in range(3))
        for h in range(4):
            hs = slice(h * 128, (h + 1) * 128)
            S.op("sync", lambda e, hs=hs: e.dma_start(out=KT[:], in_=dr["sbkT"][hs, :]), r=[C.Bdr["sbkT"]], w=[BKT], dma=True)
            S.op("sync", lambda e, hs=hs: e.dma_start(out=QT[:], in_=dr["sbqT"][hs, :]), r=[C.Bdr["sbqT"]], w=[BQT], dma=True)
            vview = dr["sbv"][:, hs].rearrange("(b p) d -> p b d", p=128)
            for b0 in range(0, NB, 8):
                S.op("sync", lambda e, vview=vview, b0=b0: e.dma_start(out=V[:, b0:b0 + 8, :], in_=vview[:, b0:b0 + 8, :]),
                     r=[C.Bdr["sbv"]], w=[BV], dma=True)
            units = []
            for qg in range(NQG):
                for kb in range(4 * qg + 3, -1, -1):
                    units.append((qg, kb))
            U = len(units)

            def emitA(u):
                qg, kb = units[u]
                a = u % 2
                S.op("tensor", lambda e: e.matmul(psA[a][:, :], lhsT=KT[:, kb * 128:(kb + 1) * 128], rhs=QT[:, qg * 512:(qg + 1) * 512],
                                                  start=True, stop=True), r=[BKT, BQT], w=[BpA[a]])

            def emitSP(u):
                qg, kb = units[u]
                a, s3 = u % 2, u % 3
                j = kb - 4 * qg
                S.op("scalar", lambda e: e.activation(out=sp[s3][:], in_=psA[a][:, :], func=AF.Exp), r=[BpA[a]], w=[Bsp[s3]])
                S.op("scalar", lambda e: e.activation(out=sp[s3][:], in_=sp[s3][:], func=AF.Ln, bias=C.one_t[:, 0:1]),
                     r=[Bsp[s3], C.Bconst], w=[Bsp[s3]])
                if j >= 0:
                    S.op("vector", lambda e: vtt(e, out=sp[s3][:], in0=sp[s3][:], in1=MD[:, j, :], op=ALU.mult),
                         r=[Bsp[s3], BK], w=[Bsp[s3]])

            def emitW(u):
                qg, kb = units[u]
                a, s3 = u % 2, u % 3
                j = kb - 4 * qg
                first = kb == 4 * qg + 3
                mm = [(KT[:, kb * 128:(kb + 1) * 128], QT[:, qg * 512:(qg + 1) * 512], [BKT, BQT]), (nuti[:], sp[s3][:], [BK, Bsp[s3]])]
                if not first:
                    mm.append((nones[:], R[:], [BK, BR]))
                if j >= 0:
                    mm.append((C.identb[:], NEGM[:, j, :], [C.Bconst, BK]))
                for i, (lt, rh, rb) in enumerate(mm):
                    S.op("tensor", lambda e, lt=lt, rh=rh, i=i: e.matmul(psW[a][:, :], lhsT=lt, rhs=rh, start=(i == 0), stop=(i == len(mm) - 1)),
                         r=rb, w=[BpW[a]])
                if first:
                    S.op("vector", lambda e: e.tensor_copy(out=R[:], in_=sp[s3][:]), r=[Bsp[s3]], w=[BR])
                else:
                    S.op("vector", lambda e: vtt(e, out=R[:], in0=R[:], in1=sp[s3][:], op=ALU.add), r=[BR, Bsp[s3]], w=[BR])

            def emitEW(u):
                a, s3 = u % 2, u % 3
                S.op("scalar", lambda e: e.activation(out=wt[s3][:], in_=psW[a][:, :], func=AF.Exp), r=[BpW[a]], w=[Bw[s3]])

            def emitPV(u):
                qg, kb = units[u]
                s3 = u % 3
                o = qg % 2
                first = kb == 4 * qg + 3
                S.op("tensor", lambda e: e.matmul(psO[o][:, :], lhsT=V[:, kb, :], rhs=wt[s3][:], start=first, stop=(kb == 0)),
                     r=[BV, Bw[s3]], w=[BpO[o]])
                if kb == 0:
                    S.op("vector", lambda e: e.tensor_copy(out=osb[o][:], in_=psO[o][:, :]), r=[BpO[o]], w=[Bosb[o]])
                    S.op("sync", lambda e, hs=hs: e.dma_start(out=dr["ysbT"][hs, qg * 512:(qg + 1) * 512], in_=osb[o][:]),
                         r=[Bosb[o]], w=[C.Bdr["ysbT"]], dma=True)
            emitA(0)
            emitSP(0)
            for u in range(U):
                if u + 1 < U:
                    emitA(u + 1)
                emitW(u)
                if u + 1 < U:
                    emitSP(u + 1)
                emitEW(u)
                if u >= 1:
                    emitPV(u - 1)
            emitPV(U - 1)
            if h == 1:
                S.flush()
        S.flush()


NEGB = -30000.0


def phase_NSA(C, l):
    nc, S, SEQ = C.nc, C.S, C.SEQ
    dr = C.dr
    NB = SEQ // 128
    NC_ = SEQ // 16 - 1
    NCH = (NC_ + 127) // 128
    NCP = NCH * 128
    for g in range(2):
        with ExitStack() as st:
            T, P = _pool(C, st)
            BK = Buf()
            qa = T("N_qa", [68, 4, SEQ], BF16)
            ksa = T("N_ksa", [68, SEQ], BF16)
            kwa = T("N_kwa", [68, SEQ], BF16)
            kca = T("N_kca", [68, NCP], BF16)
            vs = T("N_vs", [128, NB, 65], BF16)
            vw = T("N_vw", [128, NB, 65], BF16)
            VR = T("N_VR", [128, NCH, 193], BF16)
            G = T("N_G", [128, SEQ], BF16)
            CAUS = T("N_caus", [128, 4, 128], BF16)
            ANTI = T("N_anti", [128, 4, 128], BF16)
            CM = T("N_cm", [128, 16, 2, 128], BF16)
            PATF = T("N_patf", [128, 256])
            PATV = T("N_patv", [128, 256])
            ng = T("N_ng", [128, NB, 24])
            qv = dr["nqT"][g * 256:(g + 1) * 256, :].rearrange("(h d) t -> d h t", d=64)
            S.op("sync", lambda e: e.dma_start(out=qa[0:64, :, :], in_=qv), r=[C.Bdr["nqT"]], w=[BK], dma=True)
            S.op("gpsimd", lambda e: e.dma_start(out=qa[64:68, :, :], in_=dr["c_qaug"][:, 4 * g:4 * g + 4, 0:SEQ]), w=[BK], dma=True)
            S.op("sync", lambda e: e.dma_start(out=ksa[0:64, :], in_=dr["kselT"][g * 64:(g + 1) * 64, :]), r=[C.Bdr["kselT"]], w=[BK], dma=True)
            S.op("sync", lambda e: e.dma_start(out=kwa[0:64, :], in_=dr["kwinT"][g * 64:(g + 1) * 64, :]), r=[C.Bdr["kwinT"]], w=[BK], dma=True)
            S.op("gpsimd", lambda e: e.dma_start(out=ksa[64:68, :], in_=dr["c_kaug"][:, 0:SEQ]), w=[BK], dma=True)
            S.op("gpsimd", lambda e: e.dma_start(out=kwa[64:68, :], in_=dr["c_kaug"][:, 0:SEQ]), w=[BK], dma=True)
            S.op("gpsimd", lambda e: e.memset(kca[0:64, :], 0.0), w=[BK])
            S.op("gpsimd", lambda e: e.dma_start(out=kca[64:68, :], in_=dr["c_kcaug"][:, 0:NCP]), w=[BK], dma=True)
            for (t_, nm) in ((vs, "vsel"), (vw, "vwin")):
                S.op("gpsimd", lambda e, t_=t_: e.memset(t_[:, :, 64:65], 1.0), w=[BK])
                vv = dr[nm][:, g * 64:(g + 1) * 64].rearrange("(b p) d -> p b d", p=128)
                for b0 in range(0, NB, 8):
                    S.op("sync", lambda e, t_=t_, vv=vv, b0=b0: e.dma_start(out=t_[:, b0:b0 + 8, 0:64], in_=vv[:, b0:b0 + 8, :]),
                         r=[C.Bdr[nm]], w=[BK], dma=True)
            S.op("gpsimd", lambda e: e.memset(VR[:, :, 0:64], 0.0), w=[BK])
            S.op("gpsimd", lambda e: e.memset(VR[:, :, 64:65], 1.0), w=[BK])
            S.op("gpsimd", lambda e: e.dma_start(out=VR[:, :, 65:193], in_=dr["c_pool"][:, 0:NCH, :]), w=[BK], dma=True)
            S.op("gpsimd", lambda e: e.dma_start(out=G[:], in_=dr["c_G"][:, 0:SEQ]), w=[BK], dma=True)
            S.op("gpsimd", lambda e: e.dma_start(out=CAUS[:], in_=dr["c_caus"][:, :, :]), w=[BK], dma=True)
            S.op("gpsimd", lambda e: e.dma_start(out=ANTI[:], in_=dr["c_anti"][:, :, :]), w=[BK], dma=True)
            S.op("gpsimd", lambda e: e.dma_start(out=CM[:], in_=dr["c_cm"][:, :, :, :]), w=[BK], dma=True)
            S.op("sync", lambda e: e.dma_start(out=PATF[:], in_=dr["c_patf"][:, :]), w=[BK], dma=True)
            S.op("sync", lambda e: e.dma_start(out=PATV[:], in_=dr["c_patv"][:, :]), w=[BK], dma=True)
            ngv = dr["ngate"].rearrange("(b p) c -> p b c", p=128)
            for b0 in range(0, NB, 8):
                S.op("sync", lambda e, b0=b0: e.dma_start(out=ng[:, b0:b0 + 8, :], in_=ngv[:, b0:b0 + 8, :]), r=[C.Bdr["ngate"]], w=[BK], dma=True)
            psc = [P(f"N_psc{i}", [128, 512]) for i in range(2)]
            Bpsc = [Buf() for _ in range(2)]
            pnc = P("N_pnc", [128, 2, 512])
            pns = P("N_pns", [128, 512])
            pnw = P("N_pnw", [128, 512])
            ptr = P("N_ptr", [128, 512])
            pty = P("N_pty", [128, 512])
            Bpnc, Bpns, Bpnw, Bptr, Bpty = (Buf() for _ in range(5))
            with ExitStack() as st0:
                T0, _ = _pool(C, st0)
                B0 = Buf()
                kin = T0("N0_kin", [64, SEQ], BF16)
                vin = T0("N0_vin", [64, SEQ], BF16)
                w1 = [T0(f"N0_w1{i}", [64, 32, 64], BF16) for i in range(2)]
                w2 = [T0(f"N0_w2{i}", [64, 64], BF16) for i in range(2)]
                pT = [T0(f"N0_pT{i}", [64, 32], BF16) for i in range(2)]
                cb = [T0(f"N0_cb{i}", [64, 1]) for i in range(2)]
                hh_ = [T0(f"N0_h{i}", [64, NCP], BF16) for i in range(2)]
                S.op("sync", lambda e: e.dma_start(out=kin[:], in_=dr["kcmpT"][g * 64:(g + 1) * 64, :]), r=[C.Bdr["kcmpT"]], w=[B0], dma=True)
                S.op("sync", lambda e: e.dma_start(out=vin[:], in_=dr["vcmpT"][g * 64:(g + 1) * 64, :]), r=[C.Bdr["vcmpT"]], w=[B0], dma=True)
                for i, kvn in enumerate(("k", "v")):
                    S.op("gpsimd", lambda e, i=i, kvn=kvn: e.dma_start(out=w1[i][:], in_=dr[f"cmp_w1_{kvn}"][l].rearrange("(j d) o -> d j o", d=64)),
                         w=[B0], dma=True)
                    S.op("gpsimd", lambda e, i=i, kvn=kvn: e.dma_start(out=w2[i][:], in_=dr[f"cmp_w2_{kvn}"][l]), w=[B0], dma=True)
                    S.op("gpsimd", lambda e, i=i, kvn=kvn: e.dma_start(out=pT[i][:], in_=dr[f"cmp_posT_{kvn}"][l]), w=[B0], dma=True)
                    S.op("gpsimd", lambda e, i=i: e.memset(hh_[i][:], 0.0), w=[B0])
                for i, src in enumerate((kin, vin)):
                    for j in range(32):
                        S.op("tensor", lambda e, i=i, j=j: e.matmul(ptr[0:64, 0:1], lhsT=w1[i][:, j, :], rhs=pT[i][:, j:j + 1],
                                                                    start=(j == 0), stop=(j == 31)), r=[B0], w=[Bptr])
                    S.op("vector", lambda e, i=i: e.tensor_copy(out=cb[i][:], in_=ptr[0:64, 0:1]), r=[Bptr], w=[B0])
                    for j in range(32):
                        S.op("tensor", lambda e, i=i, j=j, src=src: e.matmul(psc[0][0:64, 0:NC_], lhsT=w1[i][:, j, :],
                                                                             rhs=src[:, j:j + 16 * (NC_ - 1) + 1:16],
                                                                             start=(j == 0), stop=(j == 31)), r=[B0], w=[Bpsc[0]])
                    S.op("scalar", lambda e, i=i: e.activation(out=hh_[i][:, 0:NC_], in_=psc[0][0:64, 0:NC_], func=AF.Silu, bias=cb[i][:, 0:1]),
                         r=[Bpsc[0], B0], w=[B0])
                    if i == 0:
                        S.op("tensor", lambda e: e.matmul(psc[1][0:64, 0:NC_], lhsT=w2[0][:], rhs=hh_[0][:, 0:NC_], start=True, stop=True),
                             r=[B0], w=[Bpsc[1]])
                        S.op("vector", lambda e: e.tensor_copy(out=kca[0:64, 0:NC_], in_=psc[1][0:64, 0:NC_]), r=[Bpsc[1]], w=[BK])
                    else:
                        for c in range(NCH):
                            S.op("tensor", lambda e, c=c: e.matmul(psc[1][:, c * 64:(c + 1) * 64], lhsT=hh_[1][:, c * 128:(c + 1) * 128], rhs=w2[1][:],
                                                                   start=True, stop=True), r=[B0], w=[Bpsc[1]])
                        S.op("vector", lambda e: e.tensor_copy(out=VR[:, :, 0:64], in_=psc[1][:, 0:NCH * 64].rearrange("p (c d) -> p c d", d=64)),
                             r=[Bpsc[1]], w=[BK])
                S.flush()
            pT_ = [T(f"N_pT{i}", [128, 512], BF16) for i in range(3)]
            BpT = [Buf() for _ in range(3)]
            sm = T("N_sm", [128, 8, 4])
            imp = T("N_imp", [128, 128])
            m8 = T("N_m8", [128, 8])
            nmt = [T(f"N_nmt{i}", [128, 128], BF16) for i in range(2)]
            yt = [T(f"N_yt{i}", [128, 256]) for i in range(2)]
            yo = [T(f"N_yo{i}", [128, 2, 128], BF16) for i in range(2)]
            Bsm, Bimp, Bm8 = Buf(), Buf(), Buf()
            Bnmt, Byt, Byo = ([Buf() for _ in range(2)] for _ in range(3))
            cnt = {"u": 0}

            pend = []

            def drain():
                while pend:
                    pend.pop(0)()

            def unit(mms, num, Bnum, vtile_of, first, last):
                u = cnt["u"]
                cnt["u"] += 1
                a, s3 = u % 2, u % 3
                for i, (lt, rh, rb) in enumerate(mms):
                    S.op("tensor", lambda e, lt=lt, rh=rh, i=i, a=a: e.matmul(psc[a][:, :], lhsT=lt, rhs=rh, start=(i == 0), stop=(i == len(mms) - 1)),
                         r=rb, w=[Bpsc[a]])
                S.op("scalar", lambda e, a=a, s3=s3: e.activation(out=pT_[s3][:], in_=psc[a][:, :], func=AF.Exp), r=[Bpsc[a]], w=[BpT[s3]])
                drain()

                def pv():
                    if first:
                        S.op("vector", lambda e, num=num: e.memset(num, 0.0), w=[Bnum])
                    for hh in range(4):
                        o_ap, r_ap = vtile_of(hh)
                        S.op("tensor", lambda e, hh=hh, s3=s3, o_ap=o_ap, r_ap=r_ap: e.matmul(o_ap, lhsT=pT_[s3][:, hh * 128:(hh + 1) * 128], rhs=r_ap,
                                                                                              start=False, stop=last, skip_group_check=True),
                             r=[BpT[s3], BK], w=[Bnum])
                pend.append(pv)

            def fin(den_ap, gate_ap, Bnum, o_den, o_f):
                S.op("vector", lambda e: e.tensor_scalar(out=o_den, in0=den_ap, scalar1=1e-30, scalar2=None, op0=ALU.max), r=[Bnum], w=[Bsm])
                S.op("vector", lambda e: e.reciprocal(out=o_den, in_=o_den), r=[Bsm], w=[Bsm])
                S.op("vector", lambda e: vtt(e, out=o_f, in0=o_den, in1=gate_ap, op=ALU.mult), r=[Bsm, BK], w=[Bsm])

            for qb in range(NB):
                qsl = slice(qb * 128, (qb + 1) * 128)
                ys = qb % 2
                q_rhs = qa[:, :, qsl]
                c_hi = min(NCH - 1, qb // 16)
                for c in range(c_hi + 1):
                    mms = [(kca[:, c * 128:(c + 1) * 128], q_rhs, [BK])]
                    rel = c_hi - c
                    if rel <= 1:
                        mms.append((C.identb[:], CM[:, qb % 16, 1 - rel, :].unsqueeze(1).to_broadcast([128, 4, 128]), [C.Bconst, BK]))
                    unit(mms, pnc[:, :, 0:386], Bpnc, lambda hh, c=c: (pnc[:, hh // 2, (hh % 2) * 193:(hh % 2) * 193 + 193], VR[:, c, :]), c == 0, c == c_hi)
                drain()
                dn = pnc[:, :, 0:386].rearrange("p b (x w) -> p b x w", w=193)
                fin(dn[:, :, :, 64], ng[:, qb, 4 * g:4 * g + 4].rearrange("p (b x) -> p b x", b=2), Bpnc,
                    sm[:, 0, :].rearrange("p (b x) -> p b x", b=2), sm[:, 1, :].rearrange("p (b x) -> p b x", b=2))
                for hh in range(4):
                    S.op("vector", lambda e, hh=hh, ys=ys: e.tensor_scalar(out=yt[ys][:, hh * 64:(hh + 1) * 64],
                                                                          in0=pnc[:, hh // 2, (hh % 2) * 193:(hh % 2) * 193 + 64],
                                                                          scalar1=sm[:, 1, hh:hh + 1], scalar2=None, op0=ALU.mult),
                         r=[Bpnc, Bsm], w=[Byt[ys]])
                for hh in range(4):
                    src = pnc[:, hh // 2, (hh % 2) * 193 + 65:(hh % 2) * 193 + 193]
                    if hh == 0:
                        S.op("vector", lambda e, src=src: e.tensor_scalar(out=imp[:], in0=src, scalar1=sm[:, 0, 0:1], scalar2=None, op0=ALU.mult),
                             r=[Bpnc, Bsm], w=[Bimp])
                    else:
                        S.op("vector", lambda e, src=src, hh=hh: e.scalar_tensor_tensor(out=imp[:], in0=src, scalar=sm[:, 0, hh:hh + 1], in1=imp[:],
                                                                                        op0=ALU.mult, op1=ALU.add), r=[Bpnc, Bsm, Bimp], w=[Bimp])
                off = 127 - 2 * qb
                S.op("vector", lambda e, off=off: vtt(e, out=imp[:], in0=imp[:], in1=PATF[:, off:off + 128], op=ALU.max), r=[Bimp, BK], w=[Bimp])
                S.op("vector", lambda e, off=off: vtt(e, out=imp[:], in0=imp[:], in1=PATV[:, off:off + 128], op=ALU.min), r=[Bimp, BK], w=[Bimp])
                S.op("vector", lambda e: e.memset(imp[:, 0:1], 1e30), r=[Bimp], w=[Bimp])
                S.op("vector", lambda e: e.max(out=m8[:], in_=imp[:]), r=[Bimp], w=[Bm8])
                S.op("vector", lambda e: e.tensor_scalar(out=imp[:], in0=imp[:], scalar1=m8[:, 7:8], scalar2=-NEGB, op0=ALU.is_ge, op1=ALU.mult),
                     r=[Bimp, Bm8], w=[Bimp])
                S.op("vector", lambda e: e.tensor_scalar(out=imp[:], in0=imp[:], scalar1=NEGB, scalar2=None, op0=ALU.add), r=[Bimp], w=[Bimp])
                S.op("tensor", lambda e: e.transpose(out=ptr[:, 0:128], in_=imp[:], identity=C.ident[:]), r=[Bimp, C.Bconst], w=[Bptr])
                S.op("scalar", lambda e, ys=ys: e.copy(out=nmt[ys][:], in_=ptr[:, 0:128]), r=[Bptr], w=[Bnmt[ys]])
                k0 = max(0, qb - 4)
                for kb in range(k0, qb + 1):
                    mms = [(kwa[:, kb * 128:(kb + 1) * 128], q_rhs, [BK])]
                    if kb == qb:
                        mms.append((C.identb[:], CAUS[:], [C.Bconst, BK]))
                    elif kb == qb - 4:
                        mms.append((C.identb[:], ANTI[:], [C.Bconst, BK]))
                    unit(mms, pnw[:, 0:260], Bpnw, lambda hh, kb=kb: (pnw[:, hh * 65:(hh + 1) * 65], vw[:, kb, :]), kb == k0, kb == qb)
                for kb in range(qb + 1):
                    mms = [(ksa[:, kb * 128:(kb + 1) * 128], q_rhs, [BK]),
                           (G[:, kb * 128:(kb + 1) * 128], nmt[ys][:].unsqueeze(1).to_broadcast([128, 4, 128]), [BK, Bnmt[ys]])]
                    if kb == qb:
                        mms.append((C.identb[:], CAUS[:], [C.Bconst, BK]))
                    unit(mms, pns[:, 0:260], Bpns, lambda hh, kb=kb: (pns[:, hh * 65:(hh + 1) * 65], vs[:, kb, :]), kb == 0, kb == qb)
                drain()
                dsv = pns[:, 0:260].rearrange("p (h w) -> p h w", w=65)
                dwv = pnw[:, 0:260].rearrange("p (h w) -> p h w", w=65)
                fin(dsv[:, :, 64], ng[:, qb, 8 + 4 * g:12 + 4 * g], Bpns, sm[:, 2, :], sm[:, 3, :])
                fin(dwv[:, :, 64], ng[:, qb, 16 + 4 * g:20 + 4 * g], Bpnw, sm[:, 4, :], sm[:, 5, :])
                for hh in range(4):
                    S.op("vector", lambda e, hh=hh, ys=ys: e.scalar_tensor_tensor(out=yt[ys][:, hh * 64:(hh + 1) * 64], in0=pns[:, hh * 65:hh * 65 + 64],
                                                                                 scalar=sm[:, 3, hh:hh + 1], in1=yt[ys][:, hh * 64:(hh + 1) * 64],
                                                                                 op0=ALU.mult, op1=ALU.add), r=[Bpns, Bsm, Byt[ys]], w=[Byt[ys]])
                    S.op("vector", lambda e, hh=hh, ys=ys: e.scalar_tensor_tensor(out=yt[ys][:, hh * 64:(hh + 1) * 64], in0=pnw[:, hh * 65:hh * 65 + 64],
                                                                                 scalar=sm[:, 5, hh:hh + 1], in1=yt[ys][:, hh * 64:(hh + 1) * 64],
                                                                                 op0=ALU.mult, op1=ALU.add), r=[Bpnw, Bsm, Byt[ys]], w=[Byt[ys]])
                for k in range(2):
                    S.op("tensor", lambda e, k=k, ys=ys: e.transpose(out=pty[:, k * 128:(k + 1) * 128], in_=yt[ys][:, k * 128:(k + 1) * 128],
                                                                     identity=C.ident[:]), r=[Byt[ys], C.Bconst], w=[Bpty])
                S.op("scalar", lambda e, ys=ys: e.copy(out=yo[ys][:], in_=pty[:, 0:256].rearrange("p (k t) -> p k t", k=2)), r=[Bpty], w=[Byo[ys]])
                S.op("sync", lambda e, ys=ys, qsl=qsl: e.dma_start(out=dr["ynsaT"][g * 256:(g + 1) * 256, qsl].rearrange("(k p) t -> p k t", p=128),
                                                                   in_=yo[ys][:]), r=[Byo[ys]], w=[C.Bdr["ynsaT"]], dma=True)
                if qb % 32 == 31 and qb + 1 < NB:
                    S.flush()
            S.flush()


ALL_PHASES = ("A", "B", "SB", "NSA", "Ea", "Eb")
_CACHE = {}


def kernel(**inputs):
    x = np.asarray(inputs["x"], dtype=np.float32)
    bsz, SEQ, _ = x.shape
    key = ("nc", SEQ)
    if key not in _CACHE:
        _CACHE[key] = (build(SEQ, set(ALL_PHASES)), nsa_consts(SEQ))
    nc, consts = _CACHE[key]
    w = host_inputs(inputs)
    w.update(consts)
    in_maps = [{"x": np.ascontiguousarray(x[b]), **w} for b in range(bsz)]
    res = run_bass_kernel_spmd(nc, in_maps, core_ids=list(range(bsz)))
    return np.stack([np.asarray(r["out"], dtype=np.float32) for r in res.results], axis=0)
```
